# Optimizing a Trainium2 kernel written in Bass

```python
import jax, jax.numpy as jnp
from jax import lax
import numpy as np

D_MODEL = 4096
BATCH = 1
SEQ = 8192
DEPTH = 4

N_A = DEPTH // 2
N_B = DEPTH - N_A
D_FF = -(-8 * D_MODEL // (3 * 256)) * 256
CONV_WIDTH = 31
HEAD_DIM = 128
N_HEADS = D_MODEL // HEAD_DIM
N_KV_GROUPS = 4
GROUP_SIZE = N_HEADS // N_KV_GROUPS
KV_WIDTH = N_KV_GROUPS * HEAD_DIM
CMP_LEN = 32
CMP_STRIDE = 16
CMP_HIDDEN = 4 * HEAD_DIM
SLC_BLOCK = 64
SLC_TOPK = 16
N_INIT_BLOCKS = 1
N_LOCAL_BLOCKS = 2
WINDOW = 512
Q_BLOCK = 128
ROPE_THETA = 10000.0
EPS = 1e-6
NEG_INF = -1e30

kernel_name = 'yoco_conformer_nsa_hybrid'


def rms_norm(x, g):
    xf = x.astype(jnp.float32)
    y = xf * lax.rsqrt(jnp.mean(xf * xf, axis=-1, keepdims=True) + EPS)
    return (y * g).astype(x.dtype)


def layer_norm(x, g, b):
    xf = x.astype(jnp.float32)
    mu = jnp.mean(xf, axis=-1, keepdims=True)
    var = jnp.mean(jnp.square(xf - mu), axis=-1, keepdims=True)
    return ((xf - mu) * lax.rsqrt(var + EPS) * g + b).astype(x.dtype)


def modulate(h, shift, scale):
    return h * (1.0 + scale[:, None, :]) + shift[:, None, :]


def rope(x, pos):
    half = HEAD_DIM // 2
    inv = ROPE_THETA ** (-jnp.arange(half, dtype=jnp.float32) / half)
    ang = pos.astype(jnp.float32)[:, None, :, None] * inv
    cos, sin = jnp.cos(ang), jnp.sin(ang)
    xf = x.astype(jnp.float32)
    x1, x2 = xf[..., :half], xf[..., half:]
    return jnp.concatenate([x1 * cos - x2 * sin, x2 * cos + x1 * sin], axis=-1).astype(x.dtype)


def swiglu(h, w_gu, w_down):
    a = h @ w_gu
    return (jax.nn.silu(a[..., :D_FF]) * a[..., D_FF:]) @ w_down


def conformer_conv(h, w_pw1, b_pw1, w_dw, b_dw, ln_g, ln_b, w_pw2, b_pw2):
    u = h @ w_pw1 + b_pw1
    u = u[..., :D_MODEL] * jax.nn.sigmoid(u[..., D_MODEL:])
    u = lax.conv_general_dilated(
        u, w_dw[:, None, :], window_strides=(1,), padding=((CONV_WIDTH - 1, 0),),
        dimension_numbers=('NWC', 'WIO', 'NWC'), feature_group_count=D_MODEL) + b_dw
    u = jax.nn.silu(layer_norm(u, ln_g, ln_b))
    return u @ w_pw2 + b_pw2


def _cmp_to_slc_matrix(n_cmp, n_slc):
    r, cl = SLC_BLOCK // CMP_STRIDE, CMP_LEN // CMP_STRIDE
    offs = (np.arange(r)[:, None] - np.arange(cl)[None, :]).reshape(-1)
    tgt = r * np.arange(n_slc)[None, :, None] + offs[None, None, :]
    return (np.arange(n_cmp)[:, None, None] == tgt).sum(-1).astype(np.float32)


def nsa_shared_kv(h, positions, kv_w, kv_k_norm, cmp_pos, cmp_w1, cmp_b1, cmp_w2, cmp_b2):
    B, S, _ = h.shape
    kv = (h @ kv_w).reshape(B, S, 6, N_KV_GROUPS, HEAD_DIM).transpose(2, 0, 3, 1, 4)
    n_cmp = (S - CMP_LEN) // CMP_STRIDE + 1
    starts = np.arange(n_cmp) * CMP_STRIDE
    idx = starts[:, None] + np.arange(CMP_LEN)[None, :]

    def compress(raw, j):
        blocks = raw[:, :, idx] + cmp_pos[j]
        flat = blocks.reshape(B, N_KV_GROUPS, n_cmp, CMP_LEN * HEAD_DIM)
        return jax.nn.silu(flat @ cmp_w1[j] + cmp_b1[j]) @ cmp_w2[j] + cmp_b2[j]

    pos_cmp = positions[:, starts + CMP_LEN - 1]
    k_cmp = rope(rms_norm(compress(kv[0], 0), kv_k_norm[0]), pos_cmp)
    v_cmp = compress(kv[1], 1)
    k_slc = rope(rms_norm(kv[2], kv_k_norm[1]), positions)
    k_win = rope(rms_norm(kv[4], kv_k_norm[2]), positions)
    return (k_cmp, v_cmp, k_slc, kv[3], k_win, kv[5])


def nsa_layer(h, positions, kv, w_q, q_norm_g, w_gate, b_gate, w_o):
    k_cmp, v_cmp, k_slc, v_slc, k_win, v_win = kv
    B, S, _ = h.shape
    G, R, QB, dh = N_KV_GROUPS, GROUP_SIZE, Q_BLOCK, HEAD_DIM
    n_qb = S // QB
    n_cmp = k_cmp.shape[2]
    n_slc = S // SLC_BLOCK
    k_top = min(SLC_TOPK, n_slc)

    q = (h @ w_q).reshape(B, S, N_HEADS, dh).transpose(0, 2, 1, 3)
    q = rope(rms_norm(q, q_norm_g), positions) * (dh ** -0.5)
    q_blocks = q.reshape(B, G, R, n_qb, QB, dh).transpose(3, 0, 1, 2, 4, 5)
    gates = jax.nn.sigmoid(h @ w_gate + b_gate)
    gates = gates.reshape(B, n_qb, QB, 3, G, R).transpose(1, 3, 0, 4, 5, 2)

    cmp_end = jnp.asarray(np.arange(n_cmp) * CMP_STRIDE + CMP_LEN - 1)
    cmp_to_slc = jnp.asarray(_cmp_to_slc_matrix(n_cmp, n_slc))
    k_slc_blk = k_slc.reshape(B, G, n_slc, SLC_BLOCK, dh)
    v_slc_blk = v_slc.reshape(B, G, n_slc, SLC_BLOCK, dh)
    pad = ((0, 0), (0, 0), (WINDOW, 0), (0, 0))
    k_win_pad, v_win_pad = jnp.pad(k_win, pad), jnp.pad(v_win, pad)
    b_ix = jnp.arange(B)[:, None, None, None]
    g_ix = jnp.arange(G)[None, :, None, None]
    blk_ids = jnp.arange(n_slc)

    def block_attend(args):
        qb, gb, blk = args
        t = blk * QB + jnp.arange(QB)
        valid_c = cmp_end[None, :] <= t[:, None]
        s_c = jnp.einsum('bgrqd,bgnd->bgrqn', qb, k_cmp).astype(jnp.float32)
        p_c = jax.nn.softmax(jnp.where(valid_c, s_c, NEG_INF), axis=-1) * valid_c
        o_c = jnp.einsum('bgrqn,bgnd->bgrqd', p_c.astype(v_cmp.dtype), v_cmp)
        imp = jnp.einsum('bgrqn,ns->bgqs', p_c, cmp_to_slc)
        dist = t[:, None] // SLC_BLOCK - blk_ids[None, :]
        forced = (blk_ids[None, :] < N_INIT_BLOCKS) | ((dist >= 0) & (dist < N_LOCAL_BLOCKS))
        imp = jnp.where(forced, jnp.inf, jnp.where(dist >= 0, imp, -jnp.inf))
        _, sel = lax.top_k(imp, k_top)
        k_sel = k_slc_blk[b_ix, g_ix, sel].reshape(B, G, QB, k_top * SLC_BLOCK, dh)
        v_sel = v_slc_blk[b_ix, g_ix, sel].reshape(B, G, QB, k_top * SLC_BLOCK, dh)
        kpos = (sel[..., None] * SLC_BLOCK + jnp.arange(SLC_BLOCK)).reshape(B, G, QB, k_top * SLC_BLOCK)
        valid_s = (kpos <= t[:, None])[:, :, None]
        s_s = jnp.einsum('bgrqd,bgqkd->bgrqk', qb, k_sel).astype(jnp.float32)
        p_s = jax.nn.softmax(jnp.where(valid_s, s_s, NEG_INF), axis=-1)
        o_s = jnp.einsum('bgrqk,bgqkd->bgrqd', p_s.astype(v_sel.dtype), v_sel)
        k_w = lax.dynamic_slice_in_dim(k_win_pad, blk * QB, WINDOW + QB, axis=2)
        v_w = lax.dynamic_slice_in_dim(v_win_pad, blk * QB, WINDOW + QB, axis=2)
        kpos_w = blk * QB - WINDOW + jnp.arange(WINDOW + QB)
        dist_w = t[:, None] - kpos_w[None, :]
        valid_w = (dist_w >= 0) & (dist_w < WINDOW) & (kpos_w[None, :] >= 0)
        s_w = jnp.einsum('bgrqd,bgkd->bgrqk', qb, k_w).astype(jnp.float32)
        p_w = jax.nn.softmax(jnp.where(valid_w, s_w, NEG_INF), axis=-1)
        o_w = jnp.einsum('bgrqk,bgkd->bgrqd', p_w.astype(v_w.dtype), v_w)
        return gb[0][..., None] * o_c + gb[1][..., None] * o_s + gb[2][..., None] * o_w

    out = lax.map(block_attend, (q_blocks, gates, jnp.arange(n_qb)))
    out = out.transpose(1, 0, 4, 2, 3, 5).reshape(B, S, N_HEADS * dh)
    return out @ w_o


def setup_inputs(seed: int = 0) -> dict:
    key = jax.random.key(seed)
    ks = jax.random.split(key, 32)
    f32 = jnp.float32
    D, F, dh = D_MODEL, D_FF, HEAD_DIM

    def nrm(k, shape, fan_in, scale=1.0):
        return jax.random.normal(k, shape, f32) * (scale * fan_in ** -0.5)

    def gain(k, shape):
        return 1.0 + 0.1 * jax.random.normal(k, shape, f32)

    def bias(k, shape):
        return 0.02 * jax.random.normal(k, shape, f32)

    gate_offset = jnp.asarray(np.array([0, 0, 1, 0, 0, 1], np.float32))[None, :, None]
    positions = (jax.random.randint(ks[2], (BATCH, 1), 0, 1024, jnp.int32)
                 + jnp.arange(SEQ, dtype=jnp.int32)[None, :])
    return {
        'x': jax.random.normal(ks[0], (BATCH, SEQ, D), f32),
        'c': jax.random.normal(ks[1], (BATCH, D), f32),
        'positions': positions,
        'w_ada': nrm(ks[3], (D, 6 * D), D, 0.5),
        'b_ada': bias(ks[4], (6 * D,)),
        'ada_emb': 0.1 * jax.random.normal(ks[5], (DEPTH, 6, D), f32) + gate_offset,
        'kv_ada_emb': 0.1 * jax.random.normal(ks[6], (2, D), f32),
        'norm_mix': gain(ks[7], (DEPTH, D)),
        'norm_ffn': gain(ks[8], (DEPTH, D)),
        'norm_kv': gain(ks[9], (D,)),
        'conv_w_pw1': nrm(ks[10], (N_A, D, 2 * D), D),
        'conv_b_pw1': bias(ks[11], (N_A, 2 * D)),
        'conv_w_dw': nrm(ks[12], (N_A, CONV_WIDTH, D), CONV_WIDTH),
        'conv_b_dw': bias(ks[13], (N_A, D)),
        'conv_ln_g': gain(ks[14], (N_A, D)),
        'conv_ln_b': bias(ks[15], (N_A, D)),
        'conv_w_pw2': nrm(ks[16], (N_A, D, D), D),
        'conv_b_pw2': bias(ks[17], (N_A, D)),
        'nsa_w_q': nrm(ks[18], (N_B, D, N_HEADS * dh), D),
        'nsa_q_norm': gain(ks[19], (N_B, dh)),
        'nsa_w_gate': nrm(ks[20], (N_B, D, 3 * N_HEADS), D),
        'nsa_b_gate': bias(ks[21], (N_B, 3 * N_HEADS)),
        'nsa_w_o': nrm(ks[22], (N_B, N_HEADS * dh, D), N_HEADS * dh),
        'kv_w': nrm(ks[23], (D, 6 * KV_WIDTH), D),
        'kv_k_norm': gain(ks[24], (3, dh)),
        'cmp_pos': 0.1 * jax.random.normal(ks[25], (2, CMP_LEN, dh), f32),
        'cmp_w1': nrm(ks[26], (2, CMP_LEN * dh, CMP_HIDDEN), CMP_LEN * dh),
        'cmp_b1': bias(ks[27], (2, CMP_HIDDEN)),
        'cmp_w2': nrm(ks[28], (2, CMP_HIDDEN, dh), CMP_HIDDEN),
        'cmp_b2': bias(ks[29], (2, dh)),
        'ffn_w_gu': nrm(ks[30], (DEPTH, D, 2 * F), D),
        'ffn_w_down': nrm(ks[31], (DEPTH, F, D), F),
    }


def reference(x, c, positions, w_ada, b_ada, ada_emb, kv_ada_emb, norm_mix, norm_ffn, norm_kv,
              conv_w_pw1, conv_b_pw1, conv_w_dw, conv_b_dw, conv_ln_g, conv_ln_b, conv_w_pw2, conv_b_pw2,
              nsa_w_q, nsa_q_norm, nsa_w_gate, nsa_b_gate, nsa_w_o,
              kv_w, kv_k_norm, cmp_pos, cmp_w1, cmp_b1, cmp_w2, cmp_b2,
              ffn_w_gu, ffn_w_down):
    B = x.shape[0]
    mod = (jax.nn.silu(c) @ w_ada + b_ada).reshape(B, 6, D_MODEL)
    kv = None
    for layer in range(DEPTH):
        m = mod + ada_emb[layer]
        h = modulate(rms_norm(x, norm_mix[layer]), m[:, 0], m[:, 1])
        if layer < N_A:
            y = conformer_conv(h, conv_w_pw1[layer], conv_b_pw1[layer], conv_w_dw[layer], conv_b_dw[layer],
                               conv_ln_g[layer], conv_ln_b[layer], conv_w_pw2[layer], conv_b_pw2[layer])
        else:
            if layer == N_A:
                h_kv = modulate(rms_norm(x, norm_kv), mod[:, 0] + kv_ada_emb[0], mod[:, 1] + kv_ada_emb[1])
                kv = nsa_shared_kv(h_kv, positions, kv_w, kv_k_norm, cmp_pos, cmp_w1, cmp_b1, cmp_w2, cmp_b2)
            i = layer - N_A
            y = nsa_layer(h, positions, kv, nsa_w_q[i], nsa_q_norm[i], nsa_w_gate[i], nsa_b_gate[i], nsa_w_o[i])
        x = x + m[:, 2][:, None, :] * y
        h = modulate(rms_norm(x, norm_ffn[layer]), m[:, 3], m[:, 4])
        x = x + m[:, 5][:, None, :] * swiglu(h, ffn_w_gu[layer], ffn_w_down[layer])
    return x
```

```python
import numpy as np
import ml_dtypes
from contextlib import ExitStack
import concourse.bass as bass
import concourse.mybir as mybir
from concourse.bass_utils import run_bass_kernel_spmd

F32 = mybir.dt.float32
BF16 = mybir.dt.bfloat16
I32 = mybir.dt.int32
AF = mybir.ActivationFunctionType
ALU = mybir.AluOpType
EPS = 1e-6
NQ = 6
CAST_ENG = ("pool", "dve")
DMA_SCRATCH = 1024
NSTAGE = 3


class Cfg:
    def __init__(s, D=4096, S=8192, NCORES=8, CONVW=31, WINDOW=512, TOPK=16):
        s.D, s.S, s.NCORES = D, S, NCORES
        s.DC = D // 128
        s.T = S // NCORES
        s.F = -(-8 * D // (3 * 256)) * 256
        s.FC = s.F // 128
        s.CONVW = CONVW
        s.H = D // 128
        s.G = 4
        s.R = s.H // s.G
        s.WINDOW = WINDOW
        s.TOPK = TOPK
        s.NT = max(1, s.T // 512)
        s.TW = s.T // s.NT
        s.NSLC = S // 64
        s.NCMP = (S - 32) // 16 + 1
        nq = max(4 if s.FC >= 40 else 2, -(-s.FC // s.DC))
        base, rem = divmod(s.FC, nq)
        s.FQ = [base + (1 if i < rem else 0) for i in range(nq)]


class Buf:
    __slots__ = ("name", "w", "r")

    def __init__(s, name):
        s.name, s.w, s.r = name, None, {}


def bufs(name, n):
    return [Buf(f"{name}{i}") for i in range(n)]


class Prog:
    def __init__(s, nc, es):
        s.nc = nc
        s.eng = {"pe": nc.tensor, "act": nc.scalar, "dve": nc.vector, "pool": nc.gpsimd, "sp": nc.sync}
        s.sem = {k: es.enter_context(nc.semaphore("s_" + k)) for k in s.eng}
        s.cnt = {k: 0 for k in s.eng}
        s.waited = {k: {} for k in s.eng}
        s.dsem = {q: [es.enter_context(nc.semaphore(f"d_{q}{i}")) for i in range(NQ)] for q in ("sp", "pool", "act")}
        s.dval = {q: [0] * NQ for q in s.dsem}
        s.dnext = {q: 0 for q in s.dsem}
        s.out_tokens = []

    def _wait(s, e, tok):
        sem, val, key = tok
        if key == "pe" and e == "pe":
            return
        w = s.waited[e]
        if w.get(key, 0) >= val:
            return
        s.eng[e].wait_ge(sem, val)
        w[key] = val

    def _deps(s, e, reads, writes):
        best = {}
        for b in reads:
            if b.w is not None:
                k = b.w[2]
                if k not in best or best[k][1] < b.w[1]:
                    best[k] = b.w
        for b in writes:
            if b.w is not None:
                k = b.w[2]
                if k not in best or best[k][1] < b.w[1]:
                    best[k] = b.w
            for k, t in b.r.items():
                if k not in best or best[k][1] < t[1]:
                    best[k] = t
        for t in best.values():
            s._wait(e, t)

    def _mark(s, tok, reads, writes):
        k = tok[2]
        for b in reads:
            b.r[k] = tok
        for b in writes:
            b.w = tok
            b.r = {}

    def op(s, e, fn, reads=(), writes=()):
        s._deps(e, reads, writes)
        ins = fn(s.eng[e])
        s.cnt[e] += 1
        ins.then_inc(s.sem[e], 1)
        s._mark((s.sem[e], s.cnt[e], e), reads, writes)

    def dma(s, q, out, in_, reads=(), writes=(), is_output=False):
        i = s.dnext[q]
        s.dnext[q] = (i + 1) % NQ
        sem = s.dsem[q][i]
        key = ("d", q, i)
        if s.dval[q][i] > 0:
            s._wait(q, (sem, s.dval[q][i], key))
        s._deps(q, reads, writes)
        ins = s.eng[q].dma_start(out=out, in_=in_)
        s.dval[q][i] += 16
        ins.then_inc(sem, 16)
        tok = (sem, s.dval[q][i], key)
        s._mark(tok, reads, writes)
        if is_output:
            s.out_tokens.append(tok)

    def finish(s):
        for q in s.dsem:
            for i in range(NQ):
                if s.dval[q][i] > 0:
                    s._wait("sp", (s.dsem[q][i], s.dval[q][i], ("d", q, i)))
        for e in ("pe", "act", "dve", "pool"):
            if s.cnt[e] > 0:
                s._wait("sp", (s.sem[e], s.cnt[e], e))


class WStream:
    SP = 2048

    def __init__(s, P, nc, es, E, nstage=NSTAGE, nbf=2):
        s.P, s.E = P, E
        s.stage = [es.enter_context(nc.sbuf_tensor(f"wst{i}", [128, s.SP], F32)) for i in range(nstage)]
        s.sbuf = bufs("wst", nstage)
        s.wb = [es.enter_context(nc.sbuf_tensor(f"wbf{i}", [128, E], BF16)) for i in range(nbf)]
        s.wbuf = bufs("wbf", nbf)
        s.i = 0
        s.j = 0
        s.c = 0

    def stage_slot(s):
        si = s.i % len(s.stage)
        s.i += 1
        return s.stage[si], s.sbuf[si]

    def load(s, src_ap, n):
        P = s.P
        bi = s.j % len(s.wb)
        s.j += 1
        wb, wbb = s.wb[bi], s.wbuf[bi]
        for a in range(0, n, s.SP):
            b = min(n, a + s.SP)
            st, sb = s.stage_slot()
            P.dma("sp", st[:, 0:b - a], src_ap[:, a:b], writes=[sb])
            ce = CAST_ENG[s.c % len(CAST_ENG)]
            s.c += 1
            P.op(ce, lambda e: e.tensor_copy(out=wb[:, a:b], in_=st[:, 0:b - a]), reads=[sb], writes=[wbb])
        return wb, wbb


class PsumBanks:
    def __init__(s, nc, es):
        s.t = [es.enter_context(nc.psum_tensor(f"pb{i}", [128, 512], F32)) for i in range(8)]
        s.b = bufs("pb", 8)
        s.rot = list(range(8))
        s.i = 0

    def next(s):
        i = s.rot[s.i % len(s.rot)]
        s.i += 1
        return s.t[i], s.b[i]

    def hold(s, n):
        got = []
        for _ in range(n):
            i = s.rot.pop(s.i % len(s.rot))
            got.append(i)
        return [(s.t[i], s.b[i]) for i in got], got

    def unhold(s, ids):
        s.rot.extend(ids)
        s.rot.sort()


def segs_of(T):
    NT = max(1, T // 512)
    TW = T // NT
    return [(i * TW, TW) for i in range(NT)]


class LayerBuilder:
    def __init__(s, cfg, kind, with_mod, with_kv=False):
        s.cfg, s.kind, s.with_mod, s.with_kv = cfg, kind, with_mod, with_kv
        s.es = ExitStack()
        s.nc = bass.Bass("TRN2", target_bir_lowering=False, dynamic_dma_scratch_size=DMA_SCRATCH)
        s.P = Prog(s.nc, s.es)
        s.inputs = {}

    def din(s, name, shape, dt=F32):
        t = s.nc.dram_tensor(name, list(shape), dt, kind="ExternalInput")
        s.inputs[name] = (tuple(shape), dt)
        return t

    def dout(s, name, shape, dt=F32):
        return s.nc.dram_tensor(name, list(shape), dt, kind="ExternalOutput")

    def sb(s, name, shape, dt=F32):
        return s.es.enter_context(s.nc.sbuf_tensor(name, list(shape), dt))

    def load_small(s, name, shape, dt=F32):
        d = s.din(name, shape, dt)
        t = s.sb("sb_" + name, shape, dt)
        b = Buf(name)
        idx = tuple(slice(None) for _ in shape)
        s.P.dma("sp", t[idx], d[idx], writes=[b])
        return t, b

    def setup_common(s):
        cfg, P, nc = s.cfg, s.P, s.nc
        DC, T = cfg.DC, cfg.T
        s.ps = PsumBanks(nc, s.es)
        s.ws = WStream(P, nc, s.es, max(DC * 128, 4096))
        s.bufA = s.sb("bufA", [128, DC, 32 + T], BF16)
        s.bA = bufs("bA", DC)
        s.NB = max(DC * T, 32768)
        s.bufB_raw = s.sb("bufB", [128, s.NB], BF16)
        s.bufB = s.bufB_raw[:, 0:DC * T].rearrange("p (j t) -> p j t", j=DC)
        s.bB = bufs("bB", DC)
        s.XB = s.sb("XB", [128, 4, T], F32)
        s.bX = bufs("bX", 4)
        s.xi = 0
        s.xb_n = 4
        s.ones = s.sb("ones", [128, 128], BF16)
        s.b_ones = Buf("ones")
        P.op("dve", lambda e: e.memset(s.ones[:], 1.0), writes=[s.b_ones])
        s.onesf = s.sb("onesf", [128, 1], F32)
        s.b_onesf = Buf("onesf")
        P.op("dve", lambda e: e.memset(s.onesf[:], 1.0), writes=[s.b_onesf])
        s.epsc = s.sb("epsc", [128, 1], F32)
        s.b_eps = Buf("eps")
        P.op("dve", lambda e: e.memset(s.epsc[:], EPS), writes=[s.b_eps])
        s.sqb = s.sb("sqb", [128, 2, 512], BF16)
        s.b_sq = bufs("sq", 2)
        s.sqi = 0
        s.rstd = s.sb("rstd", [128, T], F32)
        s.b_rstd = Buf("rstd")
        s.rstd2 = s.sb("rstd2", [128, T], F32)
        s.b_rstd2 = Buf("rstd2")

    def xbuf(s):
        i = s.xi % s.xb_n
        s.xi += 1
        return s.XB[:, i, :], s.bX[i]

    def sq_slot(s):
        i = s.sqi % 2
        s.sqi += 1
        return s.sqb[:, i, :], s.b_sq[i]

    def vec_setup(s, layer_has_mod_input):
        cfg, P, nc = s.cfg, s.P, s.nc
        DC, T, D = cfg.DC, cfg.T, cfg.D
        s.mod = s.sb("mod", [128, 6 * DC], F32)
        s.b_mod = Buf("mod")
        if s.with_mod:
            c_t, c_b = s.load_small("c_l", [128, DC])
            bada_t, bada_b = s.load_small("b_ada_l", [128, 6 * DC])
            wada = s.din("w_ada_l", [DC, 128, 6 * D])
            sc = s.sb("silc", [128, DC], F32)
            b_sc = Buf("silc")
            P.op("act", lambda e: e.activation(out=sc[:], in_=c_t[:], func=AF.Silu), reads=[c_b], writes=[b_sc])
            NR = min(2 * T, 6 * D, WStream.SP)
            acc = s.XB[:, 0:2, :].rearrange("p a t -> p (a t)")
            b_acc = [s.bX[0], s.bX[1]]
            (mpp,), mids = s.ps.hold(1)
            mp, mpb = mpp
            for r0 in range(0, 6 * D, NR):
                for kc in range(DC):
                    st, stb = s.ws.stage_slot()
                    P.dma("sp", st[:, 0:NR], wada[kc, :, r0:r0 + NR], writes=[stb])
                    if kc == 0:
                        P.op("dve", lambda e: e.tensor_scalar(out=acc[:, 0:NR], in0=st[:, 0:NR], scalar1=sc[:, 0:1],
                                                              scalar2=None, op0=ALU.mult),
                             reads=[stb, b_sc], writes=b_acc)
                    else:
                        P.op("dve", lambda e: e.scalar_tensor_tensor(out=acc[:, 0:NR], in0=st[:, 0:NR],
                                                                     scalar=sc[:, kc:kc + 1], in1=acc[:, 0:NR],
                                                                     op0=ALU.mult, op1=ALU.add),
                             reads=[stb, b_sc] + b_acc, writes=b_acc)

                def mm(e, r0=r0):
                    ins = None
                    for cc in range(NR // 128):
                        col = (r0 // 128) + cc
                        ins = e.matmul(mp[:, col:col + 1], lhsT=acc[:, cc * 128:(cc + 1) * 128], rhs=s.onesf[:, 0:1],
                                       start=True, stop=True)
                    return ins
                P.op("pe", mm, reads=b_acc + [s.b_onesf], writes=[mpb])
            P.op("dve", lambda e: e.tensor_tensor(out=s.mod[:], in0=mp[:, 0:6 * DC], in1=bada_t[:], op=ALU.add),
                 reads=[mpb, bada_b], writes=[s.b_mod])
            s.ps.unhold(mids)
            s.mod_out = s.dout("mod_out", [128, 6 * DC])
            P.dma("act", s.mod_out[:, :], s.mod[:], reads=[s.b_mod], is_output=True)
        else:
            d = s.din("mod_in", [128, 6 * DC])
            P.dma("sp", s.mod[:], d[:, :], writes=[s.b_mod])
        ada_t, ada_b = s.load_small("ada_l", [128, 6 * DC])
        nm_t, nm_b = s.load_small("norm_mix_l", [128, DC])
        nf_t, nf_b = s.load_small("norm_ffn_l", [128, DC])
        m = s.sb("mvec", [128, 6 * DC], F32)
        s.b_m = Buf("mvec")
        P.op("dve", lambda e: e.tensor_tensor(out=m[:], in0=s.mod[:], in1=ada_t[:], op=ALU.add),
             reads=[s.b_mod, ada_b], writes=[s.b_m])
        s.m = m
        s.Amix = s.sb("Amix", [128, DC], F32)
        s.Affn = s.sb("Affn", [128, DC], F32)
        s.b_A = Buf("Avec")
        P.op("dve", lambda e: e.scalar_tensor_tensor(out=s.Amix[:], in0=m[:, DC:2 * DC], scalar=1.0, in1=nm_t[:],
                                                     op0=ALU.add, op1=ALU.mult), reads=[s.b_m, nm_b], writes=[s.b_A])
        P.op("dve", lambda e: e.scalar_tensor_tensor(out=s.Affn[:], in0=m[:, 4 * DC:5 * DC], scalar=1.0, in1=nf_t[:],
                                                     op0=ALU.add, op1=ALU.mult), reads=[s.b_m, nf_b, s.b_A],
             writes=[s.b_A])

    def mvec(s, i, j):
        DC = s.cfg.DC
        return s.m[:, i * DC + j:i * DC + j + 1]

    def rms_stats(s, xsrc, xsrc_bufs, width, rstd_ap, rstd_buf, col0=0):
        cfg, P = s.cfg, s.P
        DC, D = cfg.DC, cfg.D
        sg = segs_of(width) if width >= 512 else [(0, width)]
        banks, bids = s.ps.hold(len(sg))
        for j in range(DC):
            xb, xbb = s.xbuf()
            P.dma("sp", xb[:, 0:width], xsrc(j), reads=[xsrc_bufs[j]], writes=[xbb])
            for si, (c0, w) in enumerate(sg):
                sq, sqb = s.sq_slot()
                P.op("act", lambda e: e.activation(out=sq[:, 0:w], in_=xb[:, c0:c0 + w], func=AF.Square),
                     reads=[xbb], writes=[sqb])
                pt, pb = banks[si]
                P.op("pe", lambda e: e.matmul(pt[:, 0:w], lhsT=s.ones[:], rhs=sq[:, 0:w], start=(j == 0),
                                              stop=(j == DC - 1)),
                     reads=[sqb, s.b_ones], writes=[pb])
        for si, (c0, w) in enumerate(sg):
            pt, pb = banks[si]
            P.op("act", lambda e: e.activation(out=rstd_ap[:, c0:c0 + w], in_=pt[:, 0:w], func=AF.Sqrt,
                                               bias=s.epsc[:, 0:1], scale=1.0 / D),
                 reads=[pb, s.b_eps], writes=[rstd_buf])
        P.op("dve", lambda e: e.reciprocal(out=rstd_ap[:, 0:width], in_=rstd_ap[:, 0:width]),
             reads=[rstd_buf], writes=[rstd_buf])
        s.ps.unhold(bids)

    def modulate(s, xsrc, xsrc_bufs, width, rstd_ap, rstd_buf, A, Bcol, dst, dst_bufs):
        cfg, P = s.cfg, s.P
        DC = cfg.DC
        for j in range(DC):
            xb, xbb = s.xbuf()
            P.dma("sp", xb[:, 0:width], xsrc(j), reads=[xsrc_bufs[j]], writes=[xbb])
            P.op("dve", lambda e: e.scalar_tensor_tensor(out=xb[:, 0:width], in0=xb[:, 0:width], scalar=A[:, j:j + 1],
                                                         in1=rstd_ap[:, 0:width], op0=ALU.mult, op1=ALU.mult),
                 reads=[xbb, rstd_buf, s.b_A], writes=[xbb])
            P.op("act", lambda e: e.activation(out=dst(j), in_=xb[:, 0:width], func=AF.Identity, bias=Bcol(j),
                                               scale=1.0),
                 reads=[xbb, s.b_m], writes=[dst_bufs[j]])


    def rope_tables(s, pos_dram, W, cos_ap, sin_ap, b_cs, invf, b_invf, t1, b_t1, t2, b_t2):
        P = s.P
        TWO_PI = 2.0 * np.pi
        ki = t2.bitcast(I32)
        si = sin_ap.bitcast(I32)
        P.dma("sp", si[:, 0:W], pos_dram, writes=[b_cs])
        P.op("dve", lambda e: e.tensor_copy(out=t1[:, 0:W], in_=si[:, 0:W]), reads=[b_cs], writes=[b_t1])
        P.op("dve", lambda e: e.tensor_scalar(out=t1[:, 0:W], in0=t1[:, 0:W], scalar1=invf[:, 0:1],
                                              scalar2=float(1.0 / TWO_PI), op0=ALU.mult, op1=ALU.mult),
             reads=[b_t1, b_invf], writes=[b_t1])
        P.op("dve", lambda e: e.tensor_copy(out=ki[:, 0:W], in_=t1[:, 0:W]), reads=[b_t1], writes=[b_t2])
        P.op("dve", lambda e: e.tensor_copy(out=sin_ap[:, 0:W], in_=ki[:, 0:W]), reads=[b_t2], writes=[b_cs])
        P.op("dve", lambda e: e.tensor_tensor(out=t1[:, 0:W], in0=t1[:, 0:W], in1=sin_ap[:, 0:W], op=ALU.subtract),
             reads=[b_t1, b_cs], writes=[b_t1])
        for (dst, shift) in ((sin_ap, 0.0), (cos_ap, 0.25)):
            P.op("dve", lambda e: e.tensor_scalar(out=dst[:, 0:W], in0=t1[:, 0:W], scalar1=float(shift), scalar2=None,
                                                  op0=ALU.add), reads=[b_t1], writes=[b_cs])
            for (cmp, thr, sign) in ((ALU.is_gt, 0.5, ALU.subtract), (ALU.is_lt, -0.5, ALU.add)):
                P.op("dve", lambda e: e.tensor_scalar(out=t2[:, 0:W], in0=dst[:, 0:W], scalar1=float(thr),
                                                      scalar2=None, op0=cmp), reads=[b_cs], writes=[b_t2])
                P.op("dve", lambda e: e.tensor_tensor(out=dst[:, 0:W], in0=dst[:, 0:W], in1=t2[:, 0:W], op=sign),
                     reads=[b_cs, b_t2], writes=[b_cs])
            P.op("dve", lambda e: e.tensor_scalar(out=dst[:, 0:W], in0=dst[:, 0:W], scalar1=0.49995,
                                                  scalar2=-0.49995, op0=ALU.min, op1=ALU.max),
                 reads=[b_cs], writes=[b_cs])
            P.op("act", lambda e: e.activation(out=dst[:, 0:W], in_=dst[:, 0:W], func=AF.Sin, scale=float(TWO_PI)),
                 reads=[b_cs], writes=[b_cs])

    def rope_consts(s, want_kout=True):
        P = s.P
        s.invf, s.b_invf = s.load_small("invf", [128, 1])
        rm_t, rm_b = s.load_small("rotm", [128, 128])
        s.rotm = s.sb("rotm_bf", [128, 128], BF16)
        s.b_rotm = Buf("rotm")
        P.op("dve", lambda e: e.tensor_copy(out=s.rotm[:], in_=rm_t[:]), reads=[rm_b], writes=[s.b_rotm])
        s.k1b = s.sb("k1b", [128, 2, 512], BF16)
        s.b_k1 = bufs("k1b", 2)
        s.k1i = 0
        if want_kout:
            s.kout = s.sb("kout", [128, 2, 512], BF16)
            s.b_kout = bufs("kout", 2)
            s.koi = 0

    def normrope(s, pk, pkb, w, gcol, b_g, cos_ap, sin_ap, b_cs, out_scale, emit_out, rk, rkb, dst=None, dstb=None):
        P = s.P
        sq, sqb = s.sq_slot()
        P.op("act", lambda e: e.activation(out=sq[:, 0:w], in_=pk[:, 0:w], func=AF.Square), reads=[pkb], writes=[sqb])
        k1, k1b = s.k1b[:, s.k1i % 2, :], s.b_k1[s.k1i % 2]
        s.k1i += 1
        P.op("act", lambda e: e.activation(out=k1[:, 0:w], in_=pk[:, 0:w], func=AF.Identity, scale=gcol),
             reads=[pkb, b_g], writes=[k1b])
        pss, pssb = s.ps.next()
        P.op("pe", lambda e: e.matmul(pss[:, 0:w], lhsT=s.ones[:], rhs=sq[:, 0:w], start=True, stop=True),
             reads=[sqb, s.b_ones], writes=[pssb])
        pr, prb = s.ps.next()
        P.op("pe", lambda e: e.matmul(pr[:, 0:w], lhsT=s.rotm[:], rhs=k1[:, 0:w], start=True, stop=True),
             reads=[k1b, s.b_rotm], writes=[prb])
        P.op("act", lambda e: e.activation(out=rk[:, 0:w], in_=pss[:, 0:w], func=AF.Sqrt, bias=s.epsc[:, 0:1],
                                           scale=1.0 / 128.0), reads=[pssb, s.b_eps], writes=[rkb])
        P.op("dve", lambda e: e.reciprocal(out=rk[:, 0:w], in_=rk[:, 0:w]), reads=[rkb], writes=[rkb])
        t1, t1b = s.xbuf()
        P.op("dve", lambda e: e.tensor_tensor(out=t1[:, 0:w], in0=k1[:, 0:w], in1=cos_ap, op=ALU.mult),
             reads=[k1b, b_cs], writes=[t1b])
        t2, t2b = s.xbuf()
        P.op("dve", lambda e: e.tensor_tensor(out=t2[:, 0:w], in0=pr[:, 0:w], in1=sin_ap, op=ALU.mult),
             reads=[prb, b_cs], writes=[t2b])
        P.op("dve", lambda e: e.tensor_tensor(out=t1[:, 0:w], in0=t1[:, 0:w], in1=t2[:, 0:w], op=ALU.add),
             reads=[t1b, t2b], writes=[t1b])
        if dst is not None:
            P.op("dve", lambda e: e.scalar_tensor_tensor(out=dst, in0=t1[:, 0:w], scalar=float(out_scale),
                                                         in1=rk[:, 0:w], op0=ALU.mult, op1=ALU.mult),
                 reads=[t1b, rkb], writes=[dstb])
            return
        ko, kob = s.kout[:, s.koi % 2, :], s.b_kout[s.koi % 2]
        s.koi += 1
        P.op("dve", lambda e: e.scalar_tensor_tensor(out=ko[:, 0:w], in0=t1[:, 0:w], scalar=float(out_scale),
                                                     in1=rk[:, 0:w], op0=ALU.mult, op1=ALU.mult),
             reads=[t1b, rkb], writes=[kob])
        emit_out(ko, kob)

    def kvproj(s, xsrc, xsrc_bufs):
        cfg, P = s.cfg, s.P
        DC, T, D = cfg.DC, cfg.T, cfg.D
        sg = segs_of(T)
        nk_t, nk_b = s.load_small("norm_kv_l", [128, DC])
        ka_t, ka_b = s.load_small("kv_ada_l", [128, 2 * DC])
        kn_t, kn_b = s.load_small("kn_l", [128, 2])
        pos_d = s.din("pos_l", [128, T], I32)
        wf = s.din("w_kvf_l", [16, 128, DC * 128])
        wt_ = s.din("w_kvt_l", [2, 128, DC * 512])
        outs_f = [s.dout(nm, [4, 128, T], BF16) for nm in ("kcr", "vcr", "ksl", "kwn")]
        outs_t = [s.dout(nm, [T, 512], BF16) for nm in ("vsl", "vwn")]
        s.rope_consts()
        Akv = s.sb("Akv", [128, DC], F32)
        Bkv = s.sb("Bkv", [128, DC], F32)
        P.op("dve", lambda e: e.tensor_tensor(out=Akv[:], in0=s.mod[:, DC:2 * DC], in1=ka_t[:, DC:2 * DC], op=ALU.add),
             reads=[s.b_mod, ka_b], writes=[s.b_A])
        P.op("dve", lambda e: e.scalar_tensor_tensor(out=Akv[:], in0=Akv[:], scalar=1.0, in1=nk_t[:], op0=ALU.add,
                                                     op1=ALU.mult), reads=[s.b_A, nk_b], writes=[s.b_A])
        P.op("dve", lambda e: e.tensor_tensor(out=Bkv[:], in0=s.mod[:, 0:DC], in1=ka_t[:, 0:DC], op=ALU.add),
             reads=[s.b_mod, ka_b, s.b_m], writes=[s.b_m])
        s.rms_stats(lambda j: xsrc[j, :, :], xsrc_bufs, T, s.rstd, s.b_rstd)
        s.modulate(lambda j: xsrc[j, :, :], xsrc_bufs, T, s.rstd, s.b_rstd, Akv, lambda j: Bkv[:, j:j + 1],
                   lambda j: s.bufB[:, j, :], s.bB)
        s.xb_n = 2
        cos, sin = s.XB[:, 2, :], s.XB[:, 3, :]
        b_cs = Buf("cossin")
        s.alias([b_cs], s.bX)
        s.rope_tables(pos_d[:, :], T, cos, sin, b_cs, s.invf, s.b_invf, s.rstd, s.b_rstd, s.rstd2,
                      s.b_rstd2)
        ob = [bufs("of%d" % i, 4) for i in range(4)]
        for bi in range(4):
            for g in range(4):
                wt, wtb = s.ws.load(wf[bi * 4 + g, :, :], DC * 128)
                for (c0, w) in sg:
                    pk, pkb = s.ps.next()

                    def mm(e):
                        ins = None
                        for kc in range(DC):
                            ins = e.matmul(pk[:, 0:w], lhsT=wt[:, kc * 128:(kc + 1) * 128],
                                           rhs=s.bufB[:, kc, c0:c0 + w], start=(kc == 0), stop=(kc == DC - 1))
                        return ins
                    P.op("pe", mm, reads=[wtb] + s.bB, writes=[pkb])

                    def emit(ko, kob, bi=bi, g=g, c0=c0, w=w):
                        P.dma("act", outs_f[bi][g, :, c0:c0 + w], ko[:, 0:w], reads=[kob], writes=[ob[bi][g]],
                              is_output=True)
                    if bi < 2:
                        ko, kob = s.kout[:, s.koi % 2, :], s.b_kout[s.koi % 2]
                        s.koi += 1
                        P.op("act", lambda e: e.activation(out=ko[:, 0:w], in_=pk[:, 0:w], func=AF.Identity),
                             reads=[pkb], writes=[kob])
                        emit(ko, kob)
                    else:
                        s.normrope(pk, pkb, w, kn_t[:, bi - 2:bi - 1], kn_b, cos[:, c0:c0 + w], sin[:, c0:c0 + w],
                                   b_cs, 1.0, emit, s.rstd, s.b_rstd)
        E = DC * 128
        kpu = max(1, E // 512)
        nun = DC // kpu
        ntt = T // 128
        grp = min(4, ntt)
        vo = s.sb("vo", [128, 2, 512], BF16)
        b_vo = bufs("vo", 2)
        voi = 0
        obt = [bufs("ot%d" % i, ntt) for i in range(2)]
        for bi in range(2):
            for t0 in range(0, ntt, grp):
                banks, ids = s.ps.hold(grp)
                for u in range(nun):
                    wt, wtb = s.ws.load(wt_[bi, :, u * kpu * 512:(u + 1) * kpu * 512], kpu * 512)
                    for ti in range(grp):
                        tt = t0 + ti
                        pv, pvb = banks[ti]

                        def mmv(e):
                            ins = None
                            for kk in range(kpu):
                                kc = u * kpu + kk
                                ins = e.matmul(pv[:, 0:512], lhsT=s.bufB[:, kc, tt * 128:(tt + 1) * 128],
                                               rhs=wt[:, kk * 512:(kk + 1) * 512], start=(kc == 0),
                                               stop=(kc == DC - 1))
                            return ins
                        P.op("pe", mmv, reads=[wtb] + s.bB, writes=[pvb])
                for ti in range(grp):
                    tt = t0 + ti
                    pv, pvb = banks[ti]
                    v_, vb_ = vo[:, voi % 2, :], b_vo[voi % 2]
                    voi += 1
                    P.op("act", lambda e: e.activation(out=v_[:, 0:512], in_=pv[:, 0:512], func=AF.Identity),
                         reads=[pvb], writes=[vb_])
                    P.dma("act", outs_t[bi][tt * 128:(tt + 1) * 128, :], v_[:, 0:512], reads=[vb_],
                          writes=[obt[bi][tt]], is_output=True)
                s.ps.unhold(ids)
        s.xb_n = 4


    def alias(s, new_bufs, old_bufs):
        best = {}
        for b in old_bufs:
            toks = list(b.r.values()) + ([b.w] if b.w is not None else [])
            for t in toks:
                k = t[2]
                if k not in best or best[k][1] < t[1]:
                    best[k] = t
        for nb in new_bufs:
            for k, t in best.items():
                if k not in nb.r or nb.r[k][1] < t[1]:
                    nb.r[k] = t

    def cbf(s, name, shape):
        return s.load_small(name, shape, BF16)

    def nsa(s, xin, xin_bufs, xs, xs_bufs):
        cfg, P, nc = s.cfg, s.P, s.nc
        DC, T, D, H, G, R, S = cfg.DC, cfg.T, cfg.D, cfg.H, cfg.G, cfg.R, cfg.S
        NQT = T // 128
        NSLC, NCMP = cfg.NSLC, cfg.NCMP
        NCT = -(-NCMP // 128)
        NCP = NCT * 128
        HH = min(4, R)
        NHF = R // HH
        CW = HH * 128
        NWT = cfg.WINDOW // 128 + 1
        KPW = min(32, NSLC)
        NGT = 3 * H
        sg = segs_of(T)
        wq = s.din("w_q_l", [H, 128, DC * 128])
        wo = s.din("w_o_l", [DC, 128, H * 128])
        wgd = s.din("w_gate_l", [128, DC * NGT])
        bg_t, bg_b = s.load_small("b_gate_l", [NGT, 1])
        qn_t, qn_b = s.load_small("qn_l", [128, 1])
        kn0_t, kn0_b = s.load_small("kn0_l", [128, 1])
        posq_d = s.din("posq_l", [128, T], I32)
        posc_d = s.din("posc_l", [128, NCP], I32)
        tq_t, tq_b = s.load_small("tq_l", [128, NQT])
        t0_t, t0_b = s.load_small("t0_l", [128, NQT])
        coff_t, coff_b = s.load_small("coff_l", [128, 8])
        wv_t, wv_b = s.load_small("wvalid_l", [128, NQT * NWT])
        ident_t, ident_b = s.load_small("ident", [128, 128])
        iotab_t, iotab_b = s.load_small("iotab", [128, NSLC])
        b0_t, b0_b = s.load_small("b0c", [128, NSLC])
        iotad_t, iotad_b = s.load_small("iotad", [128, 128])
        cio_t, cio_b = s.load_small("cmpiota", [128, NCT, 128])
        cmat_t, cmat_b = s.cbf("cmat", [128, NCT, NSLC])
        wmask_t, wmask_b = s.cbf("wmask", [128, NWT, 128])
        cb1_t, cb1_b = s.load_small("cmp_b1_l", [128, 8])
        cb2k_t, cb2k_b = s.load_small("cmp_b2k_l", [128, 1])
        cb2v_t, cb2v_b = s.load_small("cmp_b2v_l", [128, 128])
        cposT_t, cposT_b = s.cbf("cmp_pos_l", [128, 2, 32])
        cw2_t, cw2_b = s.cbf("cmp_w2_l", [128, 2, 512])
        cw1 = s.din("cmp_w1_l", [8, 128, 32 * 128])
        kcr = s.din("kcrT", [G, 128, S], BF16)
        vcr = s.din("vcrT", [G, 128, S], BF16)
        ksl = s.din("kslT", [G, 128, S], BF16)
        vsl = s.din("vslk", [S // 128, 128, G * 128], BF16)
        kwd = s.din("kwT", [NQT, G, 128, NWT * 128], BF16)
        vwd = s.din("vwk", [NQT, NWT, 128, G * 128], BF16)
        x0rd = s.din("x0r", [128, 2048], BF16)
        gates_d = nc.dram_tensor("gates_d", [NGT, T], F32)
        b_gd = Buf("gates_d")
        ot_d = nc.dram_tensor("ot_d", [H, 128, T], BF16)
        b_ot = bufs("ot", H)
        s.rope_consts(False)
        s.rms_stats(lambda j: xin[j, :, :], xin_bufs, T, s.rstd, s.b_rstd)
        s.modulate(lambda j: xin[j, :, :], xin_bufs, T, s.rstd, s.b_rstd, s.Amix, lambda j: s.mvec(0, j),
                   lambda j: s.bufB[:, j, :], s.bB)
        s.xb_n = 2
        cos, sin = s.XB[:, 2, :], s.XB[:, 3, :]
        b_cs = Buf("cossin")
        s.alias([b_cs], s.bX)
        s.rope_tables(posq_d[:, :], T, cos, sin, b_cs, s.invf, s.b_invf, s.rstd, s.b_rstd, s.rstd2,
                      s.b_rstd2)
        for h in range(H):
            wt, wtb = s.ws.load(wq[h, :, :], DC * 128)
            for (c0, w) in sg:
                pk, pkb = s.ps.next()

                def mm(e):
                    ins = None
                    for kc in range(DC):
                        ins = e.matmul(pk[:, 0:w], lhsT=wt[:, kc * 128:(kc + 1) * 128], rhs=s.bufB[:, kc, c0:c0 + w],
                                       start=(kc == 0), stop=(kc == DC - 1))
                    return ins
                P.op("pe", mm, reads=[wtb] + s.bB, writes=[pkb])
                s.normrope(pk, pkb, w, qn_t[:, 0:1], qn_b, cos[:, c0:c0 + w], sin[:, c0:c0 + w], b_cs,
                           128.0 ** -0.5, None, s.rstd, s.b_rstd, dst=s.bufA[:, h, 32 + c0:32 + c0 + w], dstb=s.bA[h])
        wgt, wgtb = s.ws.load(wgd[:, :], DC * NGT)
        for (c0, w) in sg:
            pg, pgb = s.ps.next()

            def mmg(e):
                ins = None
                for kc in range(DC):
                    ins = e.matmul(pg[0:NGT, 0:w], lhsT=wgt[:, kc * NGT:(kc + 1) * NGT], rhs=s.bufB[:, kc, c0:c0 + w],
                                   start=(kc == 0), stop=(kc == DC - 1))
                return ins
            P.op("pe", mmg, reads=[wgtb] + s.bB, writes=[pgb])
            gs, gsb = s.xbuf()
            P.op("act", lambda e: e.activation(out=gs[0:NGT, 0:w], in_=pg[0:NGT, 0:w], func=AF.Sigmoid,
                                               bias=bg_t[0:NGT, 0:1], scale=1.0), reads=[pgb, bg_b], writes=[gsb])
            P.dma("act", gates_d[:, c0:c0 + w], gs[0:NGT, 0:w], reads=[gsb], writes=[b_gd])
        off = [0]

        def view(n):
            v = s.bufB_raw[:, off[0]:off[0] + n]
            off[0] += n
            return v
        kch = [view(1024) for _ in range(2)]
        vch = [view(1024).rearrange("p (k d) -> p k d", k=8) for _ in range(2)]
        mch = [view(1024).rearrange("p (k q) -> p k q", k=8) for _ in range(2)]
        x0r = view(2048)
        kcT = view(G * NCP).rearrange("p (g n) -> p g n", g=G)
        vcm = view(G * NCP).rearrange("p (g k d) -> p g k d", g=G, k=NCT)
        hid = view(4 * NCP).rearrange("p (m n) -> p m n", m=4)
        Ec = view(NHF * NCT * CW).rearrange("p (f k c) -> p f k c", f=NHF, k=NCT)
        Et = view(3 * 512).rearrange("p (a c) -> p a c", a=3)
        obf = view(2 * 512).rearrange("p (a c) -> p a c", a=2)
        dm = view(8 * 128).rearrange("p (a c) -> p a c", a=8)
        vmask = view(NCT * 128).rearrange("p (a c) -> p a c", a=NCT)
        kwt = view(2 * NWT * 128).rearrange("p (a c) -> p a c", a=2)
        vwt = view(2 * NWT * 128).rearrange("p (a c) -> p a c", a=2)
        accs = view(2 * NHF * 512).bitcast(F32).rearrange("p (a c) -> p a c", a=NHF)
        s.nr_t = view(2 * 3 * 512).bitcast(F32).rearrange("p (a c) -> p a c", a=3)
        tk = view(2 * NCP).bitcast(F32)
        assert off[0] <= s.NB, (off[0], s.NB)
        b_kch, b_vch, b_mch = bufs("kch", 2), bufs("vch", 2), bufs("mch", 2)
        b_x0r, b_kcT, b_vcm, b_hid = Buf("x0r"), bufs("kcT", G), bufs("vcm", G), Buf("hid")
        b_Ec = bufs("Ec", NHF)
        b_Et, b_obf, b_dm, b_vmask = bufs("Et", 3), bufs("obf", 2), Buf("dm"), Buf("vmask")
        b_kw, b_vw, b_acc = bufs("kwt", 2), bufs("vwt", 2), bufs("accs", NHF)
        s.b_nr = bufs("nr_t", 3)
        b_tk = Buf("tkc")
        newb = (b_kch + b_vch + b_mch + [b_x0r, b_hid] + b_kcT + b_vcm + b_Ec + b_Et + b_obf + [b_dm, b_vmask]
                + b_kw + b_vw + b_acc + s.b_nr + [b_tk])
        s.alias(newb, s.bB)
        P.dma("sp", x0r, x0rd[:, :], writes=[b_x0r])
        P.op("pool", lambda e: e.memset(hid, 0.0), writes=[b_hid])
        P.op("pool", lambda e: e.memset(kcT, 0.0), writes=b_kcT)
        cc = s.sb("ccmp", [128, NCP], F32)
        sc_ = s.sb("scmp", [128, NCP], F32)
        b_ccs = Buf("ccs")
        s.rope_tables(posc_d[:, :], NCP, cc, sc_, b_ccs, s.invf, s.b_invf, s.rstd, s.b_rstd, s.rstd2,
                      s.b_rstd2)
        s.alias(s.bX, [b_cs])
        c1 = s.sb("c1", [128, 8], F32)
        b_c1 = Buf("c1")
        rawv = s.XB[:].rearrange("p a t -> p (a t)").bitcast(BF16)
        s.xb_n = 0
        for jj in range(2):
            src_d = kcr if jj == 0 else vcr
            for g in range(G):
                P.dma("sp", rawv[:, 0:S], src_d[g, :, :], writes=s.bX)
                for m in range(4):
                    wt, wtb = s.ws.load(cw1[jj * 4 + m, :, :], 32 * 128)
                    if g == 0:
                        (pcp,), pcid = s.ps.hold(1)
                        pc1, pc1b = pcp

                        def mmc(e):
                            ins = None
                            for l in range(32):
                                ins = e.matmul(pc1[:, 0:1], lhsT=wt[:, l * 128:(l + 1) * 128],
                                               rhs=cposT_t[:, jj, l:l + 1], start=(l == 0), stop=(l == 31))
                            return ins
                        P.op("pe", mmc, reads=[wtb, cposT_b], writes=[pc1b])
                        P.op("dve", lambda e: e.tensor_tensor(out=c1[:, jj * 4 + m:jj * 4 + m + 1], in0=pc1[:, 0:1],
                                                              in1=cb1_t[:, jj * 4 + m:jj * 4 + m + 1], op=ALU.add),
                             reads=[pc1b, cb1_b], writes=[b_c1])
                        s.ps.unhold(pcid)
                    ph, phb = s.ps.next()

                    def mmh(e):
                        ins = None
                        for l in range(32):
                            ins = e.matmul(ph[:, 0:NCMP], lhsT=wt[:, l * 128:(l + 1) * 128],
                                           rhs=rawv[:, l:l + 16 * (NCMP - 1) + 1:16], start=(l == 0), stop=(l == 31))
                        return ins
                    P.op("pe", mmh, reads=[wtb] + s.bX, writes=[phb])
                    P.op("act", lambda e: e.activation(out=hid[:, m, 0:NCMP], in_=ph[:, 0:NCMP], func=AF.Silu,
                                                       bias=c1[:, jj * 4 + m:jj * 4 + m + 1], scale=1.0),
                         reads=[phb, b_c1], writes=[b_hid])
                if jj == 0:
                    p2, p2b = s.ps.next()

                    def mm2(e):
                        ins = None
                        for m in range(4):
                            ins = e.matmul(p2[:, 0:NCMP], lhsT=cw2_t[:, 0, m * 128:(m + 1) * 128],
                                           rhs=hid[:, m, 0:NCMP], start=(m == 0), stop=(m == 3))
                        return ins
                    P.op("pe", mm2, reads=[cw2_b, b_hid], writes=[p2b])
                    P.op("act", lambda e: e.activation(out=tk[:, 0:NCMP], in_=p2[:, 0:NCMP], func=AF.Identity,
                                                       bias=cb2k_t[:, 0:1], scale=1.0), reads=[p2b, cb2k_b],
                         writes=[b_tk])
                    s.xb_n = 0
                    s.normrope_sb(tk, b_tk, NCMP, kn0_t[:, 0:1], kn0_b, cc, sc_, b_ccs, kcT[:, g, 0:NCMP], b_kcT[g])
                else:
                    for kt in range(NCT):
                        p2, p2b = s.ps.next()

                        def mm3(e):
                            ins = None
                            for m in range(4):
                                ins = e.matmul(p2[:, 0:128], lhsT=hid[:, m, kt * 128:(kt + 1) * 128],
                                               rhs=cw2_t[:, 1, m * 128:(m + 1) * 128], start=(m == 0), stop=(m == 3))
                            return ins
                        P.op("pe", mm3, reads=[cw2_b, b_hid], writes=[p2b])
                        P.op("dve", lambda e: e.tensor_tensor(out=vcm[:, g, kt, :], in0=p2[:, 0:128], in1=cb2v_t[:],
                                                              op=ALU.add), reads=[p2b, cb2v_b], writes=[b_vcm[g]])
        s.xb_n = 4
        XF = s.XB[:].rearrange("p a t -> p (a t)")
        gbt = [XF[:, k * 2 * T:k * 2 * T + 3 * CW].rearrange("p (b c) -> p b c", b=3) for k in range(2)]
        b_gb = [[s.bX[0], s.bX[1]], [s.bX[2], s.bX[3]]]
        RW = min(512, T // 2)
        rl_t = [s.rstd[:, 0:RW], s.rstd[:, RW:2 * RW], s.rstd2[:, 0:RW], s.rstd2[:, RW:2 * RW]]
        assert RW >= CW
        b_rl = bufs("rl", 4)
        s.alias(b_rl, [s.b_rstd, s.b_rstd2])
        rli = [0]

        def rlslot():
            i = rli[0] % 4
            rli[0] += 1
            return rl_t[i], b_rl[i]
        eti = [0]
        obi = [0]
        for d in range(8):
            P.op("dve", lambda e: e.tensor_scalar(out=dm[:, d, :], in0=iotad_t[:], scalar1=coff_t[:, d:d + 1],
                                                  scalar2=0.0, op0=ALU.add, op1=ALU.is_ge),
                 reads=[iotad_b, coff_b], writes=[b_dm])
        it = s.sb("impt", [128, 6, NSLC], F32)
        b_it = bufs("impt", 6)
        m8 = s.sb("m8", [128, 8], F32)
        b_m8 = Buf("m8")
        selb = s.sb("selT", [128, 128], BF16)
        b_selT = Buf("selT")
        selhi = s.sb("selThi", [32, 128], BF16)
        ones_bf = s.ones
        cki = [0]
        for g in range(G):
            for i in range(NQT):
                qc0 = 32 + i * 128
                P.op("dve", lambda e: e.tensor_scalar(out=vmask[:], in0=cio_t[:], scalar1=t0_t[:, i:i + 1],
                                                      scalar2=0.0, op0=ALU.add, op1=ALU.is_ge),
                     reads=[cio_b, t0_b], writes=[b_vmask])
                gbs = []
                for hf in range(NHF):
                    gk = (g * NQT * NHF + i * NHF + hf) % 2
                    gbv, gbb = gbt[gk], b_gb[gk]
                    for br in range(3):
                        r0 = br * H + g * R + hf * HH
                        P.dma("sp", gbv[:, br, :].rearrange("p (r q) -> p r q", r=HH),
                              gates_d[r0:r0 + HH, i * 128:(i + 1) * 128].partition_broadcast(128),
                              reads=[b_gd], writes=gbb)
                    gbs.append((gbv, gbb))

                def qrhs(hf):
                    h0 = g * R + hf * HH
                    return s.bufA[:, h0:h0 + HH, qc0:qc0 + 128]

                def attend(hf, ktiles, po, pob, pl, plb, keep=None):
                    n = len(ktiles)
                    for idx, (ka, kb_, va, vb_, mfn) in enumerate(ktiles):
                        pss, pssb = s.ps.next()
                        P.op("pe", lambda e: e.matmul(pss[:, 0:CW], lhsT=ka, rhs=qrhs(hf), start=True, stop=True),
                             reads=kb_ + s.bA[g * R + hf * HH:g * R + hf * HH + HH], writes=[pssb])
                        e1, e1b = Et[:, eti[0] % 3, 0:CW], b_Et[eti[0] % 3]
                        eti[0] += 1
                        P.op("act", lambda e: e.activation(out=e1, in_=pss[:, 0:CW], func=AF.Exp), reads=[pssb],
                             writes=[e1b])
                        if keep is not None:
                            e2, e2b = keep[idx]
                        else:
                            e2, e2b = Et[:, eti[0] % 3, 0:CW], b_Et[eti[0] % 3]
                            eti[0] += 1
                        mfn(e1, e1b, e2, e2b)
                        P.op("pe", lambda e: e.matmul(po[:, 0:CW], lhsT=va, rhs=e2, start=(idx == 0),
                                                      stop=(idx == n - 1)), reads=vb_ + [e2b], writes=[pob])
                        P.op("pe", lambda e: e.matmul(pl[:, 0:CW], lhsT=ones_bf[:], rhs=e2, start=(idx == 0),
                                                      stop=(idx == n - 1)), reads=[e2b, s.b_ones], writes=[plb])

                def bmask(mask_ap, mbufs, scal=None, sbufs=()):
                    def f(e1, e1b, e2, e2b):
                        e13 = e1.rearrange("p (r q) -> p r q", r=HH)
                        e23 = e2.rearrange("p (r q) -> p r q", r=HH)
                        mb = mask_ap.unsqueeze(1).broadcast_to([128, HH, 128])
                        if scal is None:
                            P.op("dve", lambda e: e.tensor_tensor(out=e23, in0=e13, in1=mb, op=ALU.mult),
                                 reads=[e1b] + mbufs, writes=[e2b])
                        else:
                            P.op("dve", lambda e: e.scalar_tensor_tensor(out=e23, in0=e13, scalar=scal, in1=mb,
                                                                         op0=ALU.mult, op1=ALU.mult),
                                 reads=[e1b] + mbufs + list(sbufs), writes=[e2b])
                    return f

                def finish_branch(br, hf, po, pob, pl, plb, first, last, want_rl=False):
                    gbv, gbb = gbs[hf]
                    rl, rlb = rlslot()
                    P.op("dve", lambda e: e.tensor_scalar(out=rl[:, 0:CW], in0=pl[:, 0:CW], scalar1=1e-30,
                                                          scalar2=None, op0=ALU.max), reads=[plb], writes=[rlb])
                    P.op("dve", lambda e: e.reciprocal(out=rl[:, 0:CW], in_=rl[:, 0:CW]), reads=[rlb], writes=[rlb])
                    wt_, wtb_ = rlslot()
                    P.op("dve", lambda e: e.tensor_tensor(out=wt_[:, 0:CW], in0=rl[:, 0:CW], in1=gbv[:, br, :],
                                                          op=ALU.mult), reads=[rlb] + gbb, writes=[wtb_])
                    if first:
                        P.op("dve", lambda e: e.tensor_tensor(out=accs[:, hf, 0:CW], in0=po[:, 0:CW], in1=wt_[:, 0:CW],
                                                              op=ALU.mult), reads=[pob, wtb_], writes=[b_acc[hf]])
                    else:
                        P.op("dve", lambda e: e.tensor_tensor(out=wt_[:, 0:CW], in0=po[:, 0:CW], in1=wt_[:, 0:CW],
                                                              op=ALU.mult), reads=[pob, wtb_], writes=[wtb_])
                        if not last:
                            P.op("dve", lambda e: e.tensor_tensor(out=accs[:, hf, 0:CW], in0=accs[:, hf, 0:CW],
                                                                  in1=wt_[:, 0:CW], op=ALU.add),
                                 reads=[b_acc[hf], wtb_], writes=[b_acc[hf]])
                        else:
                            ob_, obb_ = obf[:, obi[0] % 2, 0:CW], b_obf[obi[0] % 2]
                            obi[0] += 1
                            P.op("dve", lambda e: e.tensor_tensor(out=ob_, in0=accs[:, hf, 0:CW], in1=wt_[:, 0:CW],
                                                                  op=ALU.add), reads=[b_acc[hf], wtb_], writes=[obb_])
                            h0 = g * R + hf * HH
                            P.dma("act", ot_d[h0:h0 + HH, :, i * 128:(i + 1) * 128].rearrange("h p q -> p h q"),
                                  ob_.rearrange("p (r q) -> p r q", r=HH), reads=[obb_], writes=b_ot[h0:h0 + HH])
                    return rl, rlb

                (pimp_,), impid = s.ps.hold(1)
                pimp, pimpb = pimp_
                held = []
                for hf in range(NHF):
                    (a, b), ids = s.ps.hold(2)
                    held.append((a, b, ids))
                    po, pob = a
                    pl, plb = b
                    kt_list = []
                    for kt in range(NCT):
                        kt_list.append((kcT[:, g, kt * 128:(kt + 1) * 128], [b_kcT[g]], vcm[:, g, kt, :], [b_vcm[g]],
                                        bmask(vmask[:, kt, :], [b_vmask])))
                    keep = [(Ec[:, hf, kt, :], b_Ec[hf]) for kt in range(NCT)]
                    attend(hf, kt_list, po, pob, pl, plb, keep=keep)
                    rl, rlb = finish_branch(0, hf, po, pob, pl, plb, True, False)
                    for kt in range(NCT):
                        pc, pcb = Et[:, eti[0] % 3, 0:CW], b_Et[eti[0] % 3]
                        eti[0] += 1
                        P.op("dve", lambda e: e.tensor_tensor(out=pc, in0=Ec[:, hf, kt, :], in1=rl[:, 0:CW],
                                                              op=ALU.mult), reads=[b_Ec[hf], rlb], writes=[pcb])

                        def mmi(e):
                            ins = None
                            for r in range(HH):
                                first = (hf == 0 and kt == 0 and r == 0)
                                lastm = (hf == NHF - 1 and kt == NCT - 1 and r == HH - 1)
                                ins = e.matmul(pimp[:, 0:NSLC], lhsT=pc[:, r * 128:(r + 1) * 128],
                                               rhs=cmat_t[:, kt, :], start=first, stop=lastm)
                            return ins
                        P.op("pe", mmi, reads=[pcb, cmat_b], writes=[pimpb])
                    s.ps.unhold(ids)
                dist, valid, forced, impp, imp2, selc = [it[:, k, :] for k in range(6)]
                bd, bv, bf_, bi_, bi2, bs_ = b_it
                P.op("dve", lambda e: e.tensor_scalar(out=dist, in0=iotab_t[:], scalar1=-1.0,
                                                      scalar2=tq_t[:, i:i + 1], op0=ALU.mult, op1=ALU.add),
                     reads=[iotab_b, tq_b], writes=[bd])
                P.op("dve", lambda e: e.tensor_scalar(out=valid, in0=dist, scalar1=0.0, scalar2=None,
                                                      op0=ALU.is_ge), reads=[bd], writes=[bv])
                P.op("dve", lambda e: e.tensor_scalar(out=forced, in0=dist, scalar1=2.0, scalar2=None,
                                                      op0=ALU.is_lt), reads=[bd], writes=[bf_])
                P.op("dve", lambda e: e.tensor_tensor(out=forced, in0=forced, in1=valid, op=ALU.mult),
                     reads=[bf_, bv], writes=[bf_])
                P.op("dve", lambda e: e.tensor_tensor(out=forced, in0=forced, in1=b0_t[:], op=ALU.max),
                     reads=[bf_, b0_b], writes=[bf_])
                P.op("dve", lambda e: e.scalar_tensor_tensor(out=impp, in0=pimp[:, 0:NSLC], scalar=1.0, in1=valid,
                                                             op0=ALU.add, op1=ALU.mult), reads=[pimpb, bv],
                     writes=[bi_])
                P.op("dve", lambda e: e.scalar_tensor_tensor(out=impp, in0=forced, scalar=1e4, in1=impp,
                                                             op0=ALU.mult, op1=ALU.add), reads=[bf_, bi_],
                     writes=[bi_])
                P.op("dve", lambda e: e.tensor_scalar(out=impp, in0=impp, scalar1=-1.0, scalar2=None, op0=ALU.add),
                     reads=[bi_], writes=[bi_])
                cur, curb = impp, bi_
                for rnd in range(cfg.TOPK // 8):
                    P.op("dve", lambda e: e.max(out=m8[:], in_=cur), reads=[curb], writes=[b_m8])
                    if rnd < cfg.TOPK // 8 - 1:
                        P.op("dve", lambda e: e.match_replace(out=imp2, in_to_replace=m8[:], in_values=cur,
                                                              imm_value=-2.0), reads=[curb, b_m8], writes=[bi2])
                        cur, curb = imp2, bi2
                P.op("dve", lambda e: e.tensor_scalar(out=selc, in0=impp, scalar1=m8[:, 7:8], scalar2=None,
                                                      op0=ALU.is_ge), reads=[bi_, b_m8], writes=[bs_])
                P.op("dve", lambda e: e.tensor_tensor(out=selc, in0=selc, in1=valid, op=ALU.mult),
                     reads=[bs_, bv], writes=[bs_])
                s.ps.unhold(impid)
                pT, pTb = s.ps.next()
                P.op("pe", lambda e: e.transpose(pT[0:NSLC, 0:128], selc, ident_t[:]), reads=[bs_, ident_b],
                     writes=[pTb])
                P.op("act", lambda e: e.activation(out=selb[0:NSLC, :], in_=pT[0:NSLC, 0:128], func=AF.Identity),
                     reads=[pTb], writes=[b_selT])
                if NSLC > 96:
                    pT2, pT2b = s.ps.next()
                    P.op("pe", lambda e: e.transpose(pT2[0:32, 0:128], selc[:, 96:128], ident_t[:]),
                         reads=[bs_, ident_b], writes=[pT2b])
                    P.op("act", lambda e: e.activation(out=selhi[0:32, :], in_=pT2[0:32, 0:128], func=AF.Identity),
                         reads=[pT2b, b_selT], writes=[b_selT])
                nkt = 8 * i + 8
                for ch in range(nkt // 8):
                    k_ = cki[0] % 2
                    cki[0] += 1
                    P.dma("sp", kch[k_], ksl[g, :, ch * 1024:(ch + 1) * 1024], writes=[b_kch[k_]])
                    P.dma("sp", vch[k_], vsl[ch * 8:(ch + 1) * 8, :, g * 128:(g + 1) * 128].rearrange("k p d -> p k d"),
                          writes=[b_vch[k_]])
                    wbase = ((ch * 16) // KPW) * KPW
                    pb0 = wbase % 128
                    for half4 in range(2):
                        pm, pmb = s.ps.next()

                        def mmx(e):
                            ins = None
                            for kk in range(4):
                                ktl = half4 * 4 + kk
                                colo = ((ch * 16) - wbase) * 64 + ktl * 128
                                if pb0 == 96:
                                    ins = e.matmul(pm[:, kk * 128:(kk + 1) * 128],
                                                   lhsT=x0r[0:KPW, colo:colo + 128],
                                                   rhs=selhi[0:KPW, :], start=True, stop=True)
                                else:
                                    ins = e.matmul(pm[:, kk * 128:(kk + 1) * 128],
                                                   lhsT=x0r[pb0:pb0 + KPW, colo:colo + 128],
                                                   rhs=selb[wbase:wbase + KPW, :], start=True, stop=True)
                            return ins
                        P.op("pe", mmx, reads=[b_x0r, b_selT], writes=[pmb])
                        if ch == nkt // 8 - 1:
                            P.op("dve", lambda e: e.tensor_tensor(
                                out=mch[k_][:, half4 * 4:half4 * 4 + 4, :],
                                in0=pm[:, 0:512].rearrange("p (k q) -> p k q", k=4),
                                in1=dm[:, half4 * 4:half4 * 4 + 4, :], op=ALU.mult),
                                reads=[pmb, b_dm], writes=[b_mch[k_]])
                        else:
                            P.op("act", lambda e: e.activation(
                                out=mch[k_][:, half4 * 4:half4 * 4 + 4, :],
                                in_=pm[:, 0:512].rearrange("p (k q) -> p k q", k=4), func=AF.Identity),
                                reads=[pmb], writes=[b_mch[k_]])
                    if ch == 0:
                        sheld = []
                        for hf in range(NHF):
                            (a, b), ids = s.ps.hold(2)
                            sheld.append((a, b, ids))
                    for hf in range(NHF):
                        (po, pob), (pl, plb), _ = sheld[hf]
                        for ktl in range(8):
                            kt = ch * 8 + ktl
                            pss, pssb = s.ps.next()
                            P.op("pe", lambda e: e.matmul(pss[:, 0:CW], lhsT=kch[k_][:, ktl * 128:(ktl + 1) * 128],
                                                          rhs=qrhs(hf), start=True, stop=True),
                                 reads=[b_kch[k_]] + s.bA[g * R + hf * HH:g * R + hf * HH + HH], writes=[pssb])
                            e1, e1b = Et[:, eti[0] % 3, 0:CW], b_Et[eti[0] % 3]
                            eti[0] += 1
                            P.op("act", lambda e: e.activation(out=e1, in_=pss[:, 0:CW], func=AF.Exp),
                                 reads=[pssb], writes=[e1b])
                            e2, e2b = Et[:, eti[0] % 3, 0:CW], b_Et[eti[0] % 3]
                            eti[0] += 1
                            bmask(mch[k_][:, ktl, :], [b_mch[k_]])(e1, e1b, e2, e2b)
                            P.op("pe", lambda e: e.matmul(po[:, 0:CW], lhsT=vch[k_][:, ktl, :], rhs=e2,
                                                          start=(kt == 0), stop=(kt == nkt - 1)),
                                 reads=[b_vch[k_], e2b], writes=[pob])
                            P.op("pe", lambda e: e.matmul(pl[:, 0:CW], lhsT=ones_bf[:], rhs=e2, start=(kt == 0),
                                                          stop=(kt == nkt - 1)), reads=[e2b, s.b_ones], writes=[plb])
                for hf in range(NHF):
                    (po, pob), (pl, plb), ids = sheld[hf]
                    finish_branch(1, hf, po, pob, pl, plb, False, False)
                    s.ps.unhold(ids)
                wk = (g * NQT + i) % 2
                P.dma("sp", kwt[:, wk, :], kwd[i, g, :, :], writes=[b_kw[wk]])
                P.dma("sp", vwt[:, wk, :].rearrange("p (k d) -> p k d", k=NWT),
                      vwd[i, :, :, g * 128:(g + 1) * 128].rearrange("k p d -> p k d"), writes=[b_vw[wk]])
                for hf in range(NHF):
                    (a, b), ids = s.ps.hold(2)
                    po, pob = a
                    pl, plb = b
                    kt_list = []
                    for jt in range(NWT):
                        kt_list.append((kwt[:, wk, jt * 128:(jt + 1) * 128], [b_kw[wk]],
                                        vwt[:, wk, jt * 128:(jt + 1) * 128], [b_vw[wk]],
                                        bmask(wmask_t[:, jt, :], [wmask_b],
                                              scal=wv_t[:, i * NWT + jt:i * NWT + jt + 1], sbufs=[wv_b])))
                    attend(hf, kt_list, po, pob, pl, plb)
                    finish_branch(2, hf, po, pob, pl, plb, False, True)
                    s.ps.unhold(ids)
        s.alias(s.bB, newb)
        s.alias([s.b_rstd, s.b_rstd2], b_rl)
        for h in range(H):
            P.dma("sp", s.bufA[:, h, 32:32 + T], ot_d[h, :, :], reads=[b_ot[h]], writes=[s.bA[h]])
        for n in range(DC):
            wt, wtb = s.ws.load(wo[n, :, :], H * 128)
            xb, xbb = s.xbuf()
            P.dma("sp", xb[:, 0:T], xin[n, :, :], reads=[xin_bufs[n]], writes=[xbb])
            for (c0, w) in sg:
                po, pob = s.ps.next()

                def mmo(e):
                    ins = None
                    for kc in range(H):
                        ins = e.matmul(po[:, 0:w], lhsT=wt[:, kc * 128:(kc + 1) * 128],
                                       rhs=s.bufA[:, kc, 32 + c0:32 + c0 + w], start=(kc == 0), stop=(kc == H - 1))
                    return ins
                P.op("pe", mmo, reads=[wtb] + s.bA, writes=[pob])
                P.op("dve", lambda e: e.scalar_tensor_tensor(out=xb[:, c0:c0 + w], in0=po[:, 0:w],
                                                             scalar=s.mvec(2, n), in1=xb[:, c0:c0 + w],
                                                             op0=ALU.mult, op1=ALU.add),
                     reads=[pob, xbb, s.b_m], writes=[xbb])
            P.dma("act", xs[n, :, :], xb[:, 0:T], reads=[xbb], writes=[xs_bufs[n]])

    def normrope_sb(s, tk, b_tk, w, gcol, b_g, cc, sc_, b_ccs, dst, dstb):
        P = s.P
        sq, sqb = s.sq_slot()
        P.op("act", lambda e: e.activation(out=sq[:, 0:w], in_=tk[:, 0:w], func=AF.Square), reads=[b_tk], writes=[sqb])
        k1, k1b = s.k1b[:, s.k1i % 2, :], s.b_k1[s.k1i % 2]
        s.k1i += 1
        P.op("act", lambda e: e.activation(out=k1[:, 0:w], in_=tk[:, 0:w], func=AF.Identity, scale=gcol),
             reads=[b_tk, b_g], writes=[k1b])
        pss, pssb = s.ps.next()
        P.op("pe", lambda e: e.matmul(pss[:, 0:w], lhsT=s.ones[:], rhs=sq[:, 0:w], start=True, stop=True),
             reads=[sqb, s.b_ones], writes=[pssb])
        pr, prb = s.ps.next()
        P.op("pe", lambda e: e.matmul(pr[:, 0:w], lhsT=s.rotm[:], rhs=k1[:, 0:w], start=True, stop=True),
             reads=[k1b, s.b_rotm], writes=[prb])
        rk, t1, t2 = [s.nr_t[:, k, :] for k in range(3)]
        rkb, t1b, t2b = s.b_nr
        P.op("act", lambda e: e.activation(out=rk[:, 0:w], in_=pss[:, 0:w], func=AF.Sqrt, bias=s.epsc[:, 0:1],
                                           scale=1.0 / 128.0), reads=[pssb, s.b_eps], writes=[rkb])
        P.op("dve", lambda e: e.reciprocal(out=rk[:, 0:w], in_=rk[:, 0:w]), reads=[rkb], writes=[rkb])
        P.op("dve", lambda e: e.tensor_tensor(out=t1[:, 0:w], in0=k1[:, 0:w], in1=cc[:, 0:w], op=ALU.mult),
             reads=[k1b, b_ccs], writes=[t1b])
        P.op("dve", lambda e: e.tensor_tensor(out=t2[:, 0:w], in0=pr[:, 0:w], in1=sc_[:, 0:w], op=ALU.mult),
             reads=[prb, b_ccs], writes=[t2b])
        P.op("dve", lambda e: e.tensor_tensor(out=t1[:, 0:w], in0=t1[:, 0:w], in1=t2[:, 0:w], op=ALU.add),
             reads=[t1b, t2b], writes=[t1b])
        P.op("dve", lambda e: e.tensor_tensor(out=dst, in0=t1[:, 0:w], in1=rk[:, 0:w], op=ALU.mult),
             reads=[t1b, rkb], writes=[dstb])

    def ffn(s, xs, xs_bufs, xdst, xdst_bufs, final_is_output):
        cfg, P = s.cfg, s.P
        DC, T, FC = cfg.DC, cfg.T, cfg.FC
        wg = s.din("w_gu_l", [2 * FC, 128, DC * 128])
        wd = s.din("w_dn_l", [DC, 128, FC * 128])
        sg = segs_of(T)
        s.rms_stats(lambda j: xs[j, :, :], xs_bufs, T, s.rstd, s.b_rstd)
        s.modulate(lambda j: xs[j, :, :], xs_bufs, T, s.rstd, s.b_rstd, s.Affn, lambda j: s.mvec(3, j),
                   lambda j: s.bufB[:, j, :], s.bB)
        sgt = s.sb("sgt", [128, 2, 512], BF16)
        b_sgt = bufs("sgt", 2)
        sgi = 0
        f0 = 0
        nq = len(cfg.FQ)
        for qi, nf in enumerate(cfg.FQ):
            last_q = qi == nq - 1
            for fl in range(nf):
                f = f0 + fl
                wgt, wgb = s.ws.load(wg[f, :, :], DC * 128)
                wut, wub = s.ws.load(wg[FC + f, :, :], DC * 128)
                for (c0, w) in sg:
                    pg, pgb = s.ps.next()
                    pu, pub = s.ps.next()

                    def mmg(e, wt=wgt, pt=pg):
                        ins = None
                        for kc in range(DC):
                            ins = e.matmul(pt[:, 0:w], lhsT=wt[:, kc * 128:(kc + 1) * 128],
                                           rhs=s.bufB[:, kc, c0:c0 + w], start=(kc == 0), stop=(kc == DC - 1))
                        return ins
                    P.op("pe", mmg, reads=[wgb] + s.bB, writes=[pgb])
                    P.op("pe", lambda e: mmg(e, wut, pu), reads=[wub] + s.bB, writes=[pub])
                    st_, stb_ = sgt[:, sgi % 2, :], b_sgt[sgi % 2]
                    sgi += 1
                    P.op("act", lambda e: e.activation(out=st_[:, 0:w], in_=pg[:, 0:w], func=AF.Silu),
                         reads=[pgb], writes=[stb_])
                    P.op("dve", lambda e: e.tensor_tensor(out=s.bufA[:, fl, c0:c0 + w], in0=st_[:, 0:w],
                                                          in1=pu[:, 0:w], op=ALU.mult),
                         reads=[stb_, pub], writes=[s.bA[fl]])
            for n in range(DC):
                nsub = -(-nf * 128 // (DC * 128))
                per = -(-nf // nsub)
                subs = []
                for su in range(nsub):
                    a, b = su * per, min(nf, (su + 1) * per)
                    wt, wb_ = s.ws.load(wd[n, :, (f0 + a) * 128:(f0 + b) * 128], (b - a) * 128)
                    subs.append((a, b, wt, wb_))
                src, srcb = (xs, xs_bufs) if True else None
                xb, xbb = s.xbuf()
                P.dma("sp", xb[:, 0:T], xs[n, :, :], reads=[xs_bufs[n]], writes=[xbb])
                for (c0, w) in sg:
                    po, pob = s.ps.next()

                    def mmd(e):
                        ins = None
                        for (a, b, wt, _) in subs:
                            for fl in range(a, b):
                                ins = e.matmul(po[:, 0:w], lhsT=wt[:, (fl - a) * 128:(fl - a + 1) * 128],
                                               rhs=s.bufA[:, fl, c0:c0 + w], start=(fl == 0), stop=(fl == nf - 1))
                        return ins
                    P.op("pe", mmd, reads=[x[3] for x in subs] + s.bA[0:nf], writes=[pob])
                    P.op("dve", lambda e: e.scalar_tensor_tensor(out=xb[:, c0:c0 + w], in0=po[:, 0:w],
                                                                 scalar=s.mvec(5, n), in1=xb[:, c0:c0 + w],
                                                                 op0=ALU.mult, op1=ALU.add),
                         reads=[pob, xbb, s.b_m], writes=[xbb])
                if last_q:
                    P.dma("act", xdst[n, :, :], xb[:, 0:T], reads=[xbb], writes=[xdst_bufs[n]],
                          is_output=final_is_output)
                else:
                    P.dma("act", xs[n, :, :], xb[:, 0:T], reads=[xbb], writes=[xs_bufs[n]])
            f0 += nf

    def conv(s, xin, xin_bufs, xh, xh_bufs, xs, xs_bufs):
        cfg, P = s.cfg, s.P
        DC, T, D, CW = cfg.DC, cfg.T, cfg.D, cfg.CONVW
        w1 = s.din("w_pw1_l", [2 * DC, 128, DC * 128])
        w2 = s.din("w_pw2_l", [DC, 128, DC * 128])
        b1_t, b1_b = s.load_small("b_pw1_l", [128, 2 * DC])
        wdw_t, wdw_b = s.load_small("w_dw_l", [128, DC, CW])
        bdw_t, bdw_b = s.load_small("b_dw_l", [128, DC])
        lg_t, lg_b = s.load_small("ln_g_l", [128, DC])
        lb_t, lb_b = s.load_small("ln_b_l", [128, DC])
        b2_t, b2_b = s.load_small("b_pw2_l", [128, DC])
        hm_t, hm_b = s.load_small("hmask", [128, 1])
        hTh = s.sb("hTh", [128, DC, 32], BF16)
        b_hTh = bufs("hTh", DC)
        rsh = s.sb("rsh", [128, 32], F32)
        b_rsh = Buf("rsh")
        sg = segs_of(T)
        s.rms_stats(lambda j: xin[j, :, :], xin_bufs, T, s.rstd, s.b_rstd)
        s.rms_stats(lambda j: xh[j, :, :], xh_bufs, 32, rsh, b_rsh)
        s.modulate(lambda j: xin[j, :, :], xin_bufs, T, s.rstd, s.b_rstd, s.Amix, lambda j: s.mvec(0, j),
                   lambda j: s.bufB[:, j, :], s.bB)
        s.modulate(lambda j: xh[j, :, :], xh_bufs, 32, rsh, b_rsh, s.Amix, lambda j: s.mvec(0, j),
                   lambda j: hTh[:, j, :], b_hTh)
        sig = s.sb("sig", [128, 2, 512], BF16)
        b_sig = bufs("sig", 2)
        sgi = 0
        ps1, ids1 = s.ps.hold(len(sg))
        ps2, ids2 = s.ps.hold(len(sg))
        allseg = [(-1, 0, 32)] + [(i, c0, w) for i, (c0, w) in enumerate(sg)]
        for j in range(DC):
            wa, wab = s.ws.load(w1[j, :, :], DC * 128)
            wg_, wgb = s.ws.load(w1[DC + j, :, :], DC * 128)
            for (si, c0, w) in allseg:
                pa, pab = s.ps.next()
                pg, pgb = s.ps.next()
                if si < 0:
                    rhs = lambda kc: hTh[:, kc, :]
                    rb = b_hTh
                    dcol = 0
                else:
                    rhs = lambda kc, c0=c0, w=w: s.bufB[:, kc, c0:c0 + w]
                    rb = s.bB
                    dcol = 32 + c0

                def mm(e, wt, pt):
                    ins = None
                    for kc in range(DC):
                        ins = e.matmul(pt[:, 0:w], lhsT=wt[:, kc * 128:(kc + 1) * 128], rhs=rhs(kc),
                                       start=(kc == 0), stop=(kc == DC - 1))
                    return ins
                P.op("pe", lambda e: mm(e, wa, pa), reads=[wab] + rb, writes=[pab])
                P.op("pe", lambda e: mm(e, wg_, pg), reads=[wgb] + rb, writes=[pgb])
                sg_, sgb_ = sig[:, sgi % 2, :], b_sig[sgi % 2]
                sgi += 1
                P.op("act", lambda e: e.activation(out=sg_[:, 0:w], in_=pg[:, 0:w], func=AF.Sigmoid,
                                                   bias=b1_t[:, DC + j:DC + j + 1], scale=1.0),
                     reads=[pgb, b1_b], writes=[sgb_])
                P.op("dve", lambda e: e.scalar_tensor_tensor(out=s.bufA[:, j, dcol:dcol + w], in0=pa[:, 0:w],
                                                             scalar=b1_t[:, j:j + 1], in1=sg_[:, 0:w],
                                                             op0=ALU.add, op1=ALU.mult),
                     reads=[pab, sgb_, b1_b], writes=[s.bA[j]])
                if si < 0:
                    P.op("dve", lambda e: e.tensor_scalar(out=s.bufA[:, j, 0:32], in0=s.bufA[:, j, 0:32],
                                                          scalar1=hm_t[:, 0:1], scalar2=None, op0=ALU.mult),
                         reads=[s.bA[j], hm_b], writes=[s.bA[j]])
            y, yb = s.xbuf()
            off = 32 - (CW - 1)
            for k in range(CW):
                if k == 0:
                    P.op("dve", lambda e: e.tensor_scalar(out=y, in0=s.bufA[:, j, off:off + T],
                                                          scalar1=wdw_t[:, j, 0:1], scalar2=bdw_t[:, j:j + 1],
                                                          op0=ALU.mult, op1=ALU.add),
                         reads=[s.bA[j], wdw_b, bdw_b], writes=[yb])
                else:
                    P.op("dve", lambda e: e.scalar_tensor_tensor(out=y, in0=s.bufA[:, j, off + k:off + k + T],
                                                                 scalar=wdw_t[:, j, k:k + 1], in1=y,
                                                                 op0=ALU.mult, op1=ALU.add),
                         reads=[s.bA[j], wdw_b, yb], writes=[yb])
            P.op("act", lambda e: e.activation(out=s.bufA[:, j, 32:32 + T], in_=y, func=AF.Identity), reads=[yb],
                 writes=[s.bA[j]])
            for si, (c0, w) in enumerate(sg):
                sq, sqb = s.sq_slot()
                P.op("act", lambda e: e.activation(out=sq[:, 0:w], in_=y[:, c0:c0 + w], func=AF.Square),
                     reads=[yb], writes=[sqb])
                p1, p1b = ps1[si]
                p2, p2b = ps2[si]
                P.op("pe", lambda e: e.matmul(p1[:, 0:w], lhsT=s.ones[:], rhs=s.bufA[:, j, 32 + c0:32 + c0 + w],
                                              start=(j == 0), stop=(j == DC - 1)),
                     reads=[s.bA[j], s.b_ones], writes=[p1b])
                P.op("pe", lambda e: e.matmul(p2[:, 0:w], lhsT=s.ones[:], rhs=sq[:, 0:w],
                                              start=(j == 0), stop=(j == DC - 1)),
                     reads=[sqb, s.b_ones], writes=[p2b])
        mean, mean_b = s.rstd2, s.b_rstd2
        for si, (c0, w) in enumerate(sg):
            p1, p1b = ps1[si]
            p2, p2b = ps2[si]
            P.op("act", lambda e: e.activation(out=mean[:, c0:c0 + w], in_=p1[:, 0:w], func=AF.Identity,
                                               scale=1.0 / D), reads=[p1b], writes=[mean_b])
            xb, xbb = s.xbuf()
            P.op("dve", lambda e: e.tensor_tensor(out=xb[:, 0:w], in0=mean[:, c0:c0 + w], in1=mean[:, c0:c0 + w],
                                                  op=ALU.mult), reads=[mean_b], writes=[xbb])
            P.op("dve", lambda e: e.scalar_tensor_tensor(out=xb[:, 0:w], in0=p2[:, 0:w], scalar=1.0 / D,
                                                         in1=xb[:, 0:w], op0=ALU.mult, op1=ALU.subtract),
                 reads=[p2b, xbb], writes=[xbb])
            P.op("act", lambda e: e.activation(out=s.rstd[:, c0:c0 + w], in_=xb[:, 0:w], func=AF.Sqrt,
                                               bias=s.epsc[:, 0:1], scale=1.0), reads=[xbb, s.b_eps],
                 writes=[s.b_rstd])
        P.op("dve", lambda e: e.reciprocal(out=s.rstd[:, 0:T], in_=s.rstd[:, 0:T]), reads=[s.b_rstd],
             writes=[s.b_rstd])
        s.ps.unhold(ids1 + ids2)
        for j in range(DC):
            xb, xbb = s.xbuf()
            P.op("dve", lambda e: e.tensor_tensor(out=xb[:, 0:T], in0=s.bufA[:, j, 32:32 + T], in1=mean[:, 0:T],
                                                  op=ALU.subtract), reads=[s.bA[j], mean_b], writes=[xbb])
            P.op("dve", lambda e: e.tensor_tensor(out=xb[:, 0:T], in0=xb[:, 0:T], in1=s.rstd[:, 0:T], op=ALU.mult),
                 reads=[xbb, s.b_rstd], writes=[xbb])
            P.op("act", lambda e: e.activation(out=s.bufA[:, j, 32:32 + T], in_=xb[:, 0:T], func=AF.Silu,
                                               bias=lb_t[:, j:j + 1], scale=lg_t[:, j:j + 1]),
                 reads=[xbb, lg_b, lb_b], writes=[s.bA[j]])
        b2g = s.sb("b2g", [128, DC], F32)
        b_b2g = Buf("b2g")
        P.op("dve", lambda e: e.tensor_tensor(out=b2g[:], in0=b2_t[:], in1=s.m[:, 2 * DC:3 * DC], op=ALU.mult),
             reads=[b2_b, s.b_m], writes=[b_b2g])
        for n in range(DC):
            wt, wtb = s.ws.load(w2[n, :, :], DC * 128)
            xb, xbb = s.xbuf()
            P.dma("sp", xb[:, 0:T], xin[n, :, :], reads=[xin_bufs[n]], writes=[xbb])
            for (c0, w) in sg:
                po, pob = s.ps.next()

                def mm2(e):
                    ins = None
                    for kc in range(DC):
                        ins = e.matmul(po[:, 0:w], lhsT=wt[:, kc * 128:(kc + 1) * 128],
                                       rhs=s.bufA[:, kc, 32 + c0:32 + c0 + w], start=(kc == 0), stop=(kc == DC - 1))
                    return ins
                P.op("pe", mm2, reads=[wtb] + s.bA, writes=[pob])
                P.op("dve", lambda e: e.scalar_tensor_tensor(out=xb[:, c0:c0 + w], in0=po[:, 0:w],
                                                             scalar=s.mvec(2, n), in1=xb[:, c0:c0 + w],
                                                             op0=ALU.mult, op1=ALU.add),
                     reads=[pob, xbb, s.b_m], writes=[xbb])
            P.op("pool", lambda e: e.tensor_scalar(out=xb[:, 0:T], in0=xb[:, 0:T], scalar1=b2g[:, n:n + 1],
                                                   scalar2=None, op0=ALU.add), reads=[xbb, b_b2g], writes=[xbb])
            P.dma("act", xs[n, :, :], xb[:, 0:T], reads=[xbb], writes=[xs_bufs[n]])


def build_conv_launch(cfg, with_mod, with_kv=False):
    L = LayerBuilder(cfg, "conv", with_mod, with_kv)
    nc, DC, T = L.nc, cfg.DC, cfg.T
    L.setup_common()
    xin = L.din("xT", [DC, 128, T])
    xh = L.din("xh", [DC, 128, 32])
    xs = nc.dram_tensor("xs", [DC, 128, T], F32)
    y = L.dout("yT", [DC, 128, T])
    b_xin, b_xh, b_xs, b_y = bufs("xin", DC), bufs("xh", DC), bufs("xs", DC), bufs("y", DC)
    L.vec_setup(False)
    L.conv(xin, b_xin, xh, b_xh, xs, b_xs)
    L.ffn(xs, b_xs, y, b_y, True)
    if with_kv:
        L.kvproj(y, b_y)
    L.P.finish()
    L.es.close()
    return L


def vecP(v):
    v = np.asarray(v, np.float32)
    return np.ascontiguousarray(v.reshape(-1, 128).T)


def wunits(W):
    K, N = W.shape
    a = W.reshape(K // 128, 128, N // 128, 128).transpose(2, 1, 0, 3)
    return np.ascontiguousarray(a).reshape(N // 128, 128, (K // 128) * 128)


def xT_layout(x2d):
    T, D = x2d.shape
    return np.ascontiguousarray(x2d.T).reshape(D // 128, 128, T)


def xT_unlayout(a):
    DC, _, T = a.shape
    return np.ascontiguousarray(a.reshape(DC * 128, T).T)


def conv_launch_inputs(cfg, layer, x_full, mod_in, P):
    DC, T, D, NCr = cfg.DC, cfg.T, cfg.D, cfg.NCORES
    shared = {}
    if mod_in is None:
        shared["c_l"] = vecP(P["c"][0])
        shared["b_ada_l"] = vecP(P["b_ada"])
        shared["w_ada_l"] = np.ascontiguousarray(P["w_ada"]).reshape(DC, 128, 6 * D)
    else:
        shared["mod_in"] = mod_in
    shared["ada_l"] = vecP(P["ada_emb"][layer].reshape(-1))
    shared["norm_mix_l"] = vecP(P["norm_mix"][layer])
    shared["norm_ffn_l"] = vecP(P["norm_ffn"][layer])
    shared["w_pw1_l"] = wunits(P["conv_w_pw1"][layer])
    shared["w_pw2_l"] = wunits(P["conv_w_pw2"][layer])
    shared["b_pw1_l"] = vecP(P["conv_b_pw1"][layer])
    shared["w_dw_l"] = np.ascontiguousarray(P["conv_w_dw"][layer].T.reshape(DC, 128, cfg.CONVW).transpose(1, 0, 2))
    shared["b_dw_l"] = vecP(P["conv_b_dw"][layer])
    shared["ln_g_l"] = vecP(P["conv_ln_g"][layer])
    shared["ln_b_l"] = vecP(P["conv_ln_b"][layer])
    shared["b_pw2_l"] = vecP(P["conv_b_pw2"][layer])
    shared["w_gu_l"] = wunits(P["ffn_w_gu"][layer])
    wd = P["ffn_w_down"][layer]
    shared["w_dn_l"] = wunits(wd)
    maps = []
    for c in range(NCr):
        m = dict(shared)
        xs_ = x_full[c * T:(c + 1) * T]
        m["xT"] = xT_layout(xs_)
        if c == 0:
            m["xh"] = np.zeros((DC, 128, 32), np.float32)
            m["hmask"] = np.zeros((128, 1), np.float32)
        else:
            m["xh"] = xT_layout(x_full[c * T - 32:c * T])
            m["hmask"] = np.ones((128, 1), np.float32)
        maps.append(m)
    return maps


def rope_const_inputs():
    half = 64
    inv = (10000.0 ** (-np.arange(half, dtype=np.float32) / half)).astype(np.float32)
    invf = np.concatenate([inv, inv]).reshape(128, 1).astype(np.float32)
    rotm = np.zeros((128, 128), np.float32)
    for m in range(64):
        rotm[m + 64, m] = -1.0
        rotm[m, m + 64] = 1.0
    return invf, rotm


def kv_launch_extra_inputs(cfg, maps, P):
    DC, T, D = cfg.DC, cfg.T, cfg.D
    invf, rotm = rope_const_inputs()
    kvw = P["kv_w"]
    uf = wunits(kvw)
    sel = [br * 4 + g for br in (0, 1, 2, 4) for g in range(4)]
    w_kvf = np.ascontiguousarray(uf[sel])
    wt = []
    for br in (3, 5):
        w = kvw[:, br * 512:(br + 1) * 512]
        wt.append(np.ascontiguousarray(w.reshape(DC, 128, 512).transpose(1, 0, 2)).reshape(128, DC * 512))
    w_kvt = np.stack(wt)
    pos = np.asarray(P["positions"][0], np.int32)
    for c, m in enumerate(maps):
        m["norm_kv_l"] = vecP(P["norm_kv"])
        m["kv_ada_l"] = vecP(P["kv_ada_emb"].reshape(-1))
        m["kn_l"] = np.ascontiguousarray(np.stack([P["kv_k_norm"][1], P["kv_k_norm"][2]], 1).astype(np.float32))
        m["pos_l"] = np.ascontiguousarray(np.broadcast_to(pos[c * T:(c + 1) * T][None, :], (128, T)))
        m["w_kvf_l"] = w_kvf
        m["w_kvt_l"] = w_kvt
        m["invf"] = invf
        m["rotm"] = rotm
    return maps


def build_nsa_launch(cfg):
    L = LayerBuilder(cfg, "nsa", False)
    nc, DC, T = L.nc, cfg.DC, cfg.T
    L.setup_common()
    xin = L.din("xT", [DC, 128, T])
    xs = nc.dram_tensor("xs", [DC, 128, T], F32)
    y = L.dout("yT", [DC, 128, T])
    b_xin, b_xs, b_y = bufs("xin", DC), bufs("xs", DC), bufs("y", DC)
    L.vec_setup(False)
    L.nsa(xin, b_xin, xs, b_xs)
    L.ffn(xs, b_xs, y, b_y, True)
    L.P.finish()
    L.es.close()
    return L


def bf(a):
    return np.ascontiguousarray(a).astype(ml_dtypes.bfloat16)


def nsa_tile_order(cfg):
    NQT = cfg.T // 128
    return [[8 * i + c for i in range(NQT)] for c in range(cfg.NCORES)]


def nsa_shard_x(cfg, x_full):
    order = nsa_tile_order(cfg)
    out = []
    for c in range(cfg.NCORES):
        rows = np.concatenate([x_full[t * 128:(t + 1) * 128] for t in order[c]], 0)
        out.append(rows)
    return out


def nsa_unshard_x(cfg, parts):
    order = nsa_tile_order(cfg)
    S, D = cfg.S, cfg.D
    x = np.empty((S, D), np.float32)
    for c in range(cfg.NCORES):
        for li, t in enumerate(order[c]):
            x[t * 128:(t + 1) * 128] = parts[c][li * 128:(li + 1) * 128]
    return x


def nsa_launch_inputs(cfg, li, x_full, mod_in, P, kv):
    DC, T, D, NCr, S, G, H = cfg.DC, cfg.T, cfg.D, cfg.NCORES, cfg.S, cfg.G, cfg.H
    layer = 2 + li
    NQT = T // 128
    NSLC, NCMP = cfg.NSLC, cfg.NCMP
    NCT = -(-NCMP // 128)
    NCP = NCT * 128
    NWT = cfg.WINDOW // 128 + 1
    invf, rotm = rope_const_inputs()
    sh = {"mod_in": mod_in, "invf": invf, "rotm": rotm}
    sh["ada_l"] = vecP(P["ada_emb"][layer].reshape(-1))
    sh["norm_mix_l"] = vecP(P["norm_mix"][layer])
    sh["norm_ffn_l"] = vecP(P["norm_ffn"][layer])
    sh["w_gu_l"] = wunits(P["ffn_w_gu"][layer])
    sh["w_dn_l"] = wunits(P["ffn_w_down"][layer])
    sh["w_q_l"] = wunits(P["nsa_w_q"][li])
    sh["w_o_l"] = wunits(P["nsa_w_o"][li])
    wg = P["nsa_w_gate"][li]
    NGT = 3 * H
    sh["w_gate_l"] = np.ascontiguousarray(wg.reshape(DC, 128, NGT).transpose(1, 0, 2)).reshape(128, DC * NGT)
    sh["b_gate_l"] = np.ascontiguousarray(P["nsa_b_gate"][li].reshape(NGT, 1).astype(np.float32))
    sh["qn_l"] = np.ascontiguousarray(P["nsa_q_norm"][li].reshape(128, 1).astype(np.float32))
    sh["kn0_l"] = np.ascontiguousarray(P["kv_k_norm"][0].reshape(128, 1).astype(np.float32))
    pos = np.asarray(P["positions"][0], np.int32)
    posc = np.zeros((NCP,), np.int32)
    posc[:NCMP] = pos[np.arange(NCMP) * 16 + 31]
    sh["posc_l"] = np.ascontiguousarray(np.broadcast_to(posc[None, :], (128, NCP)))
    sh["ident"] = np.eye(128, dtype=np.float32)
    sh["iotab"] = np.ascontiguousarray(np.broadcast_to(np.arange(NSLC, dtype=np.float32)[None, :], (128, NSLC)))
    sh["b0c"] = np.ascontiguousarray((sh["iotab"] == 0).astype(np.float32))
    jj, qq = np.meshgrid(np.arange(128), np.arange(128), indexing="ij")
    sh["iotad"] = (qq - jj).astype(np.float32)
    cio = np.zeros((128, NCT, 128), np.float32)
    for kt in range(NCT):
        cio[:, kt, :] = qq - 16 * (128 * kt + jj) - 31
    sh["cmpiota"] = cio
    r, cl = 4, 2
    offs = (np.arange(r)[:, None] - np.arange(cl)[None, :]).reshape(-1)
    tgt = r * np.arange(NSLC)[None, :, None] + offs[None, None, :]
    C = (np.arange(NCP)[:, None, None] == tgt).sum(-1).astype(np.float32)
    C[NCMP:] = 0
    sh["cmat"] = bf(C.reshape(NCT, 128, NSLC).transpose(1, 0, 2))
    wm = np.zeros((128, NWT, 128), np.float32)
    for jt in range(NWT):
        dist = (cfg.WINDOW + qq) - (128 * jt + jj)
        wm[:, jt, :] = ((dist >= 0) & (dist < cfg.WINDOW)).astype(np.float32)
    sh["wmask"] = bf(wm)
    x0 = np.zeros((128, 2048), np.float32)
    mcol = np.arange(2048) // 64
    for p in range(128):
        x0[p] = (mcol == (p % 32))
    sh["x0r"] = bf(x0)
    sh["cmp_b1_l"] = np.ascontiguousarray(np.concatenate([vecP(P["cmp_b1"][0]), vecP(P["cmp_b1"][1])], 1))
    sh["cmp_b2k_l"] = np.ascontiguousarray(P["cmp_b2"][0].reshape(128, 1).astype(np.float32))
    sh["cmp_b2v_l"] = np.ascontiguousarray(np.broadcast_to(P["cmp_b2"][1][None, :], (128, 128)).astype(np.float32))
    sh["cmp_pos_l"] = bf(np.stack([P["cmp_pos"][0].T, P["cmp_pos"][1].T], 1))
    w2 = np.stack([P["cmp_w2"][j].reshape(4, 128, 128).transpose(1, 0, 2).reshape(128, 512) for j in range(2)], 1)
    sh["cmp_w2_l"] = bf(w2)
    sh["cmp_w1_l"] = np.concatenate([wunits(P["cmp_w1"][0]), wunits(P["cmp_w1"][1])], 0)
    sh["kcrT"], sh["vcrT"], sh["kslT"] = kv["kcrT"], kv["vcrT"], kv["kslT"]
    sh["vslk"] = kv["vsl"].reshape(S // 128, 128, G * 128)
    order = nsa_tile_order(cfg)
    xparts = nsa_shard_x(cfg, x_full)
    kwT_full, vw_full = kv["kwnT"], kv["vwn"]
    maps = []
    for c in range(NCr):
        m = dict(sh)
        m["xT"] = xT_layout(xparts[c])
        tl = order[c]
        posq = np.concatenate([pos[t * 128:(t + 1) * 128] for t in tl])
        m["posq_l"] = np.ascontiguousarray(np.broadcast_to(posq[None, :], (128, T)))
        t0 = np.array([t * 128 for t in tl], np.float32)
        m["t0_l"] = np.ascontiguousarray(np.broadcast_to(t0[None, :], (128, NQT)))
        tq = (t0[None, :] / 64.0 + (np.arange(128)[:, None] >= 64)).astype(np.float32)
        m["tq_l"] = np.ascontiguousarray(tq)
        m["coff_l"] = np.ascontiguousarray(np.broadcast_to((128.0 * (c - np.arange(8)))[None, :], (128, 8)).astype(np.float32))
        wv = np.zeros((128, NQT * NWT), np.float32)
        kw = np.zeros((NQT, G, 128, NWT * 128), ml_dtypes.bfloat16)
        vw = np.zeros((NQT, NWT, 128, G * 128), ml_dtypes.bfloat16)
        for li_, t in enumerate(tl):
            k0 = t * 128 - cfg.WINDOW
            for jt in range(NWT):
                kp = k0 + jt * 128
                if kp >= 0:
                    wv[:, li_ * NWT + jt] = 1.0
                    kw[li_, :, :, jt * 128:(jt + 1) * 128] = kwT_full[:, :, kp:kp + 128]
                    vw[li_, jt] = vw_full[kp:kp + 128]
        m["wvalid_l"] = wv
        m["kwT"] = kw
        m["vwk"] = vw
        maps.append(m)
    return maps


_PROG_CACHE = {}


def _prog(key, fn):
    return fn()


def _run(L, maps, ncores):
    for k, (shp, dt) in L.inputs.items():
        assert k in maps[0], k
        assert tuple(maps[0][k].shape) == tuple(shp), (k, maps[0][k].shape, shp)
    maps = [{k: m[k] for k in L.inputs} for m in maps]
    return run_bass_kernel_spmd(L.nc, maps, core_ids=list(range(ncores))).results


_CFG = None


def kernel(**inputs):
    cfg = _CFG or Cfg()
    P = {k: np.asarray(v) for k, v in inputs.items()}
    NCr, S, T, G = cfg.NCORES, cfg.S, cfg.T, cfg.G
    x0 = np.ascontiguousarray(P["x"][0], dtype=np.float32)
    LA = build_conv_launch(cfg, True, False)
    res = _run(LA, conv_launch_inputs(cfg, 0, x0, None, P), NCr)
    x1 = np.concatenate([xT_unlayout(r["yT"]) for r in res], 0)
    mod_l = np.ascontiguousarray(res[0]["mod_out"])
    del res, LA
    LB = build_conv_launch(cfg, False, True)
    maps = kv_launch_extra_inputs(cfg, conv_launch_inputs(cfg, 1, x1, mod_l, P), P)
    res = _run(LB, maps, NCr)
    x2 = np.concatenate([xT_unlayout(r["yT"]) for r in res], 0)
    kv = {
        "kcrT": np.ascontiguousarray(np.concatenate([np.asarray(r["kcr"]) for r in res], 2)),
        "vcrT": np.ascontiguousarray(np.concatenate([np.asarray(r["vcr"]) for r in res], 2)),
        "kslT": np.ascontiguousarray(np.concatenate([np.asarray(r["ksl"]) for r in res], 2)),
        "kwnT": np.ascontiguousarray(np.concatenate([np.asarray(r["kwn"]) for r in res], 2)),
        "vsl": np.ascontiguousarray(np.concatenate([np.asarray(r["vsl"]) for r in res], 0)),
        "vwn": np.ascontiguousarray(np.concatenate([np.asarray(r["vwn"]) for r in res], 0)),
    }
    del res, LB, maps
    x = x2
    for li in range(2):
        LC = build_nsa_launch(cfg)
        res = _run(LC, nsa_launch_inputs(cfg, li, x, mod_l, P, kv), NCr)
        x = nsa_unshard_x(cfg, [xT_unlayout(r["yT"]) for r in res])
        del res, LC
    return x[None].astype(np.float32)
```

```python
import numpy as np
import ml_dtypes
from contextlib import ExitStack
import concourse.bass as bass
import concourse.mybir as mybir
from concourse.bass_utils import run_bass_kernel_spmd

F32 = mybir.dt.float32
BF16 = mybir.dt.bfloat16
I32 = mybir.dt.int32
AF = mybir.ActivationFunctionType
ALU = mybir.AluOpType
EPS = 1e-6
NQ = 6
CAST_ENG = ("pool", "dve")
DMA_SCRATCH = 1024
NSTAGE = 3


class Cfg:
    def __init__(s, D=4096, S=8192, NCORES=8, CONVW=31, WINDOW=512, TOPK=16):
        s.D, s.S, s.NCORES = D, S, NCORES
        s.DC = D // 128
        s.T = S // NCORES
        s.F = -(-8 * D // (3 * 256)) * 256
        s.FC = s.F // 128
        s.CONVW = CONVW
        s.H = D // 128
        s.G = 4
        s.R = s.H // s.G
        s.WINDOW = WINDOW
        s.TOPK = TOPK
        s.NT = max(1, s.T // 512)
        s.TW = s.T // s.NT
        s.NSLC = S // 64
        s.NCMP = (S - 32) // 16 + 1
        nq = max(4 if s.FC >= 40 else 2, -(-s.FC // s.DC))
        base, rem = divmod(s.FC, nq)
        s.FQ = [base + (1 if i < rem else 0) for i in range(nq)]


class Buf:
    __slots__ = ("name", "w", "r")

    def __init__(s, name):
        s.name, s.w, s.r = name, None, {}


def bufs(name, n):
    return [Buf(f"{name}{i}") for i in range(n)]


class Prog:
    def __init__(s, nc, es):
        s.nc = nc
        s.eng = {"pe": nc.tensor, "act": nc.scalar, "dve": nc.vector, "pool": nc.gpsimd, "sp": nc.sync}
        s.sem = {k: es.enter_context(nc.semaphore("s_" + k)) for k in s.eng}
        s.cnt = {k: 0 for k in s.eng}
        s.waited = {k: {} for k in s.eng}
        s.dsem = {q: [es.enter_context(nc.semaphore(f"d_{q}{i}")) for i in range(NQ)] for q in ("sp", "pool", "act")}
        s.dval = {q: [0] * NQ for q in s.dsem}
        s.dnext = {q: 0 for q in s.dsem}
        s.out_tokens = []
        s.deferred = []

    def defer(s, fn):
        fn()

    def flush(s):
        d, s.deferred = s.deferred, []
        for fn in d:
            fn()

    def _wait(s, e, tok):
        sem, val, key = tok
        if key == "pe" and e == "pe":
            return
        w = s.waited[e]
        if w.get(key, 0) >= val:
            return
        s.eng[e].wait_ge(sem, val)
        w[key] = val

    def _deps(s, e, reads, writes):
        best = {}
        for b in reads:
            if b.w is not None:
                k = b.w[2]
                if k not in best or best[k][1] < b.w[1]:
                    best[k] = b.w
        for b in writes:
            if b.w is not None:
                k = b.w[2]
                if k not in best or best[k][1] < b.w[1]:
                    best[k] = b.w
            for k, t in b.r.items():
                if k not in best or best[k][1] < t[1]:
                    best[k] = t
        for t in best.values():
            s._wait(e, t)

    def _mark(s, tok, reads, writes):
        k = tok[2]
        for b in reads:
            b.r[k] = tok
        for b in writes:
            b.w = tok
            b.r = {}

    def op(s, e, fn, reads=(), writes=()):
        s._deps(e, reads, writes)
        ins = fn(s.eng[e])
        s.cnt[e] += 1
        ins.then_inc(s.sem[e], 1)
        s._mark((s.sem[e], s.cnt[e], e), reads, writes)

    def dma(s, q, out, in_, reads=(), writes=(), is_output=False):
        i = s.dnext[q]
        s.dnext[q] = (i + 1) % NQ
        sem = s.dsem[q][i]
        key = ("d", q, i)
        if s.dval[q][i] > 0:
            s._wait(q, (sem, s.dval[q][i], key))
        s._deps(q, reads, writes)
        ins = s.eng[q].dma_start(out=out, in_=in_)
        s.dval[q][i] += 16
        ins.then_inc(sem, 16)
        tok = (sem, s.dval[q][i], key)
        s._mark(tok, reads, writes)
        if is_output:
            s.out_tokens.append(tok)

    def finish(s):
        for q in s.dsem:
            for i in range(NQ):
                if s.dval[q][i] > 0:
                    s._wait("sp", (s.dsem[q][i], s.dval[q][i], ("d", q, i)))
        for e in ("pe", "act", "dve", "pool"):
            if s.cnt[e] > 0:
                s._wait("sp", (s.sem[e], s.cnt[e], e))


class WStream:
    SP = 2048

    def __init__(s, P, nc, es, E, nstage=NSTAGE, nbf=2):
        s.P, s.E = P, E
        s.stage = [es.enter_context(nc.sbuf_tensor(f"wst{i}", [128, s.SP], F32)) for i in range(nstage)]
        s.sbuf = bufs("wst", nstage)
        s.wb = [es.enter_context(nc.sbuf_tensor(f"wbf{i}", [128, E], BF16)) for i in range(nbf)]
        s.wbuf = bufs("wbf", nbf)
        s.i = 0
        s.j = 0
        s.c = 0

    def stage_slot(s):
        si = s.i % len(s.stage)
        s.i += 1
        return s.stage[si], s.sbuf[si]

    def load(s, src_ap, n):
        P = s.P
        bi = s.j % len(s.wb)
        s.j += 1
        wb, wbb = s.wb[bi], s.wbuf[bi]
        for a in range(0, n, s.SP):
            b = min(n, a + s.SP)
            st, sb = s.stage_slot()
            P.dma("sp", st[:, 0:b - a], src_ap[:, a:b], writes=[sb])
            ce = CAST_ENG[s.c % len(CAST_ENG)]
            s.c += 1
            if ce == "act":
                P.op(ce, lambda e: e.activation(out=wb[:, a:b], in_=st[:, 0:b - a], func=AF.Identity), reads=[sb],
                     writes=[wbb])
            else:
                P.op(ce, lambda e: e.tensor_copy(out=wb[:, a:b], in_=st[:, 0:b - a]), reads=[sb], writes=[wbb])
        return wb, wbb


class PsumBanks:
    def __init__(s, nc, es):
        s.t = [es.enter_context(nc.psum_tensor(f"pb{i}", [128, 512], F32)) for i in range(8)]
        s.b = bufs("pb", 8)
        s.rot = list(range(8))
        s.i = 0

    def next(s):
        i = s.rot[s.i % len(s.rot)]
        s.i += 1
        return s.t[i], s.b[i]

    def hold(s, n):
        got = []
        for _ in range(n):
            i = s.rot.pop(s.i % len(s.rot))
            got.append(i)
        return [(s.t[i], s.b[i]) for i in got], got

    def unhold(s, ids):
        s.rot.extend(ids)
        s.rot.sort()


def segs_of(T):
    NT = max(1, T // 512)
    TW = T // NT
    return [(i * TW, TW) for i in range(NT)]


class LayerBuilder:
    def __init__(s, cfg, kind, with_mod, with_kv=False):
        s.cfg, s.kind, s.with_mod, s.with_kv = cfg, kind, with_mod, with_kv
        s.es = ExitStack()
        s.nc = bass.Bass("TRN2", target_bir_lowering=False, dynamic_dma_scratch_size=DMA_SCRATCH)
        s.P = Prog(s.nc, s.es)
        s.inputs = {}

    def din(s, name, shape, dt=F32):
        t = s.nc.dram_tensor(name, list(shape), dt, kind="ExternalInput")
        s.inputs[name] = (tuple(shape), dt)
        return t

    def dout(s, name, shape, dt=F32):
        return s.nc.dram_tensor(name, list(shape), dt, kind="ExternalOutput")

    def sb(s, name, shape, dt=F32):
        return s.es.enter_context(s.nc.sbuf_tensor(name, list(shape), dt))

    def load_small(s, name, shape, dt=F32):
        d = s.din(name, shape, dt)
        t = s.sb("sb_" + name, shape, dt)
        b = Buf(name)
        idx = tuple(slice(None) for _ in shape)
        s.P.dma("sp", t[idx], d[idx], writes=[b])
        return t, b

    def setup_common(s):
        cfg, P, nc = s.cfg, s.P, s.nc
        DC, T = cfg.DC, cfg.T
        s.ps = PsumBanks(nc, s.es)
        s.ws = WStream(P, nc, s.es, max(DC * 128, 4096))
        s.bufA = s.sb("bufA", [128, DC, 32 + T], BF16)
        s.bA = bufs("bA", DC)
        s.NB = max(DC * T, 32768)
        s.bufB_raw = s.sb("bufB", [128, s.NB], BF16)
        s.bufB = s.bufB_raw[:, 0:DC * T].rearrange("p (j t) -> p j t", j=DC)
        s.bB = bufs("bB", DC)
        s.XB = s.sb("XB", [128, 4, T], F32)
        s.bX = bufs("bX", 4)
        s.xi = 0
        s.xb_n = 4
        s.ones = s.sb("ones", [128, 128], BF16)
        s.b_ones = Buf("ones")
        P.op("dve", lambda e: e.memset(s.ones[:], 1.0), writes=[s.b_ones])
        s.onesf = s.sb("onesf", [128, 1], F32)
        s.b_onesf = Buf("onesf")
        P.op("dve", lambda e: e.memset(s.onesf[:], 1.0), writes=[s.b_onesf])
        s.epsc = s.sb("epsc", [128, 1], F32)
        s.b_eps = Buf("eps")
        P.op("dve", lambda e: e.memset(s.epsc[:], EPS), writes=[s.b_eps])
        s.sqb = s.sb("sqb", [128, 2, 512], BF16)
        s.b_sq = bufs("sq", 2)
        s.sqi = 0
        s.rstd = s.sb("rstd", [128, T], F32)
        s.b_rstd = Buf("rstd")
        s.rstd2 = s.sb("rstd2", [128, T], F32)
        s.b_rstd2 = Buf("rstd2")

    def xbuf(s):
        i = s.xi % s.xb_n
        s.xi += 1
        return s.XB[:, i, :], s.bX[i]

    def sq_slot(s):
        i = s.sqi % 2
        s.sqi += 1
        return s.sqb[:, i, :], s.b_sq[i]

    def vec_setup(s, layer_has_mod_input):
        cfg, P, nc = s.cfg, s.P, s.nc
        DC, T, D = cfg.DC, cfg.T, cfg.D
        s.mod = s.sb("mod", [128, 6 * DC], F32)
        s.b_mod = Buf("mod")
        if s.with_mod:
            c_t, c_b = s.load_small("c_l", [128, DC])
            bada_t, bada_b = s.load_small("b_ada_l", [128, 6 * DC])
            wada = s.din("w_ada_l", [DC, 128, 6 * D])
            sc = s.sb("silc", [128, DC], F32)
            b_sc = Buf("silc")
            P.op("act", lambda e: e.activation(out=sc[:], in_=c_t[:], func=AF.Silu), reads=[c_b], writes=[b_sc])
            NR = min(2 * T, 6 * D, WStream.SP)
            acc = s.XB[:, 0:2, :].rearrange("p a t -> p (a t)")
            b_acc = [s.bX[0], s.bX[1]]
            (mpp,), mids = s.ps.hold(1)
            mp, mpb = mpp
            for r0 in range(0, 6 * D, NR):
                for kc in range(DC):
                    st, stb = s.ws.stage_slot()
                    P.dma("sp", st[:, 0:NR], wada[kc, :, r0:r0 + NR], writes=[stb])
                    if kc == 0:
                        P.op("dve", lambda e: e.tensor_scalar(out=acc[:, 0:NR], in0=st[:, 0:NR], scalar1=sc[:, 0:1],
                                                              scalar2=None, op0=ALU.mult),
                             reads=[stb, b_sc], writes=b_acc)
                    else:
                        P.op("dve", lambda e: e.scalar_tensor_tensor(out=acc[:, 0:NR], in0=st[:, 0:NR],
                                                                     scalar=sc[:, kc:kc + 1], in1=acc[:, 0:NR],
                                                                     op0=ALU.mult, op1=ALU.add),
                             reads=[stb, b_sc] + b_acc, writes=b_acc)

                def mm(e, r0=r0):
                    ins = None
                    for cc in range(NR // 128):
                        col = (r0 // 128) + cc
                        ins = e.matmul(mp[:, col:col + 1], lhsT=acc[:, cc * 128:(cc + 1) * 128], rhs=s.onesf[:, 0:1],
                                       start=True, stop=True)
                    return ins
                P.op("pe", mm, reads=b_acc + [s.b_onesf], writes=[mpb])
            P.op("dve", lambda e: e.tensor_tensor(out=s.mod[:], in0=mp[:, 0:6 * DC], in1=bada_t[:], op=ALU.add),
                 reads=[mpb, bada_b], writes=[s.b_mod])
            s.ps.unhold(mids)
            s.mod_out = s.dout("mod_out", [128, 6 * DC])
            P.dma("act", s.mod_out[:, :], s.mod[:], reads=[s.b_mod], is_output=True)
        else:
            d = s.din("mod_in", [128, 6 * DC])
            P.dma("sp", s.mod[:], d[:, :], writes=[s.b_mod])
        ada_t, ada_b = s.load_small("ada_l", [128, 6 * DC])
        nm_t, nm_b = s.load_small("norm_mix_l", [128, DC])
        nf_t, nf_b = s.load_small("norm_ffn_l", [128, DC])
        m = s.sb("mvec", [128, 6 * DC], F32)
        s.b_m = Buf("mvec")
        P.op("dve", lambda e: e.tensor_tensor(out=m[:], in0=s.mod[:], in1=ada_t[:], op=ALU.add),
             reads=[s.b_mod, ada_b], writes=[s.b_m])
        s.m = m
        s.Amix = s.sb("Amix", [128, DC], F32)
        s.Affn = s.sb("Affn", [128, DC], F32)
        s.b_A = Buf("Avec")
        P.op("dve", lambda e: e.scalar_tensor_tensor(out=s.Amix[:], in0=m[:, DC:2 * DC], scalar=1.0, in1=nm_t[:],
                                                     op0=ALU.add, op1=ALU.mult), reads=[s.b_m, nm_b], writes=[s.b_A])
        P.op("dve", lambda e: e.scalar_tensor_tensor(out=s.Affn[:], in0=m[:, 4 * DC:5 * DC], scalar=1.0, in1=nf_t[:],
                                                     op0=ALU.add, op1=ALU.mult), reads=[s.b_m, nf_b, s.b_A],
             writes=[s.b_A])

    def mvec(s, i, j):
        DC = s.cfg.DC
        return s.m[:, i * DC + j:i * DC + j + 1]

    def rms_stats(s, xsrc, xsrc_bufs, width, rstd_ap, rstd_buf, col0=0):
        cfg, P = s.cfg, s.P
        DC, D = cfg.DC, cfg.D
        sg = segs_of(width) if width >= 512 else [(0, width)]
        banks, bids = s.ps.hold(len(sg))
        for j in range(DC):
            xb, xbb = s.xbuf()
            P.dma("sp", xb[:, 0:width], xsrc(j), reads=[xsrc_bufs[j]], writes=[xbb])
            for si, (c0, w) in enumerate(sg):
                sq, sqb = s.sq_slot()
                P.op("act", lambda e: e.activation(out=sq[:, 0:w], in_=xb[:, c0:c0 + w], func=AF.Square),
                     reads=[xbb], writes=[sqb])
                pt, pb = banks[si]
                P.op("pe", lambda e: e.matmul(pt[:, 0:w], lhsT=s.ones[:], rhs=sq[:, 0:w], start=(j == 0),
                                              stop=(j == DC - 1)),
                     reads=[sqb, s.b_ones], writes=[pb])
        for si, (c0, w) in enumerate(sg):
            pt, pb = banks[si]
            P.op("act", lambda e: e.activation(out=rstd_ap[:, c0:c0 + w], in_=pt[:, 0:w], func=AF.Sqrt,
                                               bias=s.epsc[:, 0:1], scale=1.0 / D),
                 reads=[pb, s.b_eps], writes=[rstd_buf])
        P.op("dve", lambda e: e.reciprocal(out=rstd_ap[:, 0:width], in_=rstd_ap[:, 0:width]),
             reads=[rstd_buf], writes=[rstd_buf])
        s.ps.unhold(bids)

    def modulate(s, xsrc, xsrc_bufs, width, rstd_ap, rstd_buf, A, Bcol, dst, dst_bufs):
        cfg, P = s.cfg, s.P
        DC = cfg.DC
        for j in range(DC):
            xb, xbb = s.xbuf()
            P.dma("sp", xb[:, 0:width], xsrc(j), reads=[xsrc_bufs[j]], writes=[xbb])
            P.op("dve", lambda e: e.scalar_tensor_tensor(out=xb[:, 0:width], in0=xb[:, 0:width], scalar=A[:, j:j + 1],
                                                         in1=rstd_ap[:, 0:width], op0=ALU.mult, op1=ALU.mult),
                 reads=[xbb, rstd_buf, s.b_A], writes=[xbb])
            P.op("act", lambda e: e.activation(out=dst(j), in_=xb[:, 0:width], func=AF.Identity, bias=Bcol(j),
                                               scale=1.0),
                 reads=[xbb, s.b_m], writes=[dst_bufs[j]])


    def rope_tables(s, pos_dram, W, cos_ap, sin_ap, b_cs, invf, b_invf, t1, b_t1, t2, b_t2):
        P = s.P
        TWO_PI = 2.0 * np.pi
        ki = t2.bitcast(I32)
        si = sin_ap.bitcast(I32)
        P.dma("sp", si[:, 0:W], pos_dram, writes=[b_cs])
        P.op("dve", lambda e: e.tensor_copy(out=t1[:, 0:W], in_=si[:, 0:W]), reads=[b_cs], writes=[b_t1])
        P.op("dve", lambda e: e.tensor_scalar(out=t1[:, 0:W], in0=t1[:, 0:W], scalar1=invf[:, 0:1],
                                              scalar2=float(1.0 / TWO_PI), op0=ALU.mult, op1=ALU.mult),
             reads=[b_t1, b_invf], writes=[b_t1])
        P.op("dve", lambda e: e.tensor_copy(out=ki[:, 0:W], in_=t1[:, 0:W]), reads=[b_t1], writes=[b_t2])
        P.op("dve", lambda e: e.tensor_copy(out=sin_ap[:, 0:W], in_=ki[:, 0:W]), reads=[b_t2], writes=[b_cs])
        P.op("dve", lambda e: e.tensor_tensor(out=t1[:, 0:W], in0=t1[:, 0:W], in1=sin_ap[:, 0:W], op=ALU.subtract),
             reads=[b_t1, b_cs], writes=[b_t1])
        for (dst, shift) in ((sin_ap, 0.0), (cos_ap, 0.25)):
            P.op("dve", lambda e: e.tensor_scalar(out=dst[:, 0:W], in0=t1[:, 0:W], scalar1=float(shift), scalar2=None,
                                                  op0=ALU.add), reads=[b_t1], writes=[b_cs])
            for (cmp, thr, sign) in ((ALU.is_gt, 0.5, ALU.subtract), (ALU.is_lt, -0.5, ALU.add)):
                P.op("dve", lambda e: e.tensor_scalar(out=t2[:, 0:W], in0=dst[:, 0:W], scalar1=float(thr),
                                                      scalar2=None, op0=cmp), reads=[b_cs], writes=[b_t2])
                P.op("dve", lambda e: e.tensor_tensor(out=dst[:, 0:W], in0=dst[:, 0:W], in1=t2[:, 0:W], op=sign),
                     reads=[b_cs, b_t2], writes=[b_cs])
            P.op("dve", lambda e: e.tensor_scalar(out=dst[:, 0:W], in0=dst[:, 0:W], scalar1=0.49995,
                                                  scalar2=-0.49995, op0=ALU.min, op1=ALU.max),
                 reads=[b_cs], writes=[b_cs])
            P.op("act", lambda e: e.activation(out=dst[:, 0:W], in_=dst[:, 0:W], func=AF.Sin, scale=float(TWO_PI)),
                 reads=[b_cs], writes=[b_cs])

    def rope_consts(s, want_kout=True):
        P = s.P
        s.invf, s.b_invf = s.load_small("invf", [128, 1])
        rm_t, rm_b = s.load_small("rotm", [128, 128])
        s.rotm = s.sb("rotm_bf", [128, 128], BF16)
        s.b_rotm = Buf("rotm")
        P.op("dve", lambda e: e.tensor_copy(out=s.rotm[:], in_=rm_t[:]), reads=[rm_b], writes=[s.b_rotm])
        s.k1b = s.sb("k1b", [128, 2, 512], BF16)
        s.b_k1 = bufs("k1b", 2)
        s.k1i = 0
        if want_kout:
            s.kout = s.sb("kout", [128, 2, 512], BF16)
            s.b_kout = bufs("kout", 2)
            s.koi = 0

    def normrope(s, pk, pkb, w, gcol, b_g, cos_ap, sin_ap, b_cs, out_scale, emit_out, rk, rkb, dst=None, dstb=None):
        P = s.P
        sq, sqb = s.sq_slot()
        P.op("act", lambda e: e.activation(out=sq[:, 0:w], in_=pk[:, 0:w], func=AF.Square), reads=[pkb], writes=[sqb])
        k1, k1b = s.k1b[:, s.k1i % 2, :], s.b_k1[s.k1i % 2]
        s.k1i += 1
        P.op("act", lambda e: e.activation(out=k1[:, 0:w], in_=pk[:, 0:w], func=AF.Identity, scale=gcol),
             reads=[pkb, b_g], writes=[k1b])
        pss, pssb = s.ps.next()
        P.op("pe", lambda e: e.matmul(pss[:, 0:w], lhsT=s.ones[:], rhs=sq[:, 0:w], start=True, stop=True),
             reads=[sqb, s.b_ones], writes=[pssb])
        pr, prb = s.ps.next()
        P.op("pe", lambda e: e.matmul(pr[:, 0:w], lhsT=s.rotm[:], rhs=k1[:, 0:w], start=True, stop=True),
             reads=[k1b, s.b_rotm], writes=[prb])
        P.op("act", lambda e: e.activation(out=rk[:, 0:w], in_=pss[:, 0:w], func=AF.Sqrt, bias=s.epsc[:, 0:1],
                                           scale=1.0 / 128.0), reads=[pssb, s.b_eps], writes=[rkb])
        P.op("dve", lambda e: e.reciprocal(out=rk[:, 0:w], in_=rk[:, 0:w]), reads=[rkb], writes=[rkb])
        t1, t1b = s.xbuf()
        P.op("dve", lambda e: e.tensor_tensor(out=t1[:, 0:w], in0=k1[:, 0:w], in1=cos_ap, op=ALU.mult),
             reads=[k1b, b_cs], writes=[t1b])
        t2, t2b = s.xbuf()
        P.op("dve", lambda e: e.tensor_tensor(out=t2[:, 0:w], in0=pr[:, 0:w], in1=sin_ap, op=ALU.mult),
             reads=[prb, b_cs], writes=[t2b])
        P.op("dve", lambda e: e.tensor_tensor(out=t1[:, 0:w], in0=t1[:, 0:w], in1=t2[:, 0:w], op=ALU.add),
             reads=[t1b, t2b], writes=[t1b])
        if dst is not None:
            P.op("dve", lambda e: e.scalar_tensor_tensor(out=dst, in0=t1[:, 0:w], scalar=float(out_scale),
                                                         in1=rk[:, 0:w], op0=ALU.mult, op1=ALU.mult),
                 reads=[t1b, rkb], writes=[dstb])
            return
        ko, kob = s.kout[:, s.koi % 2, :], s.b_kout[s.koi % 2]
        s.koi += 1
        P.op("dve", lambda e: e.scalar_tensor_tensor(out=ko[:, 0:w], in0=t1[:, 0:w], scalar=float(out_scale),
                                                     in1=rk[:, 0:w], op0=ALU.mult, op1=ALU.mult),
             reads=[t1b, rkb], writes=[kob])
        emit_out(ko, kob)

    def kvproj(s, xsrc, xsrc_bufs):
        cfg, P = s.cfg, s.P
        DC, T, D = cfg.DC, cfg.T, cfg.D
        sg = segs_of(T)
        nk_t, nk_b = s.load_small("norm_kv_l", [128, DC])
        ka_t, ka_b = s.load_small("kv_ada_l", [128, 2 * DC])
        kn_t, kn_b = s.load_small("kn_l", [128, 2])
        pos_d = s.din("pos_l", [128, T], I32)
        wf = s.din("w_kvf_l", [16, 128, DC * 128])
        wt_ = s.din("w_kvt_l", [2, 128, DC * 512])
        outs_f = [s.dout(nm, [4, 128, T], BF16) for nm in ("kcr", "vcr", "ksl", "kwn")]
        outs_t = [s.dout(nm, [T, 512], BF16) for nm in ("vsl", "vwn")]
        s.rope_consts()
        Akv = s.sb("Akv", [128, DC], F32)
        Bkv = s.sb("Bkv", [128, DC], F32)
        P.op("dve", lambda e: e.tensor_tensor(out=Akv[:], in0=s.mod[:, DC:2 * DC], in1=ka_t[:, DC:2 * DC], op=ALU.add),
             reads=[s.b_mod, ka_b], writes=[s.b_A])
        P.op("dve", lambda e: e.scalar_tensor_tensor(out=Akv[:], in0=Akv[:], scalar=1.0, in1=nk_t[:], op0=ALU.add,
                                                     op1=ALU.mult), reads=[s.b_A, nk_b], writes=[s.b_A])
        P.op("dve", lambda e: e.tensor_tensor(out=Bkv[:], in0=s.mod[:, 0:DC], in1=ka_t[:, 0:DC], op=ALU.add),
             reads=[s.b_mod, ka_b, s.b_m], writes=[s.b_m])
        s.rms_stats(lambda j: xsrc[j, :, :], xsrc_bufs, T, s.rstd, s.b_rstd)
        s.modulate(lambda j: xsrc[j, :, :], xsrc_bufs, T, s.rstd, s.b_rstd, Akv, lambda j: Bkv[:, j:j + 1],
                   lambda j: s.bufB[:, j, :], s.bB)
        s.xb_n = 2
        cos, sin = s.XB[:, 2, :], s.XB[:, 3, :]
        b_cs = Buf("cossin")
        s.alias([b_cs], s.bX)
        s.rope_tables(pos_d[:, :], T, cos, sin, b_cs, s.invf, s.b_invf, s.rstd, s.b_rstd, s.rstd2,
                      s.b_rstd2)
        ob = [bufs("of%d" % i, 4) for i in range(4)]
        for bi in range(4):
            for g in range(4):
                wt, wtb = s.ws.load(wf[bi * 4 + g, :, :], DC * 128)
                for (c0, w) in sg:
                    pk, pkb = s.ps.next()

                    def mm(e):
                        ins = None
                        for kc in range(DC):
                            ins = e.matmul(pk[:, 0:w], lhsT=wt[:, kc * 128:(kc + 1) * 128],
                                           rhs=s.bufB[:, kc, c0:c0 + w], start=(kc == 0), stop=(kc == DC - 1))
                        return ins
                    P.op("pe", mm, reads=[wtb] + s.bB, writes=[pkb])

                    def emit(ko, kob, bi=bi, g=g, c0=c0, w=w):
                        P.dma("act", outs_f[bi][g, :, c0:c0 + w], ko[:, 0:w], reads=[kob], writes=[ob[bi][g]],
                              is_output=True)
                    if bi < 2:
                        ko, kob = s.kout[:, s.koi % 2, :], s.b_kout[s.koi % 2]
                        s.koi += 1
                        P.op("act", lambda e: e.activation(out=ko[:, 0:w], in_=pk[:, 0:w], func=AF.Identity),
                             reads=[pkb], writes=[kob])
                        emit(ko, kob)
                    else:
                        s.normrope(pk, pkb, w, kn_t[:, bi - 2:bi - 1], kn_b, cos[:, c0:c0 + w], sin[:, c0:c0 + w],
                                   b_cs, 1.0, emit, s.rstd, s.b_rstd)
        E = DC * 128
        kpu = max(1, E // 512)
        nun = DC // kpu
        ntt = T // 128
        grp = min(4, ntt)
        vo = s.sb("vo", [128, 2, 512], BF16)
        b_vo = bufs("vo", 2)
        voi = 0
        obt = [bufs("ot%d" % i, ntt) for i in range(2)]
        for bi in range(2):
            for t0 in range(0, ntt, grp):
                banks, ids = s.ps.hold(grp)
                for u in range(nun):
                    wt, wtb = s.ws.load(wt_[bi, :, u * kpu * 512:(u + 1) * kpu * 512], kpu * 512)
                    for ti in range(grp):
                        tt = t0 + ti
                        pv, pvb = banks[ti]

                        def mmv(e):
                            ins = None
                            for kk in range(kpu):
                                kc = u * kpu + kk
                                ins = e.matmul(pv[:, 0:512], lhsT=s.bufB[:, kc, tt * 128:(tt + 1) * 128],
                                               rhs=wt[:, kk * 512:(kk + 1) * 512], start=(kc == 0),
                                               stop=(kc == DC - 1))
                            return ins
                        P.op("pe", mmv, reads=[wtb] + s.bB, writes=[pvb])
                for ti in range(grp):
                    tt = t0 + ti
                    pv, pvb = banks[ti]
                    v_, vb_ = vo[:, voi % 2, :], b_vo[voi % 2]
                    voi += 1
                    P.op("act", lambda e: e.activation(out=v_[:, 0:512], in_=pv[:, 0:512], func=AF.Identity),
                         reads=[pvb], writes=[vb_])
                    P.dma("act", outs_t[bi][tt * 128:(tt + 1) * 128, :], v_[:, 0:512], reads=[vb_],
                          writes=[obt[bi][tt]], is_output=True)
                s.ps.unhold(ids)
        s.xb_n = 4


    def alias(s, new_bufs, old_bufs):
        best = {}
        for b in old_bufs:
            toks = list(b.r.values()) + ([b.w] if b.w is not None else [])
            for t in toks:
                k = t[2]
                if k not in best or best[k][1] < t[1]:
                    best[k] = t
        for nb in new_bufs:
            for k, t in best.items():
                if k not in nb.r or nb.r[k][1] < t[1]:
                    nb.r[k] = t

    def cbf(s, name, shape):
        return s.load_small(name, shape, BF16)

    def nsa(s, xin, xin_bufs, xs, xs_bufs):
        cfg, P, nc = s.cfg, s.P, s.nc
        DC, T, D, H, G, R, S = cfg.DC, cfg.T, cfg.D, cfg.H, cfg.G, cfg.R, cfg.S
        NQT = T // 128
        NSLC, NCMP = cfg.NSLC, cfg.NCMP
        NCT = -(-NCMP // 128)
        NCP = NCT * 128
        HH = min(4, R)
        NHF = R // HH
        CW = HH * 128
        NWT = cfg.WINDOW // 128 + 1
        KPW = min(32, NSLC)
        NGT = 3 * H
        sg = segs_of(T)
        wq = s.din("w_q_l", [H, 128, DC * 128])
        wo = s.din("w_o_l", [DC, 128, H * 128])
        wgd = s.din("w_gate_l", [128, DC * NGT])
        bg_t, bg_b = s.load_small("b_gate_l", [NGT, 1])
        qn_t, qn_b = s.load_small("qn_l", [128, 1])
        kn0_t, kn0_b = s.load_small("kn0_l", [128, 1])
        posq_d = s.din("posq_l", [128, T], I32)
        posc_d = s.din("posc_l", [128, NCP], I32)
        tq_t, tq_b = s.load_small("tq_l", [128, NQT])
        t0_t, t0_b = s.load_small("t0_l", [128, NQT])
        coff_t, coff_b = s.load_small("coff_l", [128, 8])
        wv_t, wv_b = s.load_small("wvalid_l", [128, NQT * NWT])
        ident_t, ident_b = s.load_small("ident", [128, 128])
        iotab_t, iotab_b = s.load_small("iotab", [128, NSLC])
        b0_t, b0_b = s.load_small("b0c", [128, NSLC])
        iotad_t, iotad_b = s.load_small("iotad", [128, 128])
        cio_t, cio_b = s.load_small("cmpiota", [128, NCT, 128])
        cmat_t, cmat_b = s.cbf("cmat", [128, NCT, NSLC])
        wmask_t, wmask_b = s.cbf("wmask", [128, NWT, 128])
        cb1_t, cb1_b = s.load_small("cmp_b1_l", [128, 8])
        cb2k_t, cb2k_b = s.load_small("cmp_b2k_l", [128, 1])
        cb2v_t, cb2v_b = s.load_small("cmp_b2v_l", [128, 128])
        cposT_t, cposT_b = s.cbf("cmp_pos_l", [128, 2, 32])
        cw2_t, cw2_b = s.cbf("cmp_w2_l", [128, 2, 512])
        cw1 = s.din("cmp_w1_l", [8, 128, 32 * 128])
        kcr = s.din("kcrT", [G, 128, S], BF16)
        vcr = s.din("vcrT", [G, 128, S], BF16)
        ksl = s.din("kslT", [G, 128, S], BF16)
        vsl = s.din("vslk", [S // 128, 128, G * 128], BF16)
        kwd = s.din("kwT", [NQT, G, 128, NWT * 128], BF16)
        vwd = s.din("vwk", [NQT, NWT, 128, G * 128], BF16)
        x0rd = s.din("x0r", [128, 2048], BF16)
        gates_d = nc.dram_tensor("gates_d", [NGT, T], F32)
        b_gd = Buf("gates_d")
        ot_d = nc.dram_tensor("ot_d", [H, 128, T], BF16)
        b_ot = bufs("ot", H)
        s.rope_consts(False)
        s.rms_stats(lambda j: xin[j, :, :], xin_bufs, T, s.rstd, s.b_rstd)
        s.modulate(lambda j: xin[j, :, :], xin_bufs, T, s.rstd, s.b_rstd, s.Amix, lambda j: s.mvec(0, j),
                   lambda j: s.bufB[:, j, :], s.bB)
        s.xb_n = 2
        cos, sin = s.XB[:, 2, :], s.XB[:, 3, :]
        b_cs = Buf("cossin")
        s.alias([b_cs], s.bX)
        s.rope_tables(posq_d[:, :], T, cos, sin, b_cs, s.invf, s.b_invf, s.rstd, s.b_rstd, s.rstd2,
                      s.b_rstd2)
        for h in range(H):
            wt, wtb = s.ws.load(wq[h, :, :], DC * 128)
            for (c0, w) in sg:
                pk, pkb = s.ps.next()

                def mm(e):
                    ins = None
                    for kc in range(DC):
                        ins = e.matmul(pk[:, 0:w], lhsT=wt[:, kc * 128:(kc + 1) * 128], rhs=s.bufB[:, kc, c0:c0 + w],
                                       start=(kc == 0), stop=(kc == DC - 1))
                    return ins
                P.op("pe", mm, reads=[wtb] + s.bB, writes=[pkb])
                s.normrope(pk, pkb, w, qn_t[:, 0:1], qn_b, cos[:, c0:c0 + w], sin[:, c0:c0 + w], b_cs,
                           128.0 ** -0.5, None, s.rstd, s.b_rstd, dst=s.bufA[:, h, 32 + c0:32 + c0 + w], dstb=s.bA[h])
        wgt, wgtb = s.ws.load(wgd[:, :], DC * NGT)
        for (c0, w) in sg:
            pg, pgb = s.ps.next()

            def mmg(e):
                ins = None
                for kc in range(DC):
                    ins = e.matmul(pg[0:NGT, 0:w], lhsT=wgt[:, kc * NGT:(kc + 1) * NGT], rhs=s.bufB[:, kc, c0:c0 + w],
                                   start=(kc == 0), stop=(kc == DC - 1))
                return ins
            P.op("pe", mmg, reads=[wgtb] + s.bB, writes=[pgb])
            gs, gsb = s.xbuf()
            P.op("act", lambda e: e.activation(out=gs[0:NGT, 0:w], in_=pg[0:NGT, 0:w], func=AF.Sigmoid,
                                               bias=bg_t[0:NGT, 0:1], scale=1.0), reads=[pgb, bg_b], writes=[gsb])
            P.dma("act", gates_d[:, c0:c0 + w], gs[0:NGT, 0:w], reads=[gsb], writes=[b_gd])
        off = [0]

        def view(n):
            v = s.bufB_raw[:, off[0]:off[0] + n]
            off[0] += n
            return v
        kch = [view(1024) for _ in range(2)]
        vch = [view(1024).rearrange("p (k d) -> p k d", k=8) for _ in range(2)]
        mch = [view(1024).rearrange("p (k q) -> p k q", k=8) for _ in range(2)]
        x0r = view(2048)
        kcT = view(G * NCP).rearrange("p (g n) -> p g n", g=G)
        vcm = view(G * NCP).rearrange("p (g k d) -> p g k d", g=G, k=NCT)
        hid = view(4 * NCP).rearrange("p (m n) -> p m n", m=4)
        Ec = view(NHF * NCT * CW).rearrange("p (f k c) -> p f k c", f=NHF, k=NCT)
        Et = view(3 * 512).rearrange("p (a c) -> p a c", a=3)
        obf = view(2 * 512).rearrange("p (a c) -> p a c", a=2)
        dm = view(8 * 128).rearrange("p (a c) -> p a c", a=8)
        vmask = view(NCT * 128).rearrange("p (a c) -> p a c", a=NCT)
        kwt = view(2 * NWT * 128).rearrange("p (a c) -> p a c", a=2)
        vwt = view(2 * NWT * 128).rearrange("p (a c) -> p a c", a=2)
        accs = view(2 * NHF * 512).bitcast(F32).rearrange("p (a c) -> p a c", a=NHF)
        s.nr_t = view(2 * 3 * 512).bitcast(F32).rearrange("p (a c) -> p a c", a=3)
        tk = view(2 * NCP).bitcast(F32)
        assert off[0] <= s.NB, (off[0], s.NB)
        b_kch, b_vch, b_mch = bufs("kch", 2), bufs("vch", 2), bufs("mch", 2)
        b_x0r, b_kcT, b_vcm, b_hid = Buf("x0r"), bufs("kcT", G), bufs("vcm", G), Buf("hid")
        b_Ec2 = [bufs("Ec%d_" % f, NCT) for f in range(NHF)]
        b_Ec = [b for l in b_Ec2 for b in l]
        b_Et, b_obf, b_dm, b_vmask = bufs("Et", 3), bufs("obf", 2), Buf("dm"), Buf("vmask")
        b_kw, b_vw, b_acc = bufs("kwt", 2), bufs("vwt", 2), bufs("accs", NHF)
        s.b_nr = bufs("nr_t", 3)
        b_tk = Buf("tkc")
        newb = (b_kch + b_vch + b_mch + [b_x0r, b_hid] + b_kcT + b_vcm + b_Ec + b_Et + b_obf + [b_dm, b_vmask]
                + b_kw + b_vw + b_acc + s.b_nr + [b_tk])
        s.alias(newb, s.bB)
        P.dma("sp", x0r, x0rd[:, :], writes=[b_x0r])
        P.op("pool", lambda e: e.memset(hid, 0.0), writes=[b_hid])
        P.op("pool", lambda e: e.memset(kcT, 0.0), writes=b_kcT)
        cc = s.sb("ccmp", [128, NCP], F32)
        sc_ = s.sb("scmp", [128, NCP], F32)
        b_ccs = Buf("ccs")
        s.rope_tables(posc_d[:, :], NCP, cc, sc_, b_ccs, s.invf, s.b_invf, s.rstd, s.b_rstd, s.rstd2,
                      s.b_rstd2)
        s.alias(s.bX, [b_cs])
        c1 = s.sb("c1", [128, 8], F32)
        b_c1 = Buf("c1")
        rawv = s.XB[:].rearrange("p a t -> p (a t)").bitcast(BF16)
        s.xb_n = 0
        for jj in range(2):
            src_d = kcr if jj == 0 else vcr
            for g in range(G):
                P.dma("sp", rawv[:, 0:S], src_d[g, :, :], writes=s.bX)
                for m in range(4):
                    wt, wtb = s.ws.load(cw1[jj * 4 + m, :, :], 32 * 128)
                    if g == 0:
                        (pcp,), pcid = s.ps.hold(1)
                        pc1, pc1b = pcp

                        def mmc(e):
                            ins = None
                            for l in range(32):
                                ins = e.matmul(pc1[:, 0:1], lhsT=wt[:, l * 128:(l + 1) * 128],
                                               rhs=cposT_t[:, jj, l:l + 1], start=(l == 0), stop=(l == 31))
                            return ins
                        P.op("pe", mmc, reads=[wtb, cposT_b], writes=[pc1b])
                        P.op("dve", lambda e: e.tensor_tensor(out=c1[:, jj * 4 + m:jj * 4 + m + 1], in0=pc1[:, 0:1],
                                                              in1=cb1_t[:, jj * 4 + m:jj * 4 + m + 1], op=ALU.add),
                             reads=[pc1b, cb1_b], writes=[b_c1])
                        s.ps.unhold(pcid)
                    ph, phb = s.ps.next()

                    def mmh(e):
                        ins = None
                        for l in range(32):
                            ins = e.matmul(ph[:, 0:NCMP], lhsT=wt[:, l * 128:(l + 1) * 128],
                                           rhs=rawv[:, l:l + 16 * (NCMP - 1) + 1:16], start=(l == 0), stop=(l == 31))
                        return ins
                    P.op("pe", mmh, reads=[wtb] + s.bX, writes=[phb])
                    P.op("act", lambda e: e.activation(out=hid[:, m, 0:NCMP], in_=ph[:, 0:NCMP], func=AF.Silu,
                                                       bias=c1[:, jj * 4 + m:jj * 4 + m + 1], scale=1.0),
                         reads=[phb, b_c1], writes=[b_hid])
                if jj == 0:
                    p2, p2b = s.ps.next()

                    def mm2(e):
                        ins = None
                        for m in range(4):
                            ins = e.matmul(p2[:, 0:NCMP], lhsT=cw2_t[:, 0, m * 128:(m + 1) * 128],
                                           rhs=hid[:, m, 0:NCMP], start=(m == 0), stop=(m == 3))
                        return ins
                    P.op("pe", mm2, reads=[cw2_b, b_hid], writes=[p2b])
                    P.op("act", lambda e: e.activation(out=tk[:, 0:NCMP], in_=p2[:, 0:NCMP], func=AF.Identity,
                                                       bias=cb2k_t[:, 0:1], scale=1.0), reads=[p2b, cb2k_b],
                         writes=[b_tk])
                    s.xb_n = 0
                    s.normrope_sb(tk, b_tk, NCMP, kn0_t[:, 0:1], kn0_b, cc, sc_, b_ccs, kcT[:, g, 0:NCMP], b_kcT[g])
                else:
                    for kt in range(NCT):
                        p2, p2b = s.ps.next()

                        def mm3(e):
                            ins = None
                            for m in range(4):
                                ins = e.matmul(p2[:, 0:128], lhsT=hid[:, m, kt * 128:(kt + 1) * 128],
                                               rhs=cw2_t[:, 1, m * 128:(m + 1) * 128], start=(m == 0), stop=(m == 3))
                            return ins
                        P.op("pe", mm3, reads=[cw2_b, b_hid], writes=[p2b])
                        P.op("dve", lambda e: e.tensor_tensor(out=vcm[:, g, kt, :], in0=p2[:, 0:128], in1=cb2v_t[:],
                                                              op=ALU.add), reads=[p2b, cb2v_b], writes=[b_vcm[g]])
        s.xb_n = 4
        XF = s.XB[:].rearrange("p a t -> p (a t)")
        gbt = [XF[:, k * 2 * T:k * 2 * T + 3 * CW].rearrange("p (b c) -> p b c", b=3) for k in range(2)]
        b_gb = [[s.bX[0], s.bX[1]], [s.bX[2], s.bX[3]]]
        RW = min(512, T // 2)
        rl_t = [s.rstd[:, 0:RW], s.rstd[:, RW:2 * RW], s.rstd2[:, 0:RW], s.rstd2[:, RW:2 * RW]]
        assert RW >= CW
        b_rl = bufs("rl", 4)
        s.alias(b_rl, [s.b_rstd, s.b_rstd2])
        rli = [0]

        def rlslot():
            i = rli[0] % 4
            rli[0] += 1
            return rl_t[i], b_rl[i]
        eti = [0]
        obi = [0]
        epool = [(Et[:, k, 0:CW], b_Et[k]) for k in range(3)]
        epool += [(Ec[:, f, k, :], b_Ec2[f][k]) for f in range(NHF) for k in range(NCT)]
        epi = [0]

        def eslot():
            k = epi[0] % len(epool)
            epi[0] += 1
            return epool[k]
        for d in range(8):
            P.op("dve", lambda e: e.tensor_scalar(out=dm[:, d, :], in0=iotad_t[:], scalar1=coff_t[:, d:d + 1],
                                                  scalar2=0.0, op0=ALU.add, op1=ALU.is_ge),
                 reads=[iotad_b, coff_b], writes=[b_dm])
        it = s.sb("impt", [128, 6, NSLC], F32)
        b_it = bufs("impt", 6)
        m8 = s.sb("m8", [128, 8], F32)
        b_m8 = Buf("m8")
        selb = s.sb("selT", [128, 128], BF16)
        b_selT = Buf("selT")
        selhi = s.sb("selThi", [32, 128], BF16)
        ones_bf = s.ones
        cki = [0]
        for g in range(G):
            for i in range(NQT):
                qc0 = 32 + i * 128
                P.op("dve", lambda e: e.tensor_scalar(out=vmask[:], in0=cio_t[:], scalar1=t0_t[:, i:i + 1],
                                                      scalar2=0.0, op0=ALU.add, op1=ALU.is_ge),
                     reads=[cio_b, t0_b], writes=[b_vmask])
                gbs = []
                for hf in range(NHF):
                    gk = (g * NQT * NHF + i * NHF + hf) % 2
                    gbv, gbb = gbt[gk], b_gb[gk]
                    for br in range(3):
                        r0 = br * H + g * R + hf * HH
                        P.dma("sp", gbv[:, br, :].rearrange("p (r q) -> p r q", r=HH),
                              gates_d[r0:r0 + HH, i * 128:(i + 1) * 128].partition_broadcast(128),
                              reads=[b_gd], writes=gbb)
                    gbs.append((gbv, gbb))

                def qrhs(hf):
                    h0 = g * R + hf * HH
                    return s.bufA[:, h0:h0 + HH, qc0:qc0 + 128]

                def attend(hf, ktiles, po, pob, pl, plb, keep=None):
                    n = len(ktiles)
                    for idx, (ka, kb_, va, vb_, mfn) in enumerate(ktiles):
                        pss, pssb = s.ps.next()
                        P.op("pe", lambda e: e.matmul(pss[:, 0:CW], lhsT=ka, rhs=qrhs(hf), start=True, stop=True),
                             reads=kb_ + s.bA[g * R + hf * HH:g * R + hf * HH + HH], writes=[pssb])
                        if keep is not None:
                            e1, e1b = Et[:, eti[0] % 3, 0:CW], b_Et[eti[0] % 3]
                            eti[0] += 1
                        else:
                            e1, e1b = eslot()
                        P.op("act", lambda e: e.activation(out=e1, in_=pss[:, 0:CW], func=AF.Exp), reads=[pssb],
                             writes=[e1b])
                        if keep is not None:
                            e2, e2b = keep[idx]
                        else:
                            e2, e2b = e1, e1b
                        mfn(e1, e1b, e2, e2b)
                        P.op("pe", lambda e: e.matmul(po[:, 0:CW], lhsT=va, rhs=e2, start=(idx == 0),
                                                      stop=(idx == n - 1)), reads=vb_ + [e2b], writes=[pob])
                        P.op("pe", lambda e: e.matmul(pl[:, 0:CW], lhsT=ones_bf[:], rhs=e2, start=(idx == 0),
                                                      stop=(idx == n - 1)), reads=[e2b, s.b_ones], writes=[plb])

                def bmask(mask_ap, mbufs, scal=None, sbufs=()):
                    def f(e1, e1b, e2, e2b):
                        e13 = e1.rearrange("p (r q) -> p r q", r=HH)
                        e23 = e2.rearrange("p (r q) -> p r q", r=HH)
                        mb = mask_ap.unsqueeze(1).broadcast_to([128, HH, 128])
                        if scal is None:
                            P.op("dve", lambda e: e.tensor_tensor(out=e23, in0=e13, in1=mb, op=ALU.mult),
                                 reads=[e1b] + mbufs, writes=[e2b])
                        else:
                            P.op("dve", lambda e: e.scalar_tensor_tensor(out=e23, in0=e13, scalar=scal, in1=mb,
                                                                         op0=ALU.mult, op1=ALU.mult),
                                 reads=[e1b] + mbufs + list(sbufs), writes=[e2b])
                    return f

                def finish_branch(br, hf, po, pob, pl, plb, first, last, want_rl=False):
                    gbv, gbb = gbs[hf]
                    rl, rlb = rlslot()
                    P.op("dve", lambda e: e.tensor_scalar(out=rl[:, 0:CW], in0=pl[:, 0:CW], scalar1=1e-30,
                                                          scalar2=None, op0=ALU.max), reads=[plb], writes=[rlb])
                    P.op("dve", lambda e: e.reciprocal(out=rl[:, 0:CW], in_=rl[:, 0:CW]), reads=[rlb], writes=[rlb])
                    wt_, wtb_ = rlslot()
                    P.op("dve", lambda e: e.tensor_tensor(out=wt_[:, 0:CW], in0=rl[:, 0:CW], in1=gbv[:, br, :],
                                                          op=ALU.mult), reads=[rlb] + gbb, writes=[wtb_])
                    if first:
                        P.op("dve", lambda e: e.tensor_tensor(out=accs[:, hf, 0:CW], in0=po[:, 0:CW], in1=wt_[:, 0:CW],
                                                              op=ALU.mult), reads=[pob, wtb_], writes=[b_acc[hf]])
                    else:
                        P.op("dve", lambda e: e.tensor_tensor(out=wt_[:, 0:CW], in0=po[:, 0:CW], in1=wt_[:, 0:CW],
                                                              op=ALU.mult), reads=[pob, wtb_], writes=[wtb_])
                        if not last:
                            P.op("dve", lambda e: e.tensor_tensor(out=accs[:, hf, 0:CW], in0=accs[:, hf, 0:CW],
                                                                  in1=wt_[:, 0:CW], op=ALU.add),
                                 reads=[b_acc[hf], wtb_], writes=[b_acc[hf]])
                        else:
                            ob_, obb_ = obf[:, obi[0] % 2, 0:CW], b_obf[obi[0] % 2]
                            obi[0] += 1
                            P.op("dve", lambda e: e.tensor_tensor(out=ob_, in0=accs[:, hf, 0:CW], in1=wt_[:, 0:CW],
                                                                  op=ALU.add), reads=[b_acc[hf], wtb_], writes=[obb_])
                            h0 = g * R + hf * HH
                            P.dma("act", ot_d[h0:h0 + HH, :, i * 128:(i + 1) * 128].rearrange("h p q -> p h q"),
                                  ob_.rearrange("p (r q) -> p r q", r=HH), reads=[obb_], writes=b_ot[h0:h0 + HH])
                    return rl, rlb

                (pimp_,), impid = s.ps.hold(1)
                pimp, pimpb = pimp_
                held = []
                for hf in range(NHF):
                    (a, b), ids = s.ps.hold(2)
                    held.append((a, b, ids))
                    po, pob = a
                    pl, plb = b
                    kt_list = []
                    for kt in range(NCT):
                        kt_list.append((kcT[:, g, kt * 128:(kt + 1) * 128], [b_kcT[g]], vcm[:, g, kt, :], [b_vcm[g]],
                                        bmask(vmask[:, kt, :], [b_vmask])))
                    keep = [(Ec[:, hf, kt, :], b_Ec2[hf][kt]) for kt in range(NCT)]
                    attend(hf, kt_list, po, pob, pl, plb, keep=keep)
                    rl, rlb = finish_branch(0, hf, po, pob, pl, plb, True, False)
                    for kt in range(NCT):
                        pc, pcb = Et[:, eti[0] % 3, 0:CW], b_Et[eti[0] % 3]
                        eti[0] += 1
                        P.op("dve", lambda e: e.tensor_tensor(out=pc, in0=Ec[:, hf, kt, :], in1=rl[:, 0:CW],
                                                              op=ALU.mult), reads=[b_Ec2[hf][kt], rlb], writes=[pcb])

                        def mmi(e):
                            ins = None
                            for r in range(HH):
                                first = (hf == 0 and kt == 0 and r == 0)
                                lastm = (hf == NHF - 1 and kt == NCT - 1 and r == HH - 1)
                                ins = e.matmul(pimp[:, 0:NSLC], lhsT=pc[:, r * 128:(r + 1) * 128],
                                               rhs=cmat_t[:, kt, :], start=first, stop=lastm)
                            return ins
                        P.op("pe", mmi, reads=[pcb, cmat_b], writes=[pimpb])
                    s.ps.unhold(ids)
                dist, valid, forced, impp, imp2, selc = [it[:, k, :] for k in range(6)]
                bd, bv, bf_, bi_, bi2, bs_ = b_it
                P.op("dve", lambda e: e.tensor_scalar(out=dist, in0=iotab_t[:], scalar1=-1.0,
                                                      scalar2=tq_t[:, i:i + 1], op0=ALU.mult, op1=ALU.add),
                     reads=[iotab_b, tq_b], writes=[bd])
                P.op("dve", lambda e: e.tensor_scalar(out=valid, in0=dist, scalar1=0.0, scalar2=None,
                                                      op0=ALU.is_ge), reads=[bd], writes=[bv])
                P.op("dve", lambda e: e.tensor_scalar(out=forced, in0=dist, scalar1=2.0, scalar2=None,
                                                      op0=ALU.is_lt), reads=[bd], writes=[bf_])
                P.op("dve", lambda e: e.tensor_tensor(out=forced, in0=forced, in1=valid, op=ALU.mult),
                     reads=[bf_, bv], writes=[bf_])
                P.op("dve", lambda e: e.tensor_tensor(out=forced, in0=forced, in1=b0_t[:], op=ALU.max),
                     reads=[bf_, b0_b], writes=[bf_])
                P.op("dve", lambda e: e.scalar_tensor_tensor(out=impp, in0=pimp[:, 0:NSLC], scalar=1.0, in1=valid,
                                                             op0=ALU.add, op1=ALU.mult), reads=[pimpb, bv],
                     writes=[bi_])
                P.op("dve", lambda e: e.scalar_tensor_tensor(out=impp, in0=forced, scalar=1e4, in1=impp,
                                                             op0=ALU.mult, op1=ALU.add), reads=[bf_, bi_],
                     writes=[bi_])
                P.op("dve", lambda e: e.tensor_scalar(out=impp, in0=impp, scalar1=-1.0, scalar2=None, op0=ALU.add),
                     reads=[bi_], writes=[bi_])
                cur, curb = impp, bi_
                for rnd in range(cfg.TOPK // 8):
                    P.op("dve", lambda e: e.max(out=m8[:], in_=cur), reads=[curb], writes=[b_m8])
                    if rnd < cfg.TOPK // 8 - 1:
                        P.op("dve", lambda e: e.match_replace(out=imp2, in_to_replace=m8[:], in_values=cur,
                                                              imm_value=-2.0), reads=[curb, b_m8], writes=[bi2])
                        cur, curb = imp2, bi2
                P.op("dve", lambda e: e.tensor_scalar(out=selc, in0=impp, scalar1=m8[:, 7:8], scalar2=None,
                                                      op0=ALU.is_ge), reads=[bi_, b_m8], writes=[bs_])
                P.op("dve", lambda e: e.tensor_tensor(out=selc, in0=selc, in1=valid, op=ALU.mult),
                     reads=[bs_, bv], writes=[bs_])
                s.ps.unhold(impid)
                pT, pTb = s.ps.next()
                P.op("pe", lambda e: e.transpose(pT[0:NSLC, 0:128], selc, ident_t[:]), reads=[bs_, ident_b],
                     writes=[pTb])
                P.op("act", lambda e: e.activation(out=selb[0:NSLC, :], in_=pT[0:NSLC, 0:128], func=AF.Identity),
                     reads=[pTb], writes=[b_selT])
                if NSLC > 96:
                    pT2, pT2b = s.ps.next()
                    P.op("pe", lambda e: e.transpose(pT2[0:32, 0:128], selc[:, 96:128], ident_t[:]),
                         reads=[bs_, ident_b], writes=[pT2b])
                    P.op("act", lambda e: e.activation(out=selhi[0:32, :], in_=pT2[0:32, 0:128], func=AF.Identity),
                         reads=[pT2b, b_selT], writes=[b_selT])
                nkt = 8 * i + 8
                for ch in range(nkt // 8):
                    k_ = cki[0] % 2
                    cki[0] += 1
                    P.dma("sp", kch[k_], ksl[g, :, ch * 1024:(ch + 1) * 1024], writes=[b_kch[k_]])
                    P.dma("sp", vch[k_], vsl[ch * 8:(ch + 1) * 8, :, g * 128:(g + 1) * 128].rearrange("k p d -> p k d"),
                          writes=[b_vch[k_]])
                    wbase = ((ch * 16) // KPW) * KPW
                    pb0 = wbase % 128
                    for half4 in range(2):
                        pm, pmb = s.ps.next()

                        def mmx(e):
                            ins = None
                            for kk in range(4):
                                ktl = half4 * 4 + kk
                                colo = ((ch * 16) - wbase) * 64 + ktl * 128
                                if pb0 == 96:
                                    ins = e.matmul(pm[:, kk * 128:(kk + 1) * 128],
                                                   lhsT=x0r[0:KPW, colo:colo + 128],
                                                   rhs=selhi[0:KPW, :], start=True, stop=True)
                                else:
                                    ins = e.matmul(pm[:, kk * 128:(kk + 1) * 128],
                                                   lhsT=x0r[pb0:pb0 + KPW, colo:colo + 128],
                                                   rhs=selb[wbase:wbase + KPW, :], start=True, stop=True)
                            return ins
                        P.op("pe", mmx, reads=[b_x0r, b_selT], writes=[pmb])
                        if ch == nkt // 8 - 1:
                            P.op("dve", lambda e: e.tensor_tensor(
                                out=mch[k_][:, half4 * 4:half4 * 4 + 4, :],
                                in0=pm[:, 0:512].rearrange("p (k q) -> p k q", k=4),
                                in1=dm[:, half4 * 4:half4 * 4 + 4, :], op=ALU.mult),
                                reads=[pmb, b_dm], writes=[b_mch[k_]])
                        else:
                            P.op("act", lambda e: e.activation(
                                out=mch[k_][:, half4 * 4:half4 * 4 + 4, :],
                                in_=pm[:, 0:512].rearrange("p (k q) -> p k q", k=4), func=AF.Identity),
                                reads=[pmb], writes=[b_mch[k_]])
                    if ch == 0:
                        sheld = []
                        for hf in range(NHF):
                            (a, b), ids = s.ps.hold(2)
                            sheld.append((a, b, ids))
                    for hf in range(NHF):
                        (po, pob), (pl, plb), _ = sheld[hf]
                        for ktl in range(8):
                            kt = ch * 8 + ktl
                            pss, pssb = s.ps.next()
                            P.op("pe", lambda e: e.matmul(pss[:, 0:CW], lhsT=kch[k_][:, ktl * 128:(ktl + 1) * 128],
                                                          rhs=qrhs(hf), start=True, stop=True),
                                 reads=[b_kch[k_]] + s.bA[g * R + hf * HH:g * R + hf * HH + HH], writes=[pssb])
                            e1, e1b = eslot()
                            P.op("act", lambda e: e.activation(out=e1, in_=pss[:, 0:CW], func=AF.Exp),
                                 reads=[pssb], writes=[e1b])
                            e2, e2b = e1, e1b
                            bmask(mch[k_][:, ktl, :], [b_mch[k_]])(e1, e1b, e2, e2b)
                            P.op("pe", lambda e: e.matmul(po[:, 0:CW], lhsT=vch[k_][:, ktl, :], rhs=e2,
                                                          start=(kt == 0), stop=(kt == nkt - 1)),
                                 reads=[b_vch[k_], e2b], writes=[pob])
                            P.op("pe", lambda e: e.matmul(pl[:, 0:CW], lhsT=ones_bf[:], rhs=e2, start=(kt == 0),
                                                          stop=(kt == nkt - 1)), reads=[e2b, s.b_ones], writes=[plb])
                for hf in range(NHF):
                    (po, pob), (pl, plb), ids = sheld[hf]
                    finish_branch(1, hf, po, pob, pl, plb, False, False)
                    s.ps.unhold(ids)
                wk = (g * NQT + i) % 2
                P.dma("sp", kwt[:, wk, :], kwd[i, g, :, :], writes=[b_kw[wk]])
                P.dma("sp", vwt[:, wk, :].rearrange("p (k d) -> p k d", k=NWT),
                      vwd[i, :, :, g * 128:(g + 1) * 128].rearrange("k p d -> p k d"), writes=[b_vw[wk]])
                for hf in range(NHF):
                    (a, b), ids = s.ps.hold(2)
                    po, pob = a
                    pl, plb = b
                    kt_list = []
                    for jt in range(NWT):
                        kt_list.append((kwt[:, wk, jt * 128:(jt + 1) * 128], [b_kw[wk]],
                                        vwt[:, wk, jt * 128:(jt + 1) * 128], [b_vw[wk]],
                                        bmask(wmask_t[:, jt, :], [wmask_b],
                                              scal=wv_t[:, i * NWT + jt:i * NWT + jt + 1], sbufs=[wv_b])))
                    attend(hf, kt_list, po, pob, pl, plb)
                    finish_branch(2, hf, po, pob, pl, plb, False, True)
                    s.ps.unhold(ids)
        s.alias(s.bB, newb)
        s.alias([s.b_rstd, s.b_rstd2], b_rl)
        for h in range(H):
            P.dma("sp", s.bufA[:, h, 32:32 + T], ot_d[h, :, :], reads=[b_ot[h]], writes=[s.bA[h]])
        for n in range(DC):
            wt, wtb = s.ws.load(wo[n, :, :], H * 128)
            xb, xbb = s.xbuf()
            P.dma("sp", xb[:, 0:T], xin[n, :, :], reads=[xin_bufs[n]], writes=[xbb])
            P.flush()
            for (c0, w) in sg:
                po, pob = s.ps.next()

                def mmo(e):
                    ins = None
                    for kc in range(H):
                        ins = e.matmul(po[:, 0:w], lhsT=wt[:, kc * 128:(kc + 1) * 128],
                                       rhs=s.bufA[:, kc, 32 + c0:32 + c0 + w], start=(kc == 0), stop=(kc == H - 1))
                    return ins
                P.op("pe", mmo, reads=[wtb] + s.bA, writes=[pob])
                P.op("dve", lambda e: e.scalar_tensor_tensor(out=xb[:, c0:c0 + w], in0=po[:, 0:w],
                                                             scalar=s.mvec(2, n), in1=xb[:, c0:c0 + w],
                                                             op0=ALU.mult, op1=ALU.add),
                     reads=[pob, xbb, s.b_m], writes=[xbb])
            P.defer(lambda n=n, xb=xb, xbb=xbb: P.dma("act", xs[n, :, :], xb[:, 0:T], reads=[xbb],
                                                  writes=[xs_bufs[n]]))
        P.flush()

    def normrope_sb(s, tk, b_tk, w, gcol, b_g, cc, sc_, b_ccs, dst, dstb):
        P = s.P
        sq, sqb = s.sq_slot()
        P.op("act", lambda e: e.activation(out=sq[:, 0:w], in_=tk[:, 0:w], func=AF.Square), reads=[b_tk], writes=[sqb])
        k1, k1b = s.k1b[:, s.k1i % 2, :], s.b_k1[s.k1i % 2]
        s.k1i += 1
        P.op("act", lambda e: e.activation(out=k1[:, 0:w], in_=tk[:, 0:w], func=AF.Identity, scale=gcol),
             reads=[b_tk, b_g], writes=[k1b])
        pss, pssb = s.ps.next()
        P.op("pe", lambda e: e.matmul(pss[:, 0:w], lhsT=s.ones[:], rhs=sq[:, 0:w], start=True, stop=True),
             reads=[sqb, s.b_ones], writes=[pssb])
        pr, prb = s.ps.next()
        P.op("pe", lambda e: e.matmul(pr[:, 0:w], lhsT=s.rotm[:], rhs=k1[:, 0:w], start=True, stop=True),
             reads=[k1b, s.b_rotm], writes=[prb])
        rk, t1, t2 = [s.nr_t[:, k, :] for k in range(3)]
        rkb, t1b, t2b = s.b_nr
        P.op("act", lambda e: e.activation(out=rk[:, 0:w], in_=pss[:, 0:w], func=AF.Sqrt, bias=s.epsc[:, 0:1],
                                           scale=1.0 / 128.0), reads=[pssb, s.b_eps], writes=[rkb])
        P.op("dve", lambda e: e.reciprocal(out=rk[:, 0:w], in_=rk[:, 0:w]), reads=[rkb], writes=[rkb])
        P.op("dve", lambda e: e.tensor_tensor(out=t1[:, 0:w], in0=k1[:, 0:w], in1=cc[:, 0:w], op=ALU.mult),
             reads=[k1b, b_ccs], writes=[t1b])
        P.op("dve", lambda e: e.tensor_tensor(out=t2[:, 0:w], in0=pr[:, 0:w], in1=sc_[:, 0:w], op=ALU.mult),
             reads=[prb, b_ccs], writes=[t2b])
        P.op("dve", lambda e: e.tensor_tensor(out=t1[:, 0:w], in0=t1[:, 0:w], in1=t2[:, 0:w], op=ALU.add),
             reads=[t1b, t2b], writes=[t1b])
        P.op("dve", lambda e: e.tensor_tensor(out=dst, in0=t1[:, 0:w], in1=rk[:, 0:w], op=ALU.mult),
             reads=[t1b, rkb], writes=[dstb])

    def ffn(s, xs, xs_bufs, xdst, xdst_bufs, final_is_output):
        cfg, P = s.cfg, s.P
        DC, T, FC = cfg.DC, cfg.T, cfg.FC
        wg = s.din("w_gu_l", [2 * FC, 128, DC * 128])
        wd = s.din("w_dn_l", [DC, 128, FC * 128])
        sg = segs_of(T)
        s.rms_stats(lambda j: xs[j, :, :], xs_bufs, T, s.rstd, s.b_rstd)
        s.modulate(lambda j: xs[j, :, :], xs_bufs, T, s.rstd, s.b_rstd, s.Affn, lambda j: s.mvec(3, j),
                   lambda j: s.bufB[:, j, :], s.bB)
        sgt = s.sb("sgt", [128, 2, 512], BF16)
        b_sgt = bufs("sgt", 2)
        sgi = 0
        f0 = 0
        nq = len(cfg.FQ)
        for qi, nf in enumerate(cfg.FQ):
            last_q = qi == nq - 1
            for fl in range(nf):
                f = f0 + fl
                wgt, wgb = s.ws.load(wg[f, :, :], DC * 128)
                wut, wub = s.ws.load(wg[FC + f, :, :], DC * 128)
                for (c0, w) in sg:
                    pg, pgb = s.ps.next()
                    pu, pub = s.ps.next()

                    def mmg(e, wt=wgt, pt=pg):
                        ins = None
                        for kc in range(DC):
                            ins = e.matmul(pt[:, 0:w], lhsT=wt[:, kc * 128:(kc + 1) * 128],
                                           rhs=s.bufB[:, kc, c0:c0 + w], start=(kc == 0), stop=(kc == DC - 1))
                        return ins
                    P.op("pe", mmg, reads=[wgb] + s.bB, writes=[pgb])
                    P.op("pe", lambda e: mmg(e, wut, pu), reads=[wub] + s.bB, writes=[pub])
                    st_, stb_ = sgt[:, sgi % 2, :], b_sgt[sgi % 2]
                    sgi += 1
                    P.op("act", lambda e: e.activation(out=st_[:, 0:w], in_=pg[:, 0:w], func=AF.Silu),
                         reads=[pgb], writes=[stb_])
                    P.op("dve", lambda e: e.tensor_tensor(out=s.bufA[:, fl, c0:c0 + w], in0=st_[:, 0:w],
                                                          in1=pu[:, 0:w], op=ALU.mult),
                         reads=[stb_, pub], writes=[s.bA[fl]])
            for n in range(DC):
                nsub = -(-nf * 128 // (DC * 128))
                per = -(-nf // nsub)
                subs = []
                for su in range(nsub):
                    a, b = su * per, min(nf, (su + 1) * per)
                    wt, wb_ = s.ws.load(wd[n, :, (f0 + a) * 128:(f0 + b) * 128], (b - a) * 128)
                    subs.append((a, b, wt, wb_))
                xb, xbb = s.xbuf()
                P.dma("sp", xb[:, 0:T], xs[n, :, :], reads=[xs_bufs[n]], writes=[xbb])
                P.flush()
                for (c0, w) in sg:
                    po, pob = s.ps.next()

                    def mmd(e):
                        ins = None
                        for (a, b, wt, _) in subs:
                            for fl in range(a, b):
                                ins = e.matmul(po[:, 0:w], lhsT=wt[:, (fl - a) * 128:(fl - a + 1) * 128],
                                               rhs=s.bufA[:, fl, c0:c0 + w], start=(fl == 0), stop=(fl == nf - 1))
                        return ins
                    P.op("pe", mmd, reads=[x[3] for x in subs] + s.bA[0:nf], writes=[pob])
                    P.op("dve", lambda e: e.scalar_tensor_tensor(out=xb[:, c0:c0 + w], in0=po[:, 0:w],
                                                                 scalar=s.mvec(5, n), in1=xb[:, c0:c0 + w],
                                                                 op0=ALU.mult, op1=ALU.add),
                         reads=[pob, xbb, s.b_m], writes=[xbb])
                if last_q:
                    P.defer(lambda n=n, xb=xb, xbb=xbb: P.dma("act", xdst[n, :, :], xb[:, 0:T], reads=[xbb],
                                                          writes=[xdst_bufs[n]], is_output=final_is_output))
                else:
                    P.defer(lambda n=n, xb=xb, xbb=xbb: P.dma("act", xs[n, :, :], xb[:, 0:T], reads=[xbb],
                                                          writes=[xs_bufs[n]]))
            P.flush()
            f0 += nf

    def conv(s, xin, xin_bufs, xh, xh_bufs, xs, xs_bufs):
        cfg, P = s.cfg, s.P
        DC, T, D, CW = cfg.DC, cfg.T, cfg.D, cfg.CONVW
        w1 = s.din("w_pw1_l", [2 * DC, 128, DC * 128])
        w2 = s.din("w_pw2_l", [DC, 128, DC * 128])
        b1_t, b1_b = s.load_small("b_pw1_l", [128, 2 * DC])
        wdw_t, wdw_b = s.load_small("w_dw_l", [128, DC, CW])
        bdw_t, bdw_b = s.load_small("b_dw_l", [128, DC])
        lg_t, lg_b = s.load_small("ln_g_l", [128, DC])
        lb_t, lb_b = s.load_small("ln_b_l", [128, DC])
        b2_t, b2_b = s.load_small("b_pw2_l", [128, DC])
        hm_t, hm_b = s.load_small("hmask", [128, 1])
        hTh = s.sb("hTh", [128, DC, 32], BF16)
        b_hTh = bufs("hTh", DC)
        rsh = s.sb("rsh", [128, 32], F32)
        b_rsh = Buf("rsh")
        sg = segs_of(T)
        s.rms_stats(lambda j: xin[j, :, :], xin_bufs, T, s.rstd, s.b_rstd)
        s.rms_stats(lambda j: xh[j, :, :], xh_bufs, 32, rsh, b_rsh)
        s.modulate(lambda j: xin[j, :, :], xin_bufs, T, s.rstd, s.b_rstd, s.Amix, lambda j: s.mvec(0, j),
                   lambda j: s.bufB[:, j, :], s.bB)
        s.modulate(lambda j: xh[j, :, :], xh_bufs, 32, rsh, b_rsh, s.Amix, lambda j: s.mvec(0, j),
                   lambda j: hTh[:, j, :], b_hTh)
        sig = s.sb("sig", [128, 2, 512], BF16)
        b_sig = bufs("sig", 2)
        sgi = 0
        ps1, ids1 = s.ps.hold(len(sg))
        ps2, ids2 = s.ps.hold(len(sg))
        allseg = [(-1, 0, 32)] + [(i, c0, w) for i, (c0, w) in enumerate(sg)]
        for j in range(DC):
            wa, wab = s.ws.load(w1[j, :, :], DC * 128)
            wg_, wgb = s.ws.load(w1[DC + j, :, :], DC * 128)
            for (si, c0, w) in allseg:
                pa, pab = s.ps.next()
                pg, pgb = s.ps.next()
                if si < 0:
                    rhs = lambda kc: hTh[:, kc, :]
                    rb = b_hTh
                    dcol = 0
                else:
                    rhs = lambda kc, c0=c0, w=w: s.bufB[:, kc, c0:c0 + w]
                    rb = s.bB
                    dcol = 32 + c0

                def mm(e, wt, pt):
                    ins = None
                    for kc in range(DC):
                        ins = e.matmul(pt[:, 0:w], lhsT=wt[:, kc * 128:(kc + 1) * 128], rhs=rhs(kc),
                                       start=(kc == 0), stop=(kc == DC - 1))
                    return ins
                P.op("pe", lambda e: mm(e, wa, pa), reads=[wab] + rb, writes=[pab])
                P.op("pe", lambda e: mm(e, wg_, pg), reads=[wgb] + rb, writes=[pgb])
                sg_, sgb_ = sig[:, sgi % 2, :], b_sig[sgi % 2]
                sgi += 1
                P.op("act", lambda e: e.activation(out=sg_[:, 0:w], in_=pg[:, 0:w], func=AF.Sigmoid,
                                                   bias=b1_t[:, DC + j:DC + j + 1], scale=1.0),
                     reads=[pgb, b1_b], writes=[sgb_])
                P.op("dve", lambda e: e.scalar_tensor_tensor(out=s.bufA[:, j, dcol:dcol + w], in0=pa[:, 0:w],
                                                             scalar=b1_t[:, j:j + 1], in1=sg_[:, 0:w],
                                                             op0=ALU.add, op1=ALU.mult),
                     reads=[pab, sgb_, b1_b], writes=[s.bA[j]])
                if si < 0:
                    P.op("dve", lambda e: e.tensor_scalar(out=s.bufA[:, j, 0:32], in0=s.bufA[:, j, 0:32],
                                                          scalar1=hm_t[:, 0:1], scalar2=None, op0=ALU.mult),
                         reads=[s.bA[j], hm_b], writes=[s.bA[j]])
            y, yb = s.xbuf()
            off = 32 - (CW - 1)
            for k in range(CW):
                if k == 0:
                    P.op("dve", lambda e: e.tensor_scalar(out=y, in0=s.bufA[:, j, off:off + T],
                                                          scalar1=wdw_t[:, j, 0:1], scalar2=bdw_t[:, j:j + 1],
                                                          op0=ALU.mult, op1=ALU.add),
                         reads=[s.bA[j], wdw_b, bdw_b], writes=[yb])
                else:
                    P.op("dve", lambda e: e.scalar_tensor_tensor(out=y, in0=s.bufA[:, j, off + k:off + k + T],
                                                                 scalar=wdw_t[:, j, k:k + 1], in1=y,
                                                                 op0=ALU.mult, op1=ALU.add),
                         reads=[s.bA[j], wdw_b, yb], writes=[yb])
            P.op("act", lambda e: e.activation(out=s.bufA[:, j, 32:32 + T], in_=y, func=AF.Identity), reads=[yb],
                 writes=[s.bA[j]])
            for si, (c0, w) in enumerate(sg):
                sq, sqb = s.sq_slot()
                P.op("act", lambda e: e.activation(out=sq[:, 0:w], in_=y[:, c0:c0 + w], func=AF.Square),
                     reads=[yb], writes=[sqb])
                p1, p1b = ps1[si]
                p2, p2b = ps2[si]
                P.op("pe", lambda e: e.matmul(p1[:, 0:w], lhsT=s.ones[:], rhs=s.bufA[:, j, 32 + c0:32 + c0 + w],
                                              start=(j == 0), stop=(j == DC - 1)),
                     reads=[s.bA[j], s.b_ones], writes=[p1b])
                P.op("pe", lambda e: e.matmul(p2[:, 0:w], lhsT=s.ones[:], rhs=sq[:, 0:w],
                                              start=(j == 0), stop=(j == DC - 1)),
                     reads=[sqb, s.b_ones], writes=[p2b])
        mean, mean_b = s.rstd2, s.b_rstd2
        for si, (c0, w) in enumerate(sg):
            p1, p1b = ps1[si]
            p2, p2b = ps2[si]
            P.op("act", lambda e: e.activation(out=mean[:, c0:c0 + w], in_=p1[:, 0:w], func=AF.Identity,
                                               scale=1.0 / D), reads=[p1b], writes=[mean_b])
            xb, xbb = s.xbuf()
            P.op("dve", lambda e: e.tensor_tensor(out=xb[:, 0:w], in0=mean[:, c0:c0 + w], in1=mean[:, c0:c0 + w],
                                                  op=ALU.mult), reads=[mean_b], writes=[xbb])
            P.op("dve", lambda e: e.scalar_tensor_tensor(out=xb[:, 0:w], in0=p2[:, 0:w], scalar=1.0 / D,
                                                         in1=xb[:, 0:w], op0=ALU.mult, op1=ALU.subtract),
                 reads=[p2b, xbb], writes=[xbb])
            P.op("act", lambda e: e.activation(out=s.rstd[:, c0:c0 + w], in_=xb[:, 0:w], func=AF.Sqrt,
                                               bias=s.epsc[:, 0:1], scale=1.0), reads=[xbb, s.b_eps],
                 writes=[s.b_rstd])
        P.op("dve", lambda e: e.reciprocal(out=s.rstd[:, 0:T], in_=s.rstd[:, 0:T]), reads=[s.b_rstd],
             writes=[s.b_rstd])
        s.ps.unhold(ids1 + ids2)
        for j in range(DC):
            xb, xbb = s.xbuf()
            P.op("dve", lambda e: e.tensor_tensor(out=xb[:, 0:T], in0=s.bufA[:, j, 32:32 + T], in1=mean[:, 0:T],
                                                  op=ALU.subtract), reads=[s.bA[j], mean_b], writes=[xbb])
            P.op("dve", lambda e: e.tensor_tensor(out=xb[:, 0:T], in0=xb[:, 0:T], in1=s.rstd[:, 0:T], op=ALU.mult),
                 reads=[xbb, s.b_rstd], writes=[xbb])
            P.op("act", lambda e: e.activation(out=s.bufA[:, j, 32:32 + T], in_=xb[:, 0:T], func=AF.Silu,
                                               bias=lb_t[:, j:j + 1], scale=lg_t[:, j:j + 1]),
                 reads=[xbb, lg_b, lb_b], writes=[s.bA[j]])
        b2g = s.sb("b2g", [128, DC], F32)
        b_b2g = Buf("b2g")
        P.op("dve", lambda e: e.tensor_tensor(out=b2g[:], in0=b2_t[:], in1=s.m[:, 2 * DC:3 * DC], op=ALU.mult),
             reads=[b2_b, s.b_m], writes=[b_b2g])
        for n in range(DC):
            wt, wtb = s.ws.load(w2[n, :, :], DC * 128)
            xb, xbb = s.xbuf()
            P.dma("sp", xb[:, 0:T], xin[n, :, :], reads=[xin_bufs[n]], writes=[xbb])
            P.flush()
            for (c0, w) in sg:
                po, pob = s.ps.next()

                def mm2(e):
                    ins = None
                    for kc in range(DC):
                        ins = e.matmul(po[:, 0:w], lhsT=wt[:, kc * 128:(kc + 1) * 128],
                                       rhs=s.bufA[:, kc, 32 + c0:32 + c0 + w], start=(kc == 0), stop=(kc == DC - 1))
                    return ins
                P.op("pe", mm2, reads=[wtb] + s.bA, writes=[pob])
                P.op("dve", lambda e: e.scalar_tensor_tensor(out=xb[:, c0:c0 + w], in0=po[:, 0:w],
                                                             scalar=s.mvec(2, n), in1=xb[:, c0:c0 + w],
                                                             op0=ALU.mult, op1=ALU.add),
                     reads=[pob, xbb, s.b_m], writes=[xbb])
            P.op("pool", lambda e: e.tensor_scalar(out=xb[:, 0:T], in0=xb[:, 0:T], scalar1=b2g[:, n:n + 1],
                                                   scalar2=None, op0=ALU.add), reads=[xbb, b_b2g], writes=[xbb])
            P.defer(lambda n=n, xb=xb, xbb=xbb: P.dma("act", xs[n, :, :], xb[:, 0:T], reads=[xbb],
                                                  writes=[xs_bufs[n]]))
        P.flush()


def build_conv_launch(cfg, with_mod, with_kv=False):
    L = LayerBuilder(cfg, "conv", with_mod, with_kv)
    nc, DC, T = L.nc, cfg.DC, cfg.T
    L.setup_common()
    xin = L.din("xT", [DC, 128, T])
    xh = L.din("xh", [DC, 128, 32])
    xs = nc.dram_tensor("xs", [DC, 128, T], F32)
    y = L.dout("yT", [DC, 128, T])
    b_xin, b_xh, b_xs, b_y = bufs("xin", DC), bufs("xh", DC), bufs("xs", DC), bufs("y", DC)
    L.vec_setup(False)
    L.conv(xin, b_xin, xh, b_xh, xs, b_xs)
    L.ffn(xs, b_xs, y, b_y, True)
    if with_kv:
        L.kvproj(y, b_y)
    L.P.finish()
    L.es.close()
    return L


def vecP(v):
    v = np.asarray(v, np.float32)
    return np.ascontiguousarray(v.reshape(-1, 128).T)


def wunits(W):
    K, N = W.shape
    a = W.reshape(K // 128, 128, N // 128, 128).transpose(2, 1, 0, 3)
    return np.ascontiguousarray(a).reshape(N // 128, 128, (K // 128) * 128)


def xT_layout(x2d):
    T, D = x2d.shape
    return np.ascontiguousarray(x2d.T).reshape(D // 128, 128, T)


def xT_unlayout(a):
    DC, _, T = a.shape
    return np.ascontiguousarray(a.reshape(DC * 128, T).T)


def conv_launch_inputs(cfg, layer, x_full, mod_in, P):
    DC, T, D, NCr = cfg.DC, cfg.T, cfg.D, cfg.NCORES
    shared = {}
    if mod_in is None:
        shared["c_l"] = vecP(P["c"][0])
        shared["b_ada_l"] = vecP(P["b_ada"])
        shared["w_ada_l"] = np.ascontiguousarray(P["w_ada"]).reshape(DC, 128, 6 * D)
    else:
        shared["mod_in"] = mod_in
    shared["ada_l"] = vecP(P["ada_emb"][layer].reshape(-1))
    shared["norm_mix_l"] = vecP(P["norm_mix"][layer])
    shared["norm_ffn_l"] = vecP(P["norm_ffn"][layer])
    shared["w_pw1_l"] = wunits(P["conv_w_pw1"][layer])
    shared["w_pw2_l"] = wunits(P["conv_w_pw2"][layer])
    shared["b_pw1_l"] = vecP(P["conv_b_pw1"][layer])
    shared["w_dw_l"] = np.ascontiguousarray(P["conv_w_dw"][layer].T.reshape(DC, 128, cfg.CONVW).transpose(1, 0, 2))
    shared["b_dw_l"] = vecP(P["conv_b_dw"][layer])
    shared["ln_g_l"] = vecP(P["conv_ln_g"][layer])
    shared["ln_b_l"] = vecP(P["conv_ln_b"][layer])
    shared["b_pw2_l"] = vecP(P["conv_b_pw2"][layer])
    shared["w_gu_l"] = wunits(P["ffn_w_gu"][layer])
    wd = P["ffn_w_down"][layer]
    shared["w_dn_l"] = wunits(wd)
    maps = []
    for c in range(NCr):
        m = dict(shared)
        xs_ = x_full[c * T:(c + 1) * T]
        m["xT"] = xT_layout(xs_)
        if c == 0:
            m["xh"] = np.zeros((DC, 128, 32), np.float32)
            m["hmask"] = np.zeros((128, 1), np.float32)
        else:
            m["xh"] = xT_layout(x_full[c * T - 32:c * T])
            m["hmask"] = np.ones((128, 1), np.float32)
        maps.append(m)
    return maps


def rope_const_inputs():
    half = 64
    inv = (10000.0 ** (-np.arange(half, dtype=np.float32) / half)).astype(np.float32)
    invf = np.concatenate([inv, inv]).reshape(128, 1).astype(np.float32)
    rotm = np.zeros((128, 128), np.float32)
    for m in range(64):
        rotm[m + 64, m] = -1.0
        rotm[m, m + 64] = 1.0
    return invf, rotm


def kv_launch_extra_inputs(cfg, maps, P):
    DC, T, D = cfg.DC, cfg.T, cfg.D
    invf, rotm = rope_const_inputs()
    kvw = P["kv_w"]
    uf = wunits(kvw)
    sel = [br * 4 + g for br in (0, 1, 2, 4) for g in range(4)]
    w_kvf = np.ascontiguousarray(uf[sel])
    wt = []
    for br in (3, 5):
        w = kvw[:, br * 512:(br + 1) * 512]
        wt.append(np.ascontiguousarray(w.reshape(DC, 128, 512).transpose(1, 0, 2)).reshape(128, DC * 512))
    w_kvt = np.stack(wt)
    pos = np.asarray(P["positions"][0], np.int32)
    for c, m in enumerate(maps):
        m["norm_kv_l"] = vecP(P["norm_kv"])
        m["kv_ada_l"] = vecP(P["kv_ada_emb"].reshape(-1))
        m["kn_l"] = np.ascontiguousarray(np.stack([P["kv_k_norm"][1], P["kv_k_norm"][2]], 1).astype(np.float32))
        m["pos_l"] = np.ascontiguousarray(np.broadcast_to(pos[c * T:(c + 1) * T][None, :], (128, T)))
        m["w_kvf_l"] = w_kvf
        m["w_kvt_l"] = w_kvt
        m["invf"] = invf
        m["rotm"] = rotm
    return maps


def build_nsa_launch(cfg):
    L = LayerBuilder(cfg, "nsa", False)
    nc, DC, T = L.nc, cfg.DC, cfg.T
    L.setup_common()
    xin = L.din("xT", [DC, 128, T])
    xs = nc.dram_tensor("xs", [DC, 128, T], F32)
    y = L.dout("yT", [DC, 128, T])
    b_xin, b_xs, b_y = bufs("xin", DC), bufs("xs", DC), bufs("y", DC)
    L.vec_setup(False)
    L.nsa(xin, b_xin, xs, b_xs)
    L.ffn(xs, b_xs, y, b_y, True)
    L.P.finish()
    L.es.close()
    return L


def bf(a):
    return np.ascontiguousarray(a).astype(ml_dtypes.bfloat16)


def nsa_tile_order(cfg):
    NQT = cfg.T // 128
    return [[8 * i + c for i in range(NQT)] for c in range(cfg.NCORES)]


def nsa_shard_x(cfg, x_full):
    order = nsa_tile_order(cfg)
    out = []
    for c in range(cfg.NCORES):
        rows = np.concatenate([x_full[t * 128:(t + 1) * 128] for t in order[c]], 0)
        out.append(rows)
    return out


def nsa_unshard_x(cfg, parts):
    order = nsa_tile_order(cfg)
    S, D = cfg.S, cfg.D
    x = np.empty((S, D), np.float32)
    for c in range(cfg.NCORES):
        for li, t in enumerate(order[c]):
            x[t * 128:(t + 1) * 128] = parts[c][li * 128:(li + 1) * 128]
    return x


def nsa_launch_inputs(cfg, li, x_full, mod_in, P, kv):
    DC, T, D, NCr, S, G, H = cfg.DC, cfg.T, cfg.D, cfg.NCORES, cfg.S, cfg.G, cfg.H
    layer = 2 + li
    NQT = T // 128
    NSLC, NCMP = cfg.NSLC, cfg.NCMP
    NCT = -(-NCMP // 128)
    NCP = NCT * 128
    NWT = cfg.WINDOW // 128 + 1
    invf, rotm = rope_const_inputs()
    sh = {"mod_in": mod_in, "invf": invf, "rotm": rotm}
    sh["ada_l"] = vecP(P["ada_emb"][layer].reshape(-1))
    sh["norm_mix_l"] = vecP(P["norm_mix"][layer])
    sh["norm_ffn_l"] = vecP(P["norm_ffn"][layer])
    sh["w_gu_l"] = wunits(P["ffn_w_gu"][layer])
    sh["w_dn_l"] = wunits(P["ffn_w_down"][layer])
    sh["w_q_l"] = wunits(P["nsa_w_q"][li])
    sh["w_o_l"] = wunits(P["nsa_w_o"][li])
    wg = P["nsa_w_gate"][li]
    NGT = 3 * H
    sh["w_gate_l"] = np.ascontiguousarray(wg.reshape(DC, 128, NGT).transpose(1, 0, 2)).reshape(128, DC * NGT)
    sh["b_gate_l"] = np.ascontiguousarray(P["nsa_b_gate"][li].reshape(NGT, 1).astype(np.float32))
    sh["qn_l"] = np.ascontiguousarray(P["nsa_q_norm"][li].reshape(128, 1).astype(np.float32))
    sh["kn0_l"] = np.ascontiguousarray(P["kv_k_norm"][0].reshape(128, 1).astype(np.float32))
    pos = np.asarray(P["positions"][0], np.int32)
    posc = np.zeros((NCP,), np.int32)
    posc[:NCMP] = pos[np.arange(NCMP) * 16 + 31]
    sh["posc_l"] = np.ascontiguousarray(np.broadcast_to(posc[None, :], (128, NCP)))
    sh["ident"] = np.eye(128, dtype=np.float32)
    sh["iotab"] = np.ascontiguousarray(np.broadcast_to(np.arange(NSLC, dtype=np.float32)[None, :], (128, NSLC)))
    sh["b0c"] = np.ascontiguousarray((sh["iotab"] == 0).astype(np.float32))
    jj, qq = np.meshgrid(np.arange(128), np.arange(128), indexing="ij")
    sh["iotad"] = (qq - jj).astype(np.float32)
    cio = np.zeros((128, NCT, 128), np.float32)
    for kt in range(NCT):
        cio[:, kt, :] = qq - 16 * (128 * kt + jj) - 31
    sh["cmpiota"] = cio
    r, cl = 4, 2
    offs = (np.arange(r)[:, None] - np.arange(cl)[None, :]).reshape(-1)
    tgt = r * np.arange(NSLC)[None, :, None] + offs[None, None, :]
    C = (np.arange(NCP)[:, None, None] == tgt).sum(-1).astype(np.float32)
    C[NCMP:] = 0
    sh["cmat"] = bf(C.reshape(NCT, 128, NSLC).transpose(1, 0, 2))
    wm = np.zeros((128, NWT, 128), np.float32)
    for jt in range(NWT):
        dist = (cfg.WINDOW + qq) - (128 * jt + jj)
        wm[:, jt, :] = ((dist >= 0) & (dist < cfg.WINDOW)).astype(np.float32)
    sh["wmask"] = bf(wm)
    x0 = np.zeros((128, 2048), np.float32)
    mcol = np.arange(2048) // 64
    for p in range(128):
        x0[p] = (mcol == (p % 32))
    sh["x0r"] = bf(x0)
    sh["cmp_b1_l"] = np.ascontiguousarray(np.concatenate([vecP(P["cmp_b1"][0]), vecP(P["cmp_b1"][1])], 1))
    sh["cmp_b2k_l"] = np.ascontiguousarray(P["cmp_b2"][0].reshape(128, 1).astype(np.float32))
    sh["cmp_b2v_l"] = np.ascontiguousarray(np.broadcast_to(P["cmp_b2"][1][None, :], (128, 128)).astype(np.float32))
    sh["cmp_pos_l"] = bf(np.stack([P["cmp_pos"][0].T, P["cmp_pos"][1].T], 1))
    w2 = np.stack([P["cmp_w2"][j].reshape(4, 128, 128).transpose(1, 0, 2).reshape(128, 512) for j in range(2)], 1)
    sh["cmp_w2_l"] = bf(w2)
    sh["cmp_w1_l"] = np.concatenate([wunits(P["cmp_w1"][0]), wunits(P["cmp_w1"][1])], 0)
    sh["kcrT"], sh["vcrT"], sh["kslT"] = kv["kcrT"], kv["vcrT"], kv["kslT"]
    sh["vslk"] = kv["vsl"].reshape(S // 128, 128, G * 128)
    order = nsa_tile_order(cfg)
    xparts = nsa_shard_x(cfg, x_full)
    kwT_full, vw_full = kv["kwnT"], kv["vwn"]
    maps = []
    for c in range(NCr):
        m = dict(sh)
        m["xT"] = xT_layout(xparts[c])
        tl = order[c]
        posq = np.concatenate([pos[t * 128:(t + 1) * 128] for t in tl])
        m["posq_l"] = np.ascontiguousarray(np.broadcast_to(posq[None, :], (128, T)))
        t0 = np.array([t * 128 for t in tl], np.float32)
        m["t0_l"] = np.ascontiguousarray(np.broadcast_to(t0[None, :], (128, NQT)))
        tq = (t0[None, :] / 64.0 + (np.arange(128)[:, None] >= 64)).astype(np.float32)
        m["tq_l"] = np.ascontiguousarray(tq)
        m["coff_l"] = np.ascontiguousarray(np.broadcast_to((128.0 * (c - np.arange(8)))[None, :], (128, 8)).astype(np.float32))
        wv = np.zeros((128, NQT * NWT), np.float32)
        kw = np.zeros((NQT, G, 128, NWT * 128), ml_dtypes.bfloat16)
        vw = np.zeros((NQT, NWT, 128, G * 128), ml_dtypes.bfloat16)
        for li_, t in enumerate(tl):
            k0 = t * 128 - cfg.WINDOW
            for jt in range(NWT):
                kp = k0 + jt * 128
                if kp >= 0:
                    wv[:, li_ * NWT + jt] = 1.0
                    kw[li_, :, :, jt * 128:(jt + 1) * 128] = kwT_full[:, :, kp:kp + 128]
                    vw[li_, jt] = vw_full[kp:kp + 128]
        m["wvalid_l"] = wv
        m["kwT"] = kw
        m["vwk"] = vw
        maps.append(m)
    return maps


_PROG_CACHE = {}


def _prog(key, fn):
    return fn()


def _run(L, maps, ncores):
    for k, (shp, dt) in L.inputs.items():
        assert k in maps[0], k
        assert tuple(maps[0][k].shape) == tuple(shp), (k, maps[0][k].shape, shp)
    maps = [{k: m[k] for k in L.inputs} for m in maps]
    return run_bass_kernel_spmd(L.nc, maps, core_ids=list(range(ncores))).results


_CFG = None


def kernel(**inputs):
    cfg = _CFG or Cfg()
    P = {k: np.asarray(v) for k, v in inputs.items()}
    NCr, S, T, G = cfg.NCORES, cfg.S, cfg.T, cfg.G
    x0 = np.ascontiguousarray(P["x"][0], dtype=np.float32)
    LA = build_conv_launch(cfg, True, False)
    res = _run(LA, conv_launch_inputs(cfg, 0, x0, None, P), NCr)
    x1 = np.concatenate([xT_unlayout(r["yT"]) for r in res], 0)
    mod_l = np.ascontiguousarray(res[0]["mod_out"])
    del res, LA
    LB = build_conv_launch(cfg, False, True)
    maps = kv_launch_extra_inputs(cfg, conv_launch_inputs(cfg, 1, x1, mod_l, P), P)
    res = _run(LB, maps, NCr)
    x2 = np.concatenate([xT_unlayout(r["yT"]) for r in res], 0)
    kv = {
        "kcrT": np.ascontiguousarray(np.concatenate([np.asarray(r["kcr"]) for r in res], 2)),
        "vcrT": np.ascontiguousarray(np.concatenate([np.asarray(r["vcr"]) for r in res], 2)),
        "kslT": np.ascontiguousarray(np.concatenate([np.asarray(r["ksl"]) for r in res], 2)),
        "kwnT": np.ascontiguousarray(np.concatenate([np.asarray(r["kwn"]) for r in res], 2)),
        "vsl": np.ascontiguousarray(np.concatenate([np.asarray(r["vsl"]) for r in res], 0)),
        "vwn": np.ascontiguousarray(np.concatenate([np.asarray(r["vwn"]) for r in res], 0)),
    }
    del res, LB, maps
    x = x2
    for li in range(2):
        LC = build_nsa_launch(cfg)
        res = _run(LC, nsa_launch_inputs(cfg, li, x, mod_l, P, kv), NCr)
        x = nsa_unshard_x(cfg, [xT_unlayout(r["yT"]) for r in res])
        del res, LC
    return x[None].astype(np.float32)
```

```python
import numpy as np
import ml_dtypes
from contextlib import ExitStack
import concourse.bass as bass
import concourse.mybir as mybir
from concourse.bass_utils import run_bass_kernel_spmd

F32 = mybir.dt.float32
BF16 = mybir.dt.bfloat16
I32 = mybir.dt.int32
AF = mybir.ActivationFunctionType
ALU = mybir.AluOpType
EPS = 1e-6
NQ = 6
CAST_ENG = ("pool", "dve")
DMA_SCRATCH = 1024
NSTAGE = 3


class Cfg:
    def __init__(s, D=4096, S=8192, NCORES=8, CONVW=31, WINDOW=512, TOPK=16):
        s.D, s.S, s.NCORES = D, S, NCORES
        s.DC = D // 128
        s.T = S // NCORES
        s.F = -(-8 * D // (3 * 256)) * 256
        s.FC = s.F // 128
        s.CONVW = CONVW
        s.H = D // 128
        s.G = 4
        s.R = s.H // s.G
        s.WINDOW = WINDOW
        s.TOPK = TOPK
        s.NT = max(1, s.T // 512)
        s.TW = s.T // s.NT
        s.NSLC = S // 64
        s.NCMP = (S - 32) // 16 + 1
        nq = max(4 if s.FC >= 40 else 2, -(-s.FC // s.DC))
        base, rem = divmod(s.FC, nq)
        s.FQ = [base + (1 if i < rem else 0) for i in range(nq)]


class Buf:
    __slots__ = ("name", "w", "r")

    def __init__(s, name):
        s.name, s.w, s.r = name, None, {}


def bufs(name, n):
    return [Buf(f"{name}{i}") for i in range(n)]


class Prog:
    def __init__(s, nc, es):
        s.nc = nc
        s.eng = {"pe": nc.tensor, "act": nc.scalar, "dve": nc.vector, "pool": nc.gpsimd, "sp": nc.sync}
        s.sem = {k: es.enter_context(nc.semaphore("s_" + k)) for k in s.eng}
        s.cnt = {k: 0 for k in s.eng}
        s.waited = {k: {} for k in s.eng}
        s.dsem = {q: [es.enter_context(nc.semaphore(f"d_{q}{i}")) for i in range(NQ)] for q in ("sp", "pool", "act")}
        s.dval = {q: [0] * NQ for q in s.dsem}
        s.dnext = {q: 0 for q in s.dsem}
        s.out_tokens = []
        s.deferred = []

    def defer(s, fn):
        fn()

    def flush(s):
        d, s.deferred = s.deferred, []
        for fn in d:
            fn()

    def _wait(s, e, tok):
        sem, val, key = tok
        if key == "pe" and e == "pe":
            return
        w = s.waited[e]
        if w.get(key, 0) >= val:
            return
        s.eng[e].wait_ge(sem, val)
        w[key] = val

    def _deps(s, e, reads, writes):
        best = {}
        for b in reads:
            if b.w is not None:
                k = b.w[2]
                if k not in best or best[k][1] < b.w[1]:
                    best[k] = b.w
        for b in writes:
            if b.w is not None:
                k = b.w[2]
                if k not in best or best[k][1] < b.w[1]:
                    best[k] = b.w
            for k, t in b.r.items():
                if k not in best or best[k][1] < t[1]:
                    best[k] = t
        for t in best.values():
            s._wait(e, t)

    def _mark(s, tok, reads, writes):
        k = tok[2]
        for b in reads:
            b.r[k] = tok
        for b in writes:
            b.w = tok
            b.r = {}

    def op(s, e, fn, reads=(), writes=()):
        s._deps(e, reads, writes)
        ins = fn(s.eng[e])
        s.cnt[e] += 1
        ins.then_inc(s.sem[e], 1)
        s._mark((s.sem[e], s.cnt[e], e), reads, writes)

    def dma(s, q, out, in_, reads=(), writes=(), is_output=False):
        i = s.dnext[q]
        s.dnext[q] = (i + 1) % NQ
        sem = s.dsem[q][i]
        key = ("d", q, i)
        if s.dval[q][i] > 0:
            s._wait(q, (sem, s.dval[q][i], key))
        s._deps(q, reads, writes)
        ins = s.eng[q].dma_start(out=out, in_=in_)
        s.dval[q][i] += 16
        ins.then_inc(sem, 16)
        tok = (sem, s.dval[q][i], key)
        s._mark(tok, reads, writes)
        if is_output:
            s.out_tokens.append(tok)

    def finish(s):
        for q in s.dsem:
            for i in range(NQ):
                if s.dval[q][i] > 0:
                    s._wait("sp", (s.dsem[q][i], s.dval[q][i], ("d", q, i)))
        for e in ("pe", "act", "dve", "pool"):
            if s.cnt[e] > 0:
                s._wait("sp", (s.sem[e], s.cnt[e], e))


class WStream:
    SP = 2048

    def __init__(s, P, nc, es, E, nstage=NSTAGE, nbf=2):
        s.P, s.E = P, E
        s.stage = [es.enter_context(nc.sbuf_tensor(f"wst{i}", [128, s.SP], F32)) for i in range(nstage)]
        s.sbuf = bufs("wst", nstage)
        s.wb = [es.enter_context(nc.sbuf_tensor(f"wbf{i}", [128, E], BF16)) for i in range(nbf)]
        s.wbuf = bufs("wbf", nbf)
        s.i = 0
        s.j = 0
        s.c = 0

    def stage_slot(s):
        si = s.i % len(s.stage)
        s.i += 1
        return s.stage[si], s.sbuf[si]

    def load(s, src_ap, n):
        P = s.P
        bi = s.j % len(s.wb)
        s.j += 1
        wb, wbb = s.wb[bi], s.wbuf[bi]
        for a in range(0, n, s.SP):
            b = min(n, a + s.SP)
            st, sb = s.stage_slot()
            P.dma("sp", st[:, 0:b - a], src_ap[:, a:b], writes=[sb])
            ce = CAST_ENG[s.c % len(CAST_ENG)]
            s.c += 1
            if ce == "act":
                P.op(ce, lambda e: e.activation(out=wb[:, a:b], in_=st[:, 0:b - a], func=AF.Identity), reads=[sb],
                     writes=[wbb])
            else:
                P.op(ce, lambda e: e.tensor_copy(out=wb[:, a:b], in_=st[:, 0:b - a]), reads=[sb], writes=[wbb])
        return wb, wbb


class PsumBanks:
    def __init__(s, nc, es):
        s.t = [es.enter_context(nc.psum_tensor(f"pb{i}", [128, 512], F32)) for i in range(8)]
        s.b = bufs("pb", 8)
        s.rot = list(range(8))
        s.i = 0

    def next(s):
        i = s.rot[s.i % len(s.rot)]
        s.i += 1
        return s.t[i], s.b[i]

    def hold(s, n):
        got = []
        for _ in range(n):
            i = s.rot.pop(s.i % len(s.rot))
            got.append(i)
        return [(s.t[i], s.b[i]) for i in got], got

    def unhold(s, ids):
        s.rot.extend(ids)
        s.rot.sort()


def segs_of(T):
    NT = max(1, T // 512)
    TW = T // NT
    return [(i * TW, TW) for i in range(NT)]


class LayerBuilder:
    def __init__(s, cfg, kind, with_mod, with_kv=False):
        s.cfg, s.kind, s.with_mod, s.with_kv = cfg, kind, with_mod, with_kv
        s.es = ExitStack()
        s.nc = bass.Bass("TRN2", target_bir_lowering=False, dynamic_dma_scratch_size=DMA_SCRATCH)
        s.P = Prog(s.nc, s.es)
        s.inputs = {}

    def din(s, name, shape, dt=F32):
        t = s.nc.dram_tensor(name, list(shape), dt, kind="ExternalInput")
        s.inputs[name] = (tuple(shape), dt)
        return t

    def dout(s, name, shape, dt=F32):
        return s.nc.dram_tensor(name, list(shape), dt, kind="ExternalOutput")

    def sb(s, name, shape, dt=F32):
        return s.es.enter_context(s.nc.sbuf_tensor(name, list(shape), dt))

    def load_small(s, name, shape, dt=F32):
        d = s.din(name, shape, dt)
        t = s.sb("sb_" + name, shape, dt)
        b = Buf(name)
        idx = tuple(slice(None) for _ in shape)
        s.P.dma("sp", t[idx], d[idx], writes=[b])
        return t, b

    def setup_common(s):
        cfg, P, nc = s.cfg, s.P, s.nc
        DC, T = cfg.DC, cfg.T
        s.ps = PsumBanks(nc, s.es)
        s.ws = WStream(P, nc, s.es, max(DC * 128, 4096))
        s.bufA = s.sb("bufA", [128, DC, 32 + T], BF16)
        s.bA = bufs("bA", DC)
        s.NB = max(DC * T, 32768)
        s.bufB_raw = s.sb("bufB", [128, s.NB], BF16)
        s.bufB = s.bufB_raw[:, 0:DC * T].rearrange("p (j t) -> p j t", j=DC)
        s.bB = bufs("bB", DC)
        s.XB = s.sb("XB", [128, 4, T], F32)
        s.bX = bufs("bX", 4)
        s.xi = 0
        s.xb_n = 4
        s.ones = s.sb("ones", [128, 128], BF16)
        s.b_ones = Buf("ones")
        P.op("dve", lambda e: e.memset(s.ones[:], 1.0), writes=[s.b_ones])
        s.onesf = s.sb("onesf", [128, 1], F32)
        s.b_onesf = Buf("onesf")
        P.op("dve", lambda e: e.memset(s.onesf[:], 1.0), writes=[s.b_onesf])
        s.epsc = s.sb("epsc", [128, 1], F32)
        s.b_eps = Buf("eps")
        P.op("dve", lambda e: e.memset(s.epsc[:], EPS), writes=[s.b_eps])
        s.sqb = s.sb("sqb", [128, 2, 512], BF16)
        s.b_sq = bufs("sq", 2)
        s.sqi = 0
        s.rstd = s.sb("rstd", [128, T], F32)
        s.b_rstd = Buf("rstd")
        s.rstd2 = s.sb("rstd2", [128, T], F32)
        s.b_rstd2 = Buf("rstd2")

    def xbuf(s):
        i = s.xi % s.xb_n
        s.xi += 1
        return s.XB[:, i, :], s.bX[i]

    def sq_slot(s):
        i = s.sqi % 2
        s.sqi += 1
        return s.sqb[:, i, :], s.b_sq[i]

    def vec_setup(s, layer_has_mod_input):
        cfg, P, nc = s.cfg, s.P, s.nc
        DC, T, D = cfg.DC, cfg.T, cfg.D
        s.mod = s.sb("mod", [128, 6 * DC], F32)
        s.b_mod = Buf("mod")
        if s.with_mod:
            c_t, c_b = s.load_small("c_l", [128, DC])
            bada_t, bada_b = s.load_small("b_ada_l", [128, 6 * DC])
            wada = s.din("w_ada_l", [DC, 128, 6 * D])
            sc = s.sb("silc", [128, DC], F32)
            b_sc = Buf("silc")
            P.op("act", lambda e: e.activation(out=sc[:], in_=c_t[:], func=AF.Silu), reads=[c_b], writes=[b_sc])
            NR = min(2 * T, 6 * D, WStream.SP)
            acc = s.XB[:, 0:2, :].rearrange("p a t -> p (a t)")
            b_acc = [s.bX[0], s.bX[1]]
            (mpp,), mids = s.ps.hold(1)
            mp, mpb = mpp
            for r0 in range(0, 6 * D, NR):
                for kc in range(DC):
                    st, stb = s.ws.stage_slot()
                    P.dma("sp", st[:, 0:NR], wada[kc, :, r0:r0 + NR], writes=[stb])
                    if kc == 0:
                        P.op("dve", lambda e: e.tensor_scalar(out=acc[:, 0:NR], in0=st[:, 0:NR], scalar1=sc[:, 0:1],
                                                              scalar2=None, op0=ALU.mult),
                             reads=[stb, b_sc], writes=b_acc)
                    else:
                        P.op("dve", lambda e: e.scalar_tensor_tensor(out=acc[:, 0:NR], in0=st[:, 0:NR],
                                                                     scalar=sc[:, kc:kc + 1], in1=acc[:, 0:NR],
                                                                     op0=ALU.mult, op1=ALU.add),
                             reads=[stb, b_sc] + b_acc, writes=b_acc)

                def mm(e, r0=r0):
                    ins = None
                    for cc in range(NR // 128):
                        col = (r0 // 128) + cc
                        ins = e.matmul(mp[:, col:col + 1], lhsT=acc[:, cc * 128:(cc + 1) * 128], rhs=s.onesf[:, 0:1],
                                       start=True, stop=True)
                    return ins
                P.op("pe", mm, reads=b_acc + [s.b_onesf], writes=[mpb])
            P.op("dve", lambda e: e.tensor_tensor(out=s.mod[:], in0=mp[:, 0:6 * DC], in1=bada_t[:], op=ALU.add),
                 reads=[mpb, bada_b], writes=[s.b_mod])
            s.ps.unhold(mids)
            s.mod_out = s.dout("mod_out", [128, 6 * DC])
            P.dma("act", s.mod_out[:, :], s.mod[:], reads=[s.b_mod], is_output=True)
        else:
            d = s.din("mod_in", [128, 6 * DC])
            P.dma("sp", s.mod[:], d[:, :], writes=[s.b_mod])
        ada_t, ada_b = s.load_small("ada_l", [128, 6 * DC])
        nm_t, nm_b = s.load_small("norm_mix_l", [128, DC])
        nf_t, nf_b = s.load_small("norm_ffn_l", [128, DC])
        m = s.sb("mvec", [128, 6 * DC], F32)
        s.b_m = Buf("mvec")
        P.op("dve", lambda e: e.tensor_tensor(out=m[:], in0=s.mod[:], in1=ada_t[:], op=ALU.add),
             reads=[s.b_mod, ada_b], writes=[s.b_m])
        s.m = m
        s.Amix = s.sb("Amix", [128, DC], F32)
        s.Affn = s.sb("Affn", [128, DC], F32)
        s.b_A = Buf("Avec")
        P.op("dve", lambda e: e.scalar_tensor_tensor(out=s.Amix[:], in0=m[:, DC:2 * DC], scalar=1.0, in1=nm_t[:],
                                                     op0=ALU.add, op1=ALU.mult), reads=[s.b_m, nm_b], writes=[s.b_A])
        P.op("dve", lambda e: e.scalar_tensor_tensor(out=s.Affn[:], in0=m[:, 4 * DC:5 * DC], scalar=1.0, in1=nf_t[:],
                                                     op0=ALU.add, op1=ALU.mult), reads=[s.b_m, nf_b, s.b_A],
             writes=[s.b_A])

    def mvec(s, i, j):
        DC = s.cfg.DC
        return s.m[:, i * DC + j:i * DC + j + 1]

    def rms_stats(s, xsrc, xsrc_bufs, width, rstd_ap, rstd_buf, col0=0):
        cfg, P = s.cfg, s.P
        DC, D = cfg.DC, cfg.D
        sg = segs_of(width) if width >= 512 else [(0, width)]
        banks, bids = s.ps.hold(len(sg))
        for j in range(DC):
            xb, xbb = s.xbuf()
            P.dma("sp", xb[:, 0:width], xsrc(j), reads=[xsrc_bufs[j]], writes=[xbb])
            for si, (c0, w) in enumerate(sg):
                sq, sqb = s.sq_slot()
                P.op("act", lambda e: e.activation(out=sq[:, 0:w], in_=xb[:, c0:c0 + w], func=AF.Square),
                     reads=[xbb], writes=[sqb])
                pt, pb = banks[si]
                P.op("pe", lambda e: e.matmul(pt[:, 0:w], lhsT=s.ones[:], rhs=sq[:, 0:w], start=(j == 0),
                                              stop=(j == DC - 1)),
                     reads=[sqb, s.b_ones], writes=[pb])
        for si, (c0, w) in enumerate(sg):
            pt, pb = banks[si]
            P.op("act", lambda e: e.activation(out=rstd_ap[:, c0:c0 + w], in_=pt[:, 0:w], func=AF.Sqrt,
                                               bias=s.epsc[:, 0:1], scale=1.0 / D),
                 reads=[pb, s.b_eps], writes=[rstd_buf])
        P.op("dve", lambda e: e.reciprocal(out=rstd_ap[:, 0:width], in_=rstd_ap[:, 0:width]),
             reads=[rstd_buf], writes=[rstd_buf])
        s.ps.unhold(bids)

    def modulate(s, xsrc, xsrc_bufs, width, rstd_ap, rstd_buf, A, Bcol, dst, dst_bufs):
        cfg, P = s.cfg, s.P
        DC = cfg.DC
        for j in range(DC):
            xb, xbb = s.xbuf()
            P.dma("sp", xb[:, 0:width], xsrc(j), reads=[xsrc_bufs[j]], writes=[xbb])
            P.op("dve", lambda e: e.scalar_tensor_tensor(out=xb[:, 0:width], in0=xb[:, 0:width], scalar=A[:, j:j + 1],
                                                         in1=rstd_ap[:, 0:width], op0=ALU.mult, op1=ALU.mult),
                 reads=[xbb, rstd_buf, s.b_A], writes=[xbb])
            P.op("act", lambda e: e.activation(out=dst(j), in_=xb[:, 0:width], func=AF.Identity, bias=Bcol(j),
                                               scale=1.0),
                 reads=[xbb, s.b_m], writes=[dst_bufs[j]])


    def rope_tables(s, pos_dram, W, cos_ap, sin_ap, b_cs, invf, b_invf, t1, b_t1, t2, b_t2):
        P = s.P
        TWO_PI = 2.0 * np.pi
        ki = t2.bitcast(I32)
        si = sin_ap.bitcast(I32)
        P.dma("sp", si[:, 0:W], pos_dram, writes=[b_cs])
        P.op("dve", lambda e: e.tensor_copy(out=t1[:, 0:W], in_=si[:, 0:W]), reads=[b_cs], writes=[b_t1])
        P.op("dve", lambda e: e.tensor_scalar(out=t1[:, 0:W], in0=t1[:, 0:W], scalar1=invf[:, 0:1],
                                              scalar2=float(1.0 / TWO_PI), op0=ALU.mult, op1=ALU.mult),
             reads=[b_t1, b_invf], writes=[b_t1])
        P.op("dve", lambda e: e.tensor_copy(out=ki[:, 0:W], in_=t1[:, 0:W]), reads=[b_t1], writes=[b_t2])
        P.op("dve", lambda e: e.tensor_copy(out=sin_ap[:, 0:W], in_=ki[:, 0:W]), reads=[b_t2], writes=[b_cs])
        P.op("dve", lambda e: e.tensor_tensor(out=t1[:, 0:W], in0=t1[:, 0:W], in1=sin_ap[:, 0:W], op=ALU.subtract),
             reads=[b_t1, b_cs], writes=[b_t1])
        for (dst, shift) in ((sin_ap, 0.0), (cos_ap, 0.25)):
            P.op("dve", lambda e: e.tensor_scalar(out=dst[:, 0:W], in0=t1[:, 0:W], scalar1=float(shift), scalar2=None,
                                                  op0=ALU.add), reads=[b_t1], writes=[b_cs])
            for (cmp, thr, sign) in ((ALU.is_gt, 0.5, ALU.subtract), (ALU.is_lt, -0.5, ALU.add)):
                P.op("dve", lambda e: e.tensor_scalar(out=t2[:, 0:W], in0=dst[:, 0:W], scalar1=float(thr),
                                                      scalar2=None, op0=cmp), reads=[b_cs], writes=[b_t2])
                P.op("dve", lambda e: e.tensor_tensor(out=dst[:, 0:W], in0=dst[:, 0:W], in1=t2[:, 0:W], op=sign),
                     reads=[b_cs, b_t2], writes=[b_cs])
            P.op("dve", lambda e: e.tensor_scalar(out=dst[:, 0:W], in0=dst[:, 0:W], scalar1=0.49995,
                                                  scalar2=-0.49995, op0=ALU.min, op1=ALU.max),
                 reads=[b_cs], writes=[b_cs])
            P.op("act", lambda e: e.activation(out=dst[:, 0:W], in_=dst[:, 0:W], func=AF.Sin, scale=float(TWO_PI)),
                 reads=[b_cs], writes=[b_cs])

    def rope_consts(s, want_kout=True):
        P = s.P
        s.invf, s.b_invf = s.load_small("invf", [128, 1])
        rm_t, rm_b = s.load_small("rotm", [128, 128])
        s.rotm = s.sb("rotm_bf", [128, 128], BF16)
        s.b_rotm = Buf("rotm")
        P.op("dve", lambda e: e.tensor_copy(out=s.rotm[:], in_=rm_t[:]), reads=[rm_b], writes=[s.b_rotm])
        s.k1b = s.sb("k1b", [128, 2, 512], BF16)
        s.b_k1 = bufs("k1b", 2)
        s.k1i = 0
        if want_kout:
            s.kout = s.sb("kout", [128, 2, 512], BF16)
            s.b_kout = bufs("kout", 2)
            s.koi = 0

    def normrope(s, pk, pkb, w, gcol, b_g, cos_ap, sin_ap, b_cs, out_scale, emit_out, rk, rkb, dst=None, dstb=None):
        P = s.P
        sq, sqb = s.sq_slot()
        P.op("act", lambda e: e.activation(out=sq[:, 0:w], in_=pk[:, 0:w], func=AF.Square), reads=[pkb], writes=[sqb])
        k1, k1b = s.k1b[:, s.k1i % 2, :], s.b_k1[s.k1i % 2]
        s.k1i += 1
        P.op("act", lambda e: e.activation(out=k1[:, 0:w], in_=pk[:, 0:w], func=AF.Identity, scale=gcol),
             reads=[pkb, b_g], writes=[k1b])
        pss, pssb = s.ps.next()
        P.op("pe", lambda e: e.matmul(pss[:, 0:w], lhsT=s.ones[:], rhs=sq[:, 0:w], start=True, stop=True),
             reads=[sqb, s.b_ones], writes=[pssb])
        pr, prb = s.ps.next()
        P.op("pe", lambda e: e.matmul(pr[:, 0:w], lhsT=s.rotm[:], rhs=k1[:, 0:w], start=True, stop=True),
             reads=[k1b, s.b_rotm], writes=[prb])
        P.op("act", lambda e: e.activation(out=rk[:, 0:w], in_=pss[:, 0:w], func=AF.Sqrt, bias=s.epsc[:, 0:1],
                                           scale=1.0 / 128.0), reads=[pssb, s.b_eps], writes=[rkb])
        P.op("dve", lambda e: e.reciprocal(out=rk[:, 0:w], in_=rk[:, 0:w]), reads=[rkb], writes=[rkb])
        t1, t1b = s.xbuf()
        P.op("dve", lambda e: e.tensor_tensor(out=t1[:, 0:w], in0=k1[:, 0:w], in1=cos_ap, op=ALU.mult),
             reads=[k1b, b_cs], writes=[t1b])
        t2, t2b = s.xbuf()
        P.op("dve", lambda e: e.tensor_tensor(out=t2[:, 0:w], in0=pr[:, 0:w], in1=sin_ap, op=ALU.mult),
             reads=[prb, b_cs], writes=[t2b])
        P.op("dve", lambda e: e.tensor_tensor(out=t1[:, 0:w], in0=t1[:, 0:w], in1=t2[:, 0:w], op=ALU.add),
             reads=[t1b, t2b], writes=[t1b])
        if dst is not None:
            P.op("dve", lambda e: e.scalar_tensor_tensor(out=dst, in0=t1[:, 0:w], scalar=float(out_scale),
                                                         in1=rk[:, 0:w], op0=ALU.mult, op1=ALU.mult),
                 reads=[t1b, rkb], writes=[dstb])
            return
        ko, kob = s.kout[:, s.koi % 2, :], s.b_kout[s.koi % 2]
        s.koi += 1
        P.op("dve", lambda e: e.scalar_tensor_tensor(out=ko[:, 0:w], in0=t1[:, 0:w], scalar=float(out_scale),
                                                     in1=rk[:, 0:w], op0=ALU.mult, op1=ALU.mult),
             reads=[t1b, rkb], writes=[kob])
        emit_out(ko, kob)

    def kvproj(s, xsrc, xsrc_bufs):
        cfg, P = s.cfg, s.P
        DC, T, D = cfg.DC, cfg.T, cfg.D
        sg = segs_of(T)
        nk_t, nk_b = s.load_small("norm_kv_l", [128, DC])
        ka_t, ka_b = s.load_small("kv_ada_l", [128, 2 * DC])
        kn_t, kn_b = s.load_small("kn_l", [128, 2])
        pos_d = s.din("pos_l", [128, T], I32)
        wf = s.din("w_kvf_l", [16, 128, DC * 128])
        wt_ = s.din("w_kvt_l", [2, 128, DC * 512])
        outs_f = [s.dout(nm, [4, 128, T], BF16) for nm in ("kcr", "vcr", "ksl", "kwn")]
        outs_t = [s.dout(nm, [T, 512], BF16) for nm in ("vsl", "vwn")]
        s.rope_consts()
        Akv = s.sb("Akv", [128, DC], F32)
        Bkv = s.sb("Bkv", [128, DC], F32)
        P.op("dve", lambda e: e.tensor_tensor(out=Akv[:], in0=s.mod[:, DC:2 * DC], in1=ka_t[:, DC:2 * DC], op=ALU.add),
             reads=[s.b_mod, ka_b], writes=[s.b_A])
        P.op("dve", lambda e: e.scalar_tensor_tensor(out=Akv[:], in0=Akv[:], scalar=1.0, in1=nk_t[:], op0=ALU.add,
                                                     op1=ALU.mult), reads=[s.b_A, nk_b], writes=[s.b_A])
        P.op("dve", lambda e: e.tensor_tensor(out=Bkv[:], in0=s.mod[:, 0:DC], in1=ka_t[:, 0:DC], op=ALU.add),
             reads=[s.b_mod, ka_b, s.b_m], writes=[s.b_m])
        s.rms_stats(lambda j: xsrc[j, :, :], xsrc_bufs, T, s.rstd, s.b_rstd)
        s.modulate(lambda j: xsrc[j, :, :], xsrc_bufs, T, s.rstd, s.b_rstd, Akv, lambda j: Bkv[:, j:j + 1],
                   lambda j: s.bufB[:, j, :], s.bB)
        s.xb_n = 2
        cos, sin = s.XB[:, 2, :], s.XB[:, 3, :]
        b_cs = Buf("cossin")
        s.alias([b_cs], s.bX)
        s.rope_tables(pos_d[:, :], T, cos, sin, b_cs, s.invf, s.b_invf, s.rstd, s.b_rstd, s.rstd2,
                      s.b_rstd2)
        ob = [bufs("of%d" % i, 4) for i in range(4)]
        for bi in range(4):
            for g in range(4):
                wt, wtb = s.ws.load(wf[bi * 4 + g, :, :], DC * 128)
                for (c0, w) in sg:
                    pk, pkb = s.ps.next()

                    def mm(e):
                        ins = None
                        for kc in range(DC):
                            ins = e.matmul(pk[:, 0:w], lhsT=wt[:, kc * 128:(kc + 1) * 128],
                                           rhs=s.bufB[:, kc, c0:c0 + w], start=(kc == 0), stop=(kc == DC - 1))
                        return ins
                    P.op("pe", mm, reads=[wtb] + s.bB, writes=[pkb])

                    def emit(ko, kob, bi=bi, g=g, c0=c0, w=w):
                        P.dma("act", outs_f[bi][g, :, c0:c0 + w], ko[:, 0:w], reads=[kob], writes=[ob[bi][g]],
                              is_output=True)
                    if bi < 2:
                        ko, kob = s.kout[:, s.koi % 2, :], s.b_kout[s.koi % 2]
                        s.koi += 1
                        P.op("act", lambda e: e.activation(out=ko[:, 0:w], in_=pk[:, 0:w], func=AF.Identity),
                             reads=[pkb], writes=[kob])
                        emit(ko, kob)
                    else:
                        s.normrope(pk, pkb, w, kn_t[:, bi - 2:bi - 1], kn_b, cos[:, c0:c0 + w], sin[:, c0:c0 + w],
                                   b_cs, 1.0, emit, s.rstd, s.b_rstd)
        E = DC * 128
        kpu = max(1, E // 512)
        nun = DC // kpu
        ntt = T // 128
        grp = min(4, ntt)
        vo = s.sb("vo", [128, 2, 512], BF16)
        b_vo = bufs("vo", 2)
        voi = 0
        obt = [bufs("ot%d" % i, ntt) for i in range(2)]
        for bi in range(2):
            for t0 in range(0, ntt, grp):
                banks, ids = s.ps.hold(grp)
                for u in range(nun):
                    wt, wtb = s.ws.load(wt_[bi, :, u * kpu * 512:(u + 1) * kpu * 512], kpu * 512)
                    for ti in range(grp):
                        tt = t0 + ti
                        pv, pvb = banks[ti]

                        def mmv(e):
                            ins = None
                            for kk in range(kpu):
                                kc = u * kpu + kk
                                ins = e.matmul(pv[:, 0:512], lhsT=s.bufB[:, kc, tt * 128:(tt + 1) * 128],
                                               rhs=wt[:, kk * 512:(kk + 1) * 512], start=(kc == 0),
                                               stop=(kc == DC - 1))
                            return ins
                        P.op("pe", mmv, reads=[wtb] + s.bB, writes=[pvb])
                for ti in range(grp):
                    tt = t0 + ti
                    pv, pvb = banks[ti]
                    v_, vb_ = vo[:, voi % 2, :], b_vo[voi % 2]
                    voi += 1
                    P.op("act", lambda e: e.activation(out=v_[:, 0:512], in_=pv[:, 0:512], func=AF.Identity),
                         reads=[pvb], writes=[vb_])
                    P.dma("act", outs_t[bi][tt * 128:(tt + 1) * 128, :], v_[:, 0:512], reads=[vb_],
                          writes=[obt[bi][tt]], is_output=True)
                s.ps.unhold(ids)
        s.xb_n = 4


    def alias(s, new_bufs, old_bufs):
        best = {}
        for b in old_bufs:
            toks = list(b.r.values()) + ([b.w] if b.w is not None else [])
            for t in toks:
                k = t[2]
                if k not in best or best[k][1] < t[1]:
                    best[k] = t
        for nb in new_bufs:
            for k, t in best.items():
                if k not in nb.r or nb.r[k][1] < t[1]:
                    nb.r[k] = t

    def cbf(s, name, shape):
        return s.load_small(name, shape, BF16)

    def nsa(s, xin, xin_bufs, xs, xs_bufs):
        cfg, P, nc = s.cfg, s.P, s.nc
        DC, T, D, H, G, R, S = cfg.DC, cfg.T, cfg.D, cfg.H, cfg.G, cfg.R, cfg.S
        NQT = T // 128
        NSLC, NCMP = cfg.NSLC, cfg.NCMP
        NCT = -(-NCMP // 128)
        NCP = NCT * 128
        HH = min(4, R)
        NHF = R // HH
        CW = HH * 128
        NWT = cfg.WINDOW // 128 + 1
        KPW = min(32, NSLC)
        NGT = 3 * H
        sg = segs_of(T)
        wq = s.din("w_q_l", [H, 128, DC * 128])
        wo = s.din("w_o_l", [DC, 128, H * 128])
        wgd = s.din("w_gate_l", [128, DC * NGT])
        bg_t, bg_b = s.load_small("b_gate_l", [NGT, 1])
        qn_t, qn_b = s.load_small("qn_l", [128, 1])
        kn0_t, kn0_b = s.load_small("kn0_l", [128, 1])
        posq_d = s.din("posq_l", [128, T], I32)
        posc_d = s.din("posc_l", [128, NCP], I32)
        tq_t, tq_b = s.load_small("tq_l", [128, NQT])
        t0_t, t0_b = s.load_small("t0_l", [128, NQT])
        coff_t, coff_b = s.load_small("coff_l", [128, 8])
        wv_t, wv_b = s.load_small("wvalid_l", [128, NQT * NWT])
        ident_t, ident_b = s.load_small("ident", [128, 128])
        iotab_t, iotab_b = s.load_small("iotab", [128, NSLC])
        b0_t, b0_b = s.load_small("b0c", [128, NSLC])
        iotad_t, iotad_b = s.load_small("iotad", [128, 128])
        cio_t, cio_b = s.load_small("cmpiota", [128, NCT, 128])
        cmat_t, cmat_b = s.cbf("cmat", [128, NCT, NSLC])
        wmask_t, wmask_b = s.cbf("wmask", [128, NWT, 128])
        cb1_t, cb1_b = s.load_small("cmp_b1_l", [128, 8])
        cb2k_t, cb2k_b = s.load_small("cmp_b2k_l", [128, 1])
        cb2v_t, cb2v_b = s.load_small("cmp_b2v_l", [128, 128])
        cposT_t, cposT_b = s.cbf("cmp_pos_l", [128, 2, 32])
        cw2_t, cw2_b = s.cbf("cmp_w2_l", [128, 2, 512])
        cw1 = s.din("cmp_w1_l", [8, 128, 32 * 128])
        kcr = s.din("kcrT", [G, 128, S], BF16)
        vcr = s.din("vcrT", [G, 128, S], BF16)
        ksl = s.din("kslT", [G, 128, S], BF16)
        vsl = s.din("vslk", [S // 128, 128, G * 128], BF16)
        kwd = s.din("kwT", [NQT, G, 128, NWT * 128], BF16)
        vwd = s.din("vwk", [NQT, NWT, 128, G * 128], BF16)
        x0rd = s.din("x0r", [128, 2048], BF16)
        gates_d = nc.dram_tensor("gates_d", [NGT, T], F32)
        b_gd = Buf("gates_d")
        ot_d = nc.dram_tensor("ot_d", [H, 128, T], BF16)
        b_ot = bufs("ot", H)
        s.rope_consts(False)
        s.rms_stats(lambda j: xin[j, :, :], xin_bufs, T, s.rstd, s.b_rstd)
        s.modulate(lambda j: xin[j, :, :], xin_bufs, T, s.rstd, s.b_rstd, s.Amix, lambda j: s.mvec(0, j),
                   lambda j: s.bufB[:, j, :], s.bB)
        s.xb_n = 2
        cos, sin = s.XB[:, 2, :], s.XB[:, 3, :]
        b_cs = Buf("cossin")
        s.alias([b_cs], s.bX)
        s.rope_tables(posq_d[:, :], T, cos, sin, b_cs, s.invf, s.b_invf, s.rstd, s.b_rstd, s.rstd2,
                      s.b_rstd2)
        for h in range(H):
            wt, wtb = s.ws.load(wq[h, :, :], DC * 128)
            for (c0, w) in sg:
                pk, pkb = s.ps.next()

                def mm(e):
                    ins = None
                    for kc in range(DC):
                        ins = e.matmul(pk[:, 0:w], lhsT=wt[:, kc * 128:(kc + 1) * 128], rhs=s.bufB[:, kc, c0:c0 + w],
                                       start=(kc == 0), stop=(kc == DC - 1))
                    return ins
                P.op("pe", mm, reads=[wtb] + s.bB, writes=[pkb])
                s.normrope(pk, pkb, w, qn_t[:, 0:1], qn_b, cos[:, c0:c0 + w], sin[:, c0:c0 + w], b_cs,
                           128.0 ** -0.5, None, s.rstd, s.b_rstd, dst=s.bufA[:, h, 32 + c0:32 + c0 + w], dstb=s.bA[h])
        wgt, wgtb = s.ws.load(wgd[:, :], DC * NGT)
        for (c0, w) in sg:
            pg, pgb = s.ps.next()

            def mmg(e):
                ins = None
                for kc in range(DC):
                    ins = e.matmul(pg[0:NGT, 0:w], lhsT=wgt[:, kc * NGT:(kc + 1) * NGT], rhs=s.bufB[:, kc, c0:c0 + w],
                                   start=(kc == 0), stop=(kc == DC - 1))
                return ins
            P.op("pe", mmg, reads=[wgtb] + s.bB, writes=[pgb])
            gs, gsb = s.xbuf()
            P.op("act", lambda e: e.activation(out=gs[0:NGT, 0:w], in_=pg[0:NGT, 0:w], func=AF.Sigmoid,
                                               bias=bg_t[0:NGT, 0:1], scale=1.0), reads=[pgb, bg_b], writes=[gsb])
            P.dma("act", gates_d[:, c0:c0 + w], gs[0:NGT, 0:w], reads=[gsb], writes=[b_gd])
        off = [0]

        def view(n):
            v = s.bufB_raw[:, off[0]:off[0] + n]
            off[0] += n
            return v
        kch = [view(1024) for _ in range(2)]
        vch = [view(1024).rearrange("p (k d) -> p k d", k=8) for _ in range(2)]
        mch = [view(1024).rearrange("p (k q) -> p k q", k=8) for _ in range(2)]
        x0r = view(2048)
        kcT = view(G * NCP).rearrange("p (g n) -> p g n", g=G)
        vcm = view(G * NCP).rearrange("p (g k d) -> p g k d", g=G, k=NCT)
        hid = view(4 * NCP).rearrange("p (m n) -> p m n", m=4)
        Ec = view(NHF * NCT * CW).rearrange("p (f k c) -> p f k c", f=NHF, k=NCT)
        Et = view(3 * 512).rearrange("p (a c) -> p a c", a=3)
        obf = view(2 * 512).rearrange("p (a c) -> p a c", a=2)
        dm = view(8 * 128).rearrange("p (a c) -> p a c", a=8)
        vmask = view(NCT * 128).rearrange("p (a c) -> p a c", a=NCT)
        kwt = view(2 * NWT * 128).rearrange("p (a c) -> p a c", a=2)
        vwt = view(2 * NWT * 128).rearrange("p (a c) -> p a c", a=2)
        accs = view(2 * NHF * 512).bitcast(F32).rearrange("p (a c) -> p a c", a=NHF)
        s.nr_t = view(2 * 3 * 512).bitcast(F32).rearrange("p (a c) -> p a c", a=3)
        tk = view(2 * NCP).bitcast(F32)
        assert off[0] <= s.NB, (off[0], s.NB)
        b_kch, b_vch, b_mch = bufs("kch", 2), bufs("vch", 2), bufs("mch", 2)
        b_x0r, b_kcT, b_vcm, b_hid = Buf("x0r"), bufs("kcT", G), bufs("vcm", G), Buf("hid")
        b_Ec2 = [bufs("Ec%d_" % f, NCT) for f in range(NHF)]
        b_Ec = [b for l in b_Ec2 for b in l]
        b_Et, b_obf, b_dm, b_vmask = bufs("Et", 3), bufs("obf", 2), Buf("dm"), Buf("vmask")
        b_kw, b_vw, b_acc = bufs("kwt", 2), bufs("vwt", 2), bufs("accs", NHF)
        s.b_nr = bufs("nr_t", 3)
        b_tk = Buf("tkc")
        newb = (b_kch + b_vch + b_mch + [b_x0r, b_hid] + b_kcT + b_vcm + b_Ec + b_Et + b_obf + [b_dm, b_vmask]
                + b_kw + b_vw + b_acc + s.b_nr + [b_tk])
        s.alias(newb, s.bB)
        P.dma("sp", x0r, x0rd[:, :], writes=[b_x0r])
        P.op("pool", lambda e: e.memset(hid, 0.0), writes=[b_hid])
        P.op("pool", lambda e: e.memset(kcT, 0.0), writes=b_kcT)
        cc = s.sb("ccmp", [128, NCP], F32)
        sc_ = s.sb("scmp", [128, NCP], F32)
        b_ccs = Buf("ccs")
        s.rope_tables(posc_d[:, :], NCP, cc, sc_, b_ccs, s.invf, s.b_invf, s.rstd, s.b_rstd, s.rstd2,
                      s.b_rstd2)
        s.alias(s.bX, [b_cs])
        c1 = s.sb("c1", [128, 8], F32)
        b_c1 = Buf("c1")
        rawv = s.XB[:].rearrange("p a t -> p (a t)").bitcast(BF16)
        s.xb_n = 0
        for jj in range(2):
            src_d = kcr if jj == 0 else vcr
            for g in range(G):
                P.dma("sp", rawv[:, 0:S], src_d[g, :, :], writes=s.bX)
                for m in range(4):
                    wt, wtb = s.ws.load(cw1[jj * 4 + m, :, :], 32 * 128)
                    if g == 0:
                        (pcp,), pcid = s.ps.hold(1)
                        pc1, pc1b = pcp

                        def mmc(e):
                            ins = None
                            for l in range(32):
                                ins = e.matmul(pc1[:, 0:1], lhsT=wt[:, l * 128:(l + 1) * 128],
                                               rhs=cposT_t[:, jj, l:l + 1], start=(l == 0), stop=(l == 31))
                            return ins
                        P.op("pe", mmc, reads=[wtb, cposT_b], writes=[pc1b])
                        P.op("dve", lambda e: e.tensor_tensor(out=c1[:, jj * 4 + m:jj * 4 + m + 1], in0=pc1[:, 0:1],
                                                              in1=cb1_t[:, jj * 4 + m:jj * 4 + m + 1], op=ALU.add),
                             reads=[pc1b, cb1_b], writes=[b_c1])
                        s.ps.unhold(pcid)
                    ph, phb = s.ps.next()

                    def mmh(e):
                        ins = None
                        for l in range(32):
                            ins = e.matmul(ph[:, 0:NCMP], lhsT=wt[:, l * 128:(l + 1) * 128],
                                           rhs=rawv[:, l:l + 16 * (NCMP - 1) + 1:16], start=(l == 0), stop=(l == 31))
                        return ins
                    P.op("pe", mmh, reads=[wtb] + s.bX, writes=[phb])
                    P.op("act", lambda e: e.activation(out=hid[:, m, 0:NCMP], in_=ph[:, 0:NCMP], func=AF.Silu,
                                                       bias=c1[:, jj * 4 + m:jj * 4 + m + 1], scale=1.0),
                         reads=[phb, b_c1], writes=[b_hid])
                if jj == 0:
                    p2, p2b = s.ps.next()

                    def mm2(e):
                        ins = None
                        for m in range(4):
                            ins = e.matmul(p2[:, 0:NCMP], lhsT=cw2_t[:, 0, m * 128:(m + 1) * 128],
                                           rhs=hid[:, m, 0:NCMP], start=(m == 0), stop=(m == 3))
                        return ins
                    P.op("pe", mm2, reads=[cw2_b, b_hid], writes=[p2b])
                    P.op("act", lambda e: e.activation(out=tk[:, 0:NCMP], in_=p2[:, 0:NCMP], func=AF.Identity,
                                                       bias=cb2k_t[:, 0:1], scale=1.0), reads=[p2b, cb2k_b],
                         writes=[b_tk])
                    s.xb_n = 0
                    s.normrope_sb(tk, b_tk, NCMP, kn0_t[:, 0:1], kn0_b, cc, sc_, b_ccs, kcT[:, g, 0:NCMP], b_kcT[g])
                else:
                    for kt in range(NCT):
                        p2, p2b = s.ps.next()

                        def mm3(e):
                            ins = None
                            for m in range(4):
                                ins = e.matmul(p2[:, 0:128], lhsT=hid[:, m, kt * 128:(kt + 1) * 128],
                                               rhs=cw2_t[:, 1, m * 128:(m + 1) * 128], start=(m == 0), stop=(m == 3))
                            return ins
                        P.op("pe", mm3, reads=[cw2_b, b_hid], writes=[p2b])
                        P.op("dve", lambda e: e.tensor_tensor(out=vcm[:, g, kt, :], in0=p2[:, 0:128], in1=cb2v_t[:],
                                                              op=ALU.add), reads=[p2b, cb2v_b], writes=[b_vcm[g]])
        s.xb_n = 4
        XF = s.XB[:].rearrange("p a t -> p (a t)")
        gbt = [XF[:, k * 2 * T:k * 2 * T + 3 * CW].rearrange("p (b c) -> p b c", b=3) for k in range(2)]
        b_gb = [[s.bX[0], s.bX[1]], [s.bX[2], s.bX[3]]]
        RW = min(512, T // 2)
        rl_t = [s.rstd[:, 0:RW], s.rstd[:, RW:2 * RW], s.rstd2[:, 0:RW], s.rstd2[:, RW:2 * RW]]
        assert RW >= CW
        b_rl = bufs("rl", 4)
        s.alias(b_rl, [s.b_rstd, s.b_rstd2])
        rli = [0]

        def rlslot():
            i = rli[0] % 4
            rli[0] += 1
            return rl_t[i], b_rl[i]
        eti = [0]
        obi = [0]
        epool = [(Et[:, k, 0:CW], b_Et[k]) for k in range(3)]
        epool += [(Ec[:, f, k, :], b_Ec2[f][k]) for f in range(NHF) for k in range(NCT)]
        epi = [0]

        def eslot():
            k = epi[0] % len(epool)
            epi[0] += 1
            return epool[k]
        for d in range(8):
            P.op("dve", lambda e: e.tensor_scalar(out=dm[:, d, :], in0=iotad_t[:], scalar1=coff_t[:, d:d + 1],
                                                  scalar2=0.0, op0=ALU.add, op1=ALU.is_ge),
                 reads=[iotad_b, coff_b], writes=[b_dm])
        it = s.sb("impt", [128, 6, NSLC], F32)
        b_it = bufs("impt", 6)
        m8 = s.sb("m8", [128, 8], F32)
        b_m8 = Buf("m8")
        selb = s.sb("selT", [128, 128], BF16)
        b_selT = Buf("selT")
        selhi = s.sb("selThi", [32, 128], BF16)
        ones_bf = s.ones
        cki = [0]
        for g in range(G):
            for i in range(NQT):
                qc0 = 32 + i * 128
                P.op("dve", lambda e: e.tensor_scalar(out=vmask[:], in0=cio_t[:], scalar1=t0_t[:, i:i + 1],
                                                      scalar2=0.0, op0=ALU.add, op1=ALU.is_ge),
                     reads=[cio_b, t0_b], writes=[b_vmask])
                gbs = []
                for hf in range(NHF):
                    gk = (g * NQT * NHF + i * NHF + hf) % 2
                    gbv, gbb = gbt[gk], b_gb[gk]
                    for br in range(3):
                        r0 = br * H + g * R + hf * HH
                        P.dma("sp", gbv[:, br, :].rearrange("p (r q) -> p r q", r=HH),
                              gates_d[r0:r0 + HH, i * 128:(i + 1) * 128].partition_broadcast(128),
                              reads=[b_gd], writes=gbb)
                    gbs.append((gbv, gbb))

                def qrhs(hf):
                    h0 = g * R + hf * HH
                    return s.bufA[:, h0:h0 + HH, qc0:qc0 + 128]

                def attend(hf, ktiles, po, pob, pl, plb, keep=None):
                    n = len(ktiles)
                    LA = 2

                    def sS(idx):
                        ka, kb_, va, vb_, mfn = ktiles[idx]
                        pss, pssb = s.ps.next()
                        P.op("pe", lambda e: e.matmul(pss[:, 0:CW], lhsT=ka, rhs=qrhs(hf), start=True, stop=True),
                             reads=kb_ + s.bA[g * R + hf * HH:g * R + hf * HH + HH], writes=[pssb])
                        return pss, pssb

                    def sR(idx, pss, pssb):
                        ka, kb_, va, vb_, mfn = ktiles[idx]
                        if keep is not None:
                            e1, e1b = Et[:, eti[0] % 3, 0:CW], b_Et[eti[0] % 3]
                            eti[0] += 1
                        else:
                            e1, e1b = eslot()
                        P.op("act", lambda e: e.activation(out=e1, in_=pss[:, 0:CW], func=AF.Exp), reads=[pssb],
                             writes=[e1b])
                        if keep is not None:
                            e2, e2b = keep[idx]
                        else:
                            e2, e2b = e1, e1b
                        mfn(e1, e1b, e2, e2b)
                        P.op("pe", lambda e: e.matmul(po[:, 0:CW], lhsT=va, rhs=e2, start=(idx == 0),
                                                      stop=(idx == n - 1)), reads=vb_ + [e2b], writes=[pob])
                        P.op("pe", lambda e: e.matmul(pl[:, 0:CW], lhsT=ones_bf[:], rhs=e2, start=(idx == 0),
                                                      stop=(idx == n - 1)), reads=[e2b, s.b_ones], writes=[plb])
                    pend = []
                    for idx in range(n):
                        pend.append(sS(idx))
                        if idx >= LA:
                            sR(idx - LA, *pend[idx - LA])
                    for idx in range(max(0, n - LA), n):
                        sR(idx, *pend[idx])

                def bmask(mask_ap, mbufs, scal=None, sbufs=()):
                    def f(e1, e1b, e2, e2b):
                        e13 = e1.rearrange("p (r q) -> p r q", r=HH)
                        e23 = e2.rearrange("p (r q) -> p r q", r=HH)
                        mb = mask_ap.unsqueeze(1).broadcast_to([128, HH, 128])
                        if scal is None:
                            P.op("dve", lambda e: e.tensor_tensor(out=e23, in0=e13, in1=mb, op=ALU.mult),
                                 reads=[e1b] + mbufs, writes=[e2b])
                        else:
                            P.op("dve", lambda e: e.scalar_tensor_tensor(out=e23, in0=e13, scalar=scal, in1=mb,
                                                                         op0=ALU.mult, op1=ALU.mult),
                                 reads=[e1b] + mbufs + list(sbufs), writes=[e2b])
                    return f

                def finish_branch(br, hf, po, pob, pl, plb, first, last, want_rl=False):
                    gbv, gbb = gbs[hf]
                    rl, rlb = rlslot()
                    P.op("dve", lambda e: e.tensor_scalar(out=rl[:, 0:CW], in0=pl[:, 0:CW], scalar1=1e-30,
                                                          scalar2=None, op0=ALU.max), reads=[plb], writes=[rlb])
                    P.op("dve", lambda e: e.reciprocal(out=rl[:, 0:CW], in_=rl[:, 0:CW]), reads=[rlb], writes=[rlb])
                    wt_, wtb_ = rlslot()
                    P.op("dve", lambda e: e.tensor_tensor(out=wt_[:, 0:CW], in0=rl[:, 0:CW], in1=gbv[:, br, :],
                                                          op=ALU.mult), reads=[rlb] + gbb, writes=[wtb_])
                    if first:
                        P.op("dve", lambda e: e.tensor_tensor(out=accs[:, hf, 0:CW], in0=po[:, 0:CW], in1=wt_[:, 0:CW],
                                                              op=ALU.mult), reads=[pob, wtb_], writes=[b_acc[hf]])
                    else:
                        P.op("dve", lambda e: e.tensor_tensor(out=wt_[:, 0:CW], in0=po[:, 0:CW], in1=wt_[:, 0:CW],
                                                              op=ALU.mult), reads=[pob, wtb_], writes=[wtb_])
                        if not last:
                            P.op("dve", lambda e: e.tensor_tensor(out=accs[:, hf, 0:CW], in0=accs[:, hf, 0:CW],
                                                                  in1=wt_[:, 0:CW], op=ALU.add),
                                 reads=[b_acc[hf], wtb_], writes=[b_acc[hf]])
                        else:
                            ob_, obb_ = obf[:, obi[0] % 2, 0:CW], b_obf[obi[0] % 2]
                            obi[0] += 1
                            P.op("dve", lambda e: e.tensor_tensor(out=ob_, in0=accs[:, hf, 0:CW], in1=wt_[:, 0:CW],
                                                                  op=ALU.add), reads=[b_acc[hf], wtb_], writes=[obb_])
                            h0 = g * R + hf * HH
                            P.dma("act", ot_d[h0:h0 + HH, :, i * 128:(i + 1) * 128].rearrange("h p q -> p h q"),
                                  ob_.rearrange("p (r q) -> p r q", r=HH), reads=[obb_], writes=b_ot[h0:h0 + HH])
                    return rl, rlb

                (pimp_,), impid = s.ps.hold(1)
                pimp, pimpb = pimp_
                held = []
                for hf in range(NHF):
                    (a, b), ids = s.ps.hold(2)
                    held.append((a, b, ids))
                    po, pob = a
                    pl, plb = b
                    kt_list = []
                    for kt in range(NCT):
                        kt_list.append((kcT[:, g, kt * 128:(kt + 1) * 128], [b_kcT[g]], vcm[:, g, kt, :], [b_vcm[g]],
                                        bmask(vmask[:, kt, :], [b_vmask])))
                    keep = [(Ec[:, hf, kt, :], b_Ec2[hf][kt]) for kt in range(NCT)]
                    attend(hf, kt_list, po, pob, pl, plb, keep=keep)
                    rl, rlb = finish_branch(0, hf, po, pob, pl, plb, True, False)
                    for kt in range(NCT):
                        pc, pcb = Et[:, eti[0] % 3, 0:CW], b_Et[eti[0] % 3]
                        eti[0] += 1
                        P.op("dve", lambda e: e.tensor_tensor(out=pc, in0=Ec[:, hf, kt, :], in1=rl[:, 0:CW],
                                                              op=ALU.mult), reads=[b_Ec2[hf][kt], rlb], writes=[pcb])

                        def mmi(e):
                            ins = None
                            for r in range(HH):
                                first = (hf == 0 and kt == 0 and r == 0)
                                lastm = (hf == NHF - 1 and kt == NCT - 1 and r == HH - 1)
                                ins = e.matmul(pimp[:, 0:NSLC], lhsT=pc[:, r * 128:(r + 1) * 128],
                                               rhs=cmat_t[:, kt, :], start=first, stop=lastm)
                            return ins
                        P.op("pe", mmi, reads=[pcb, cmat_b], writes=[pimpb])
                    s.ps.unhold(ids)
                dist, valid, forced, impp, imp2, selc = [it[:, k, :] for k in range(6)]
                bd, bv, bf_, bi_, bi2, bs_ = b_it
                P.op("dve", lambda e: e.tensor_scalar(out=dist, in0=iotab_t[:], scalar1=-1.0,
                                                      scalar2=tq_t[:, i:i + 1], op0=ALU.mult, op1=ALU.add),
                     reads=[iotab_b, tq_b], writes=[bd])
                P.op("dve", lambda e: e.tensor_scalar(out=valid, in0=dist, scalar1=0.0, scalar2=None,
                                                      op0=ALU.is_ge), reads=[bd], writes=[bv])
                P.op("dve", lambda e: e.tensor_scalar(out=forced, in0=dist, scalar1=2.0, scalar2=None,
                                                      op0=ALU.is_lt), reads=[bd], writes=[bf_])
                P.op("dve", lambda e: e.tensor_tensor(out=forced, in0=forced, in1=valid, op=ALU.mult),
                     reads=[bf_, bv], writes=[bf_])
                P.op("dve", lambda e: e.tensor_tensor(out=forced, in0=forced, in1=b0_t[:], op=ALU.max),
                     reads=[bf_, b0_b], writes=[bf_])
                P.op("dve", lambda e: e.scalar_tensor_tensor(out=impp, in0=pimp[:, 0:NSLC], scalar=1.0, in1=valid,
                                                             op0=ALU.add, op1=ALU.mult), reads=[pimpb, bv],
                     writes=[bi_])
                P.op("dve", lambda e: e.scalar_tensor_tensor(out=impp, in0=forced, scalar=1e4, in1=impp,
                                                             op0=ALU.mult, op1=ALU.add), reads=[bf_, bi_],
                     writes=[bi_])
                P.op("dve", lambda e: e.tensor_scalar(out=impp, in0=impp, scalar1=-1.0, scalar2=None, op0=ALU.add),
                     reads=[bi_], writes=[bi_])
                cur, curb = impp, bi_
                for rnd in range(cfg.TOPK // 8):
                    P.op("dve", lambda e: e.max(out=m8[:], in_=cur), reads=[curb], writes=[b_m8])
                    if rnd < cfg.TOPK // 8 - 1:
                        P.op("dve", lambda e: e.match_replace(out=imp2, in_to_replace=m8[:], in_values=cur,
                                                              imm_value=-2.0), reads=[curb, b_m8], writes=[bi2])
                        cur, curb = imp2, bi2
                P.op("dve", lambda e: e.tensor_scalar(out=selc, in0=impp, scalar1=m8[:, 7:8], scalar2=None,
                                                      op0=ALU.is_ge), reads=[bi_, b_m8], writes=[bs_])
                P.op("dve", lambda e: e.tensor_tensor(out=selc, in0=selc, in1=valid, op=ALU.mult),
                     reads=[bs_, bv], writes=[bs_])
                s.ps.unhold(impid)
                pT, pTb = s.ps.next()
                P.op("pe", lambda e: e.transpose(pT[0:NSLC, 0:128], selc, ident_t[:]), reads=[bs_, ident_b],
                     writes=[pTb])
                P.op("act", lambda e: e.activation(out=selb[0:NSLC, :], in_=pT[0:NSLC, 0:128], func=AF.Identity),
                     reads=[pTb], writes=[b_selT])
                if NSLC > 96:
                    pT2, pT2b = s.ps.next()
                    P.op("pe", lambda e: e.transpose(pT2[0:32, 0:128], selc[:, 96:128], ident_t[:]),
                         reads=[bs_, ident_b], writes=[pT2b])
                    P.op("act", lambda e: e.activation(out=selhi[0:32, :], in_=pT2[0:32, 0:128], func=AF.Identity),
                         reads=[pT2b, b_selT], writes=[b_selT])
                nkt = 8 * i + 8
                for ch in range(nkt // 8):
                    k_ = cki[0] % 2
                    cki[0] += 1
                    P.dma("sp", kch[k_], ksl[g, :, ch * 1024:(ch + 1) * 1024], writes=[b_kch[k_]])
                    P.dma("sp", vch[k_], vsl[ch * 8:(ch + 1) * 8, :, g * 128:(g + 1) * 128].rearrange("k p d -> p k d"),
                          writes=[b_vch[k_]])
                    wbase = ((ch * 16) // KPW) * KPW
                    pb0 = wbase % 128
                    for half4 in range(2):
                        pm, pmb = s.ps.next()

                        def mmx(e):
                            ins = None
                            for kk in range(4):
                                ktl = half4 * 4 + kk
                                colo = ((ch * 16) - wbase) * 64 + ktl * 128
                                if pb0 == 96:
                                    ins = e.matmul(pm[:, kk * 128:(kk + 1) * 128],
                                                   lhsT=x0r[0:KPW, colo:colo + 128],
                                                   rhs=selhi[0:KPW, :], start=True, stop=True)
                                else:
                                    ins = e.matmul(pm[:, kk * 128:(kk + 1) * 128],
                                                   lhsT=x0r[pb0:pb0 + KPW, colo:colo + 128],
                                                   rhs=selb[wbase:wbase + KPW, :], start=True, stop=True)
                            return ins
                        P.op("pe", mmx, reads=[b_x0r, b_selT], writes=[pmb])
                        if ch == nkt // 8 - 1:
                            P.op("dve", lambda e: e.tensor_tensor(
                                out=mch[k_][:, half4 * 4:half4 * 4 + 4, :],
                                in0=pm[:, 0:512].rearrange("p (k q) -> p k q", k=4),
                                in1=dm[:, half4 * 4:half4 * 4 + 4, :], op=ALU.mult),
                                reads=[pmb, b_dm], writes=[b_mch[k_]])
                        else:
                            P.op("act", lambda e: e.activation(
                                out=mch[k_][:, half4 * 4:half4 * 4 + 4, :],
                                in_=pm[:, 0:512].rearrange("p (k q) -> p k q", k=4), func=AF.Identity),
                                reads=[pmb], writes=[b_mch[k_]])
                    if ch == 0:
                        sheld = []
                        for hf in range(NHF):
                            (a, b), ids = s.ps.hold(2)
                            sheld.append((a, b, ids))
                    items = [(hf, ktl) for hf in range(NHF) for ktl in range(8)]
                    LA = 2

                    def sS2(hf, ktl):
                        pss, pssb = s.ps.next()
                        P.op("pe", lambda e: e.matmul(pss[:, 0:CW], lhsT=kch[k_][:, ktl * 128:(ktl + 1) * 128],
                                                      rhs=qrhs(hf), start=True, stop=True),
                             reads=[b_kch[k_]] + s.bA[g * R + hf * HH:g * R + hf * HH + HH], writes=[pssb])
                        return pss, pssb

                    def sR2(hf, ktl, pss, pssb):
                        (po, pob), (pl, plb), _ = sheld[hf]
                        kt = ch * 8 + ktl
                        e1, e1b = eslot()
                        P.op("act", lambda e: e.activation(out=e1, in_=pss[:, 0:CW], func=AF.Exp),
                             reads=[pssb], writes=[e1b])
                        bmask(mch[k_][:, ktl, :], [b_mch[k_]])(e1, e1b, e1, e1b)
                        P.op("pe", lambda e: e.matmul(po[:, 0:CW], lhsT=vch[k_][:, ktl, :], rhs=e1,
                                                      start=(kt == 0), stop=(kt == nkt - 1)),
                             reads=[b_vch[k_], e1b], writes=[pob])
                        P.op("pe", lambda e: e.matmul(pl[:, 0:CW], lhsT=ones_bf[:], rhs=e1, start=(kt == 0),
                                                      stop=(kt == nkt - 1)), reads=[e1b, s.b_ones], writes=[plb])
                    pend = []
                    for n_, (hf, ktl) in enumerate(items):
                        pend.append(sS2(hf, ktl))
                        if n_ >= LA:
                            sR2(*items[n_ - LA], *pend[n_ - LA])
                    for n_ in range(max(0, len(items) - LA), len(items)):
                        sR2(*items[n_], *pend[n_])
                for hf in range(NHF):
                    (po, pob), (pl, plb), ids = sheld[hf]
                    finish_branch(1, hf, po, pob, pl, plb, False, False)
                    s.ps.unhold(ids)
                wk = (g * NQT + i) % 2
                P.dma("sp", kwt[:, wk, :], kwd[i, g, :, :], writes=[b_kw[wk]])
                P.dma("sp", vwt[:, wk, :].rearrange("p (k d) -> p k d", k=NWT),
                      vwd[i, :, :, g * 128:(g + 1) * 128].rearrange("k p d -> p k d"), writes=[b_vw[wk]])
                for hf in range(NHF):
                    (a, b), ids = s.ps.hold(2)
                    po, pob = a
                    pl, plb = b
                    kt_list = []
                    for jt in range(NWT):
                        kt_list.append((kwt[:, wk, jt * 128:(jt + 1) * 128], [b_kw[wk]],
                                        vwt[:, wk, jt * 128:(jt + 1) * 128], [b_vw[wk]],
                                        bmask(wmask_t[:, jt, :], [wmask_b],
                                              scal=wv_t[:, i * NWT + jt:i * NWT + jt + 1], sbufs=[wv_b])))
                    attend(hf, kt_list, po, pob, pl, plb)
                    finish_branch(2, hf, po, pob, pl, plb, False, True)
                    s.ps.unhold(ids)
        s.alias(s.bB, newb)
        s.alias([s.b_rstd, s.b_rstd2], b_rl)
        for h in range(H):
            P.dma("sp", s.bufA[:, h, 32:32 + T], ot_d[h, :, :], reads=[b_ot[h]], writes=[s.bA[h]])
        for n in range(DC):
            wt, wtb = s.ws.load(wo[n, :, :], H * 128)
            xb, xbb = s.xbuf()
            P.dma("sp", xb[:, 0:T], xin[n, :, :], reads=[xin_bufs[n]], writes=[xbb])
            P.flush()
            for (c0, w) in sg:
                po, pob = s.ps.next()

                def mmo(e):
                    ins = None
                    for kc in range(H):
                        ins = e.matmul(po[:, 0:w], lhsT=wt[:, kc * 128:(kc + 1) * 128],
                                       rhs=s.bufA[:, kc, 32 + c0:32 + c0 + w], start=(kc == 0), stop=(kc == H - 1))
                    return ins
                P.op("pe", mmo, reads=[wtb] + s.bA, writes=[pob])
                P.op("dve", lambda e: e.scalar_tensor_tensor(out=xb[:, c0:c0 + w], in0=po[:, 0:w],
                                                             scalar=s.mvec(2, n), in1=xb[:, c0:c0 + w],
                                                             op0=ALU.mult, op1=ALU.add),
                     reads=[pob, xbb, s.b_m], writes=[xbb])
            P.defer(lambda n=n, xb=xb, xbb=xbb: P.dma("act", xs[n, :, :], xb[:, 0:T], reads=[xbb],
                                                  writes=[xs_bufs[n]]))
        P.flush()

    def normrope_sb(s, tk, b_tk, w, gcol, b_g, cc, sc_, b_ccs, dst, dstb):
        P = s.P
        sq, sqb = s.sq_slot()
        P.op("act", lambda e: e.activation(out=sq[:, 0:w], in_=tk[:, 0:w], func=AF.Square), reads=[b_tk], writes=[sqb])
        k1, k1b = s.k1b[:, s.k1i % 2, :], s.b_k1[s.k1i % 2]
        s.k1i += 1
        P.op("act", lambda e: e.activation(out=k1[:, 0:w], in_=tk[:, 0:w], func=AF.Identity, scale=gcol),
             reads=[b_tk, b_g], writes=[k1b])
        pss, pssb = s.ps.next()
        P.op("pe", lambda e: e.matmul(pss[:, 0:w], lhsT=s.ones[:], rhs=sq[:, 0:w], start=True, stop=True),
             reads=[sqb, s.b_ones], writes=[pssb])
        pr, prb = s.ps.next()
        P.op("pe", lambda e: e.matmul(pr[:, 0:w], lhsT=s.rotm[:], rhs=k1[:, 0:w], start=True, stop=True),
             reads=[k1b, s.b_rotm], writes=[prb])
        rk, t1, t2 = [s.nr_t[:, k, :] for k in range(3)]
        rkb, t1b, t2b = s.b_nr
        P.op("act", lambda e: e.activation(out=rk[:, 0:w], in_=pss[:, 0:w], func=AF.Sqrt, bias=s.epsc[:, 0:1],
                                           scale=1.0 / 128.0), reads=[pssb, s.b_eps], writes=[rkb])
        P.op("dve", lambda e: e.reciprocal(out=rk[:, 0:w], in_=rk[:, 0:w]), reads=[rkb], writes=[rkb])
        P.op("dve", lambda e: e.tensor_tensor(out=t1[:, 0:w], in0=k1[:, 0:w], in1=cc[:, 0:w], op=ALU.mult),
             reads=[k1b, b_ccs], writes=[t1b])
        P.op("dve", lambda e: e.tensor_tensor(out=t2[:, 0:w], in0=pr[:, 0:w], in1=sc_[:, 0:w], op=ALU.mult),
             reads=[prb, b_ccs], writes=[t2b])
        P.op("dve", lambda e: e.tensor_tensor(out=t1[:, 0:w], in0=t1[:, 0:w], in1=t2[:, 0:w], op=ALU.add),
             reads=[t1b, t2b], writes=[t1b])
        P.op("dve", lambda e: e.tensor_tensor(out=dst, in0=t1[:, 0:w], in1=rk[:, 0:w], op=ALU.mult),
             reads=[t1b, rkb], writes=[dstb])

    def ffn(s, xs, xs_bufs, xdst, xdst_bufs, final_is_output):
        cfg, P = s.cfg, s.P
        DC, T, FC = cfg.DC, cfg.T, cfg.FC
        wg = s.din("w_gu_l", [2 * FC, 128, DC * 128])
        wd = s.din("w_dn_l", [DC, 128, FC * 128])
        sg = segs_of(T)
        s.rms_stats(lambda j: xs[j, :, :], xs_bufs, T, s.rstd, s.b_rstd)
        s.modulate(lambda j: xs[j, :, :], xs_bufs, T, s.rstd, s.b_rstd, s.Affn, lambda j: s.mvec(3, j),
                   lambda j: s.bufB[:, j, :], s.bB)
        sgt = s.sb("sgt", [128, 2, 512], BF16)
        b_sgt = bufs("sgt", 2)
        sgi = 0
        f0 = 0
        nq = len(cfg.FQ)
        for qi, nf in enumerate(cfg.FQ):
            last_q = qi == nq - 1
            for fl in range(nf):
                f = f0 + fl
                wgt, wgb = s.ws.load(wg[f, :, :], DC * 128)
                wut, wub = s.ws.load(wg[FC + f, :, :], DC * 128)
                for (c0, w) in sg:
                    pg, pgb = s.ps.next()
                    pu, pub = s.ps.next()

                    def mmg(e, wt=wgt, pt=pg):
                        ins = None
                        for kc in range(DC):
                            ins = e.matmul(pt[:, 0:w], lhsT=wt[:, kc * 128:(kc + 1) * 128],
                                           rhs=s.bufB[:, kc, c0:c0 + w], start=(kc == 0), stop=(kc == DC - 1))
                        return ins
                    P.op("pe", mmg, reads=[wgb] + s.bB, writes=[pgb])
                    P.op("pe", lambda e: mmg(e, wut, pu), reads=[wub] + s.bB, writes=[pub])
                    st_, stb_ = sgt[:, sgi % 2, :], b_sgt[sgi % 2]
                    sgi += 1
                    P.op("act", lambda e: e.activation(out=st_[:, 0:w], in_=pg[:, 0:w], func=AF.Silu),
                         reads=[pgb], writes=[stb_])
                    P.op("dve", lambda e: e.tensor_tensor(out=s.bufA[:, fl, c0:c0 + w], in0=st_[:, 0:w],
                                                          in1=pu[:, 0:w], op=ALU.mult),
                         reads=[stb_, pub], writes=[s.bA[fl]])
            for n in range(DC):
                nsub = -(-nf * 128 // (DC * 128))
                per = -(-nf // nsub)
                subs = []
                for su in range(nsub):
                    a, b = su * per, min(nf, (su + 1) * per)
                    wt, wb_ = s.ws.load(wd[n, :, (f0 + a) * 128:(f0 + b) * 128], (b - a) * 128)
                    subs.append((a, b, wt, wb_))
                xb, xbb = s.xbuf()
                P.dma("sp", xb[:, 0:T], xs[n, :, :], reads=[xs_bufs[n]], writes=[xbb])
                P.flush()
                for (c0, w) in sg:
                    po, pob = s.ps.next()

                    def mmd(e):
                        ins = None
                        for (a, b, wt, _) in subs:
                            for fl in range(a, b):
                                ins = e.matmul(po[:, 0:w], lhsT=wt[:, (fl - a) * 128:(fl - a + 1) * 128],
                                               rhs=s.bufA[:, fl, c0:c0 + w], start=(fl == 0), stop=(fl == nf - 1))
                        return ins
                    P.op("pe", mmd, reads=[x[3] for x in subs] + s.bA[0:nf], writes=[pob])
                    P.op("dve", lambda e: e.scalar_tensor_tensor(out=xb[:, c0:c0 + w], in0=po[:, 0:w],
                                                                 scalar=s.mvec(5, n), in1=xb[:, c0:c0 + w],
                                                                 op0=ALU.mult, op1=ALU.add),
                         reads=[pob, xbb, s.b_m], writes=[xbb])
                if last_q:
                    P.defer(lambda n=n, xb=xb, xbb=xbb: P.dma("act", xdst[n, :, :], xb[:, 0:T], reads=[xbb],
                                                          writes=[xdst_bufs[n]], is_output=final_is_output))
                else:
                    P.defer(lambda n=n, xb=xb, xbb=xbb: P.dma("act", xs[n, :, :], xb[:, 0:T], reads=[xbb],
                                                          writes=[xs_bufs[n]]))
            P.flush()
            f0 += nf

    def conv(s, xin, xin_bufs, xh, xh_bufs, xs, xs_bufs):
        cfg, P = s.cfg, s.P
        DC, T, D, CW = cfg.DC, cfg.T, cfg.D, cfg.CONVW
        w1 = s.din("w_pw1_l", [2 * DC, 128, DC * 128])
        w2 = s.din("w_pw2_l", [DC, 128, DC * 128])
        b1_t, b1_b = s.load_small("b_pw1_l", [128, 2 * DC])
        wdw_t, wdw_b = s.load_small("w_dw_l", [128, DC, CW])
        bdw_t, bdw_b = s.load_small("b_dw_l", [128, DC])
        lg_t, lg_b = s.load_small("ln_g_l", [128, DC])
        lb_t, lb_b = s.load_small("ln_b_l", [128, DC])
        b2_t, b2_b = s.load_small("b_pw2_l", [128, DC])
        hm_t, hm_b = s.load_small("hmask", [128, 1])
        hTh = s.sb("hTh", [128, DC, 32], BF16)
        b_hTh = bufs("hTh", DC)
        rsh = s.sb("rsh", [128, 32], F32)
        b_rsh = Buf("rsh")
        sg = segs_of(T)
        s.rms_stats(lambda j: xin[j, :, :], xin_bufs, T, s.rstd, s.b_rstd)
        s.rms_stats(lambda j: xh[j, :, :], xh_bufs, 32, rsh, b_rsh)
        s.modulate(lambda j: xin[j, :, :], xin_bufs, T, s.rstd, s.b_rstd, s.Amix, lambda j: s.mvec(0, j),
                   lambda j: s.bufB[:, j, :], s.bB)
        s.modulate(lambda j: xh[j, :, :], xh_bufs, 32, rsh, b_rsh, s.Amix, lambda j: s.mvec(0, j),
                   lambda j: hTh[:, j, :], b_hTh)
        sig = s.sb("sig", [128, 2, 512], BF16)
        b_sig = bufs("sig", 2)
        sgi = 0
        ps1, ids1 = s.ps.hold(len(sg))
        ps2, ids2 = s.ps.hold(len(sg))
        allseg = [(-1, 0, 32)] + [(i, c0, w) for i, (c0, w) in enumerate(sg)]
        for j in range(DC):
            wa, wab = s.ws.load(w1[j, :, :], DC * 128)
            wg_, wgb = s.ws.load(w1[DC + j, :, :], DC * 128)
            for (si, c0, w) in allseg:
                pa, pab = s.ps.next()
                pg, pgb = s.ps.next()
                if si < 0:
                    rhs = lambda kc: hTh[:, kc, :]
                    rb = b_hTh
                    dcol = 0
                else:
                    rhs = lambda kc, c0=c0, w=w: s.bufB[:, kc, c0:c0 + w]
                    rb = s.bB
                    dcol = 32 + c0

                def mm(e, wt, pt):
                    ins = None
                    for kc in range(DC):
                        ins = e.matmul(pt[:, 0:w], lhsT=wt[:, kc * 128:(kc + 1) * 128], rhs=rhs(kc),
                                       start=(kc == 0), stop=(kc == DC - 1))
                    return ins
                P.op("pe", lambda e: mm(e, wa, pa), reads=[wab] + rb, writes=[pab])
                P.op("pe", lambda e: mm(e, wg_, pg), reads=[wgb] + rb, writes=[pgb])
                sg_, sgb_ = sig[:, sgi % 2, :], b_sig[sgi % 2]
                sgi += 1
                P.op("act", lambda e: e.activation(out=sg_[:, 0:w], in_=pg[:, 0:w], func=AF.Sigmoid,
                                                   bias=b1_t[:, DC + j:DC + j + 1], scale=1.0),
                     reads=[pgb, b1_b], writes=[sgb_])
                P.op("dve", lambda e: e.scalar_tensor_tensor(out=s.bufA[:, j, dcol:dcol + w], in0=pa[:, 0:w],
                                                             scalar=b1_t[:, j:j + 1], in1=sg_[:, 0:w],
                                                             op0=ALU.add, op1=ALU.mult),
                     reads=[pab, sgb_, b1_b], writes=[s.bA[j]])
                if si < 0:
                    P.op("dve", lambda e: e.tensor_scalar(out=s.bufA[:, j, 0:32], in0=s.bufA[:, j, 0:32],
                                                          scalar1=hm_t[:, 0:1], scalar2=None, op0=ALU.mult),
                         reads=[s.bA[j], hm_b], writes=[s.bA[j]])
            y, yb = s.xbuf()
            off = 32 - (CW - 1)
            for k in range(CW):
                if k == 0:
                    P.op("dve", lambda e: e.tensor_scalar(out=y, in0=s.bufA[:, j, off:off + T],
                                                          scalar1=wdw_t[:, j, 0:1], scalar2=bdw_t[:, j:j + 1],
                                                          op0=ALU.mult, op1=ALU.add),
                         reads=[s.bA[j], wdw_b, bdw_b], writes=[yb])
                else:
                    P.op("dve", lambda e: e.scalar_tensor_tensor(out=y, in0=s.bufA[:, j, off + k:off + k + T],
                                                                 scalar=wdw_t[:, j, k:k + 1], in1=y,
                                                                 op0=ALU.mult, op1=ALU.add),
                         reads=[s.bA[j], wdw_b, yb], writes=[yb])
            P.op("act", lambda e: e.activation(out=s.bufA[:, j, 32:32 + T], in_=y, func=AF.Identity), reads=[yb],
                 writes=[s.bA[j]])
            for si, (c0, w) in enumerate(sg):
                sq, sqb = s.sq_slot()
                P.op("act", lambda e: e.activation(out=sq[:, 0:w], in_=y[:, c0:c0 + w], func=AF.Square),
                     reads=[yb], writes=[sqb])
                p1, p1b = ps1[si]
                p2, p2b = ps2[si]
                P.op("pe", lambda e: e.matmul(p1[:, 0:w], lhsT=s.ones[:], rhs=s.bufA[:, j, 32 + c0:32 + c0 + w],
                                              start=(j == 0), stop=(j == DC - 1)),
                     reads=[s.bA[j], s.b_ones], writes=[p1b])
                P.op("pe", lambda e: e.matmul(p2[:, 0:w], lhsT=s.ones[:], rhs=sq[:, 0:w],
                                              start=(j == 0), stop=(j == DC - 1)),
                     reads=[sqb, s.b_ones], writes=[p2b])
        mean, mean_b = s.rstd2, s.b_rstd2
        for si, (c0, w) in enumerate(sg):
            p1, p1b = ps1[si]
            p2, p2b = ps2[si]
            P.op("act", lambda e: e.activation(out=mean[:, c0:c0 + w], in_=p1[:, 0:w], func=AF.Identity,
                                               scale=1.0 / D), reads=[p1b], writes=[mean_b])
            xb, xbb = s.xbuf()
            P.op("dve", lambda e: e.tensor_tensor(out=xb[:, 0:w], in0=mean[:, c0:c0 + w], in1=mean[:, c0:c0 + w],
                                                  op=ALU.mult), reads=[mean_b], writes=[xbb])
            P.op("dve", lambda e: e.scalar_tensor_tensor(out=xb[:, 0:w], in0=p2[:, 0:w], scalar=1.0 / D,
                                                         in1=xb[:, 0:w], op0=ALU.mult, op1=ALU.subtract),
                 reads=[p2b, xbb], writes=[xbb])
            P.op("act", lambda e: e.activation(out=s.rstd[:, c0:c0 + w], in_=xb[:, 0:w], func=AF.Sqrt,
                                               bias=s.epsc[:, 0:1], scale=1.0), reads=[xbb, s.b_eps],
                 writes=[s.b_rstd])
        P.op("dve", lambda e: e.reciprocal(out=s.rstd[:, 0:T], in_=s.rstd[:, 0:T]), reads=[s.b_rstd],
             writes=[s.b_rstd])
        s.ps.unhold(ids1 + ids2)
        for j in range(DC):
            xb, xbb = s.xbuf()
            P.op("dve", lambda e: e.tensor_tensor(out=xb[:, 0:T], in0=s.bufA[:, j, 32:32 + T], in1=mean[:, 0:T],
                                                  op=ALU.subtract), reads=[s.bA[j], mean_b], writes=[xbb])
            P.op("dve", lambda e: e.tensor_tensor(out=xb[:, 0:T], in0=xb[:, 0:T], in1=s.rstd[:, 0:T], op=ALU.mult),
                 reads=[xbb, s.b_rstd], writes=[xbb])
            P.op("act", lambda e: e.activation(out=s.bufA[:, j, 32:32 + T], in_=xb[:, 0:T], func=AF.Silu,
                                               bias=lb_t[:, j:j + 1], scale=lg_t[:, j:j + 1]),
                 reads=[xbb, lg_b, lb_b], writes=[s.bA[j]])
        b2g = s.sb("b2g", [128, DC], F32)
        b_b2g = Buf("b2g")
        P.op("dve", lambda e: e.tensor_tensor(out=b2g[:], in0=b2_t[:], in1=s.m[:, 2 * DC:3 * DC], op=ALU.mult),
             reads=[b2_b, s.b_m], writes=[b_b2g])
        for n in range(DC):
            wt, wtb = s.ws.load(w2[n, :, :], DC * 128)
            xb, xbb = s.xbuf()
            P.dma("sp", xb[:, 0:T], xin[n, :, :], reads=[xin_bufs[n]], writes=[xbb])
            P.flush()
            for (c0, w) in sg:
                po, pob = s.ps.next()

                def mm2(e):
                    ins = None
                    for kc in range(DC):
                        ins = e.matmul(po[:, 0:w], lhsT=wt[:, kc * 128:(kc + 1) * 128],
                                       rhs=s.bufA[:, kc, 32 + c0:32 + c0 + w], start=(kc == 0), stop=(kc == DC - 1))
                    return ins
                P.op("pe", mm2, reads=[wtb] + s.bA, writes=[pob])
                P.op("dve", lambda e: e.scalar_tensor_tensor(out=xb[:, c0:c0 + w], in0=po[:, 0:w],
                                                             scalar=s.mvec(2, n), in1=xb[:, c0:c0 + w],
                                                             op0=ALU.mult, op1=ALU.add),
                     reads=[pob, xbb, s.b_m], writes=[xbb])
            P.op("pool", lambda e: e.tensor_scalar(out=xb[:, 0:T], in0=xb[:, 0:T], scalar1=b2g[:, n:n + 1],
                                                   scalar2=None, op0=ALU.add), reads=[xbb, b_b2g], writes=[xbb])
            P.defer(lambda n=n, xb=xb, xbb=xbb: P.dma("act", xs[n, :, :], xb[:, 0:T], reads=[xbb],
                                                  writes=[xs_bufs[n]]))
        P.flush()


def build_conv_launch(cfg, with_mod, with_kv=False):
    L = LayerBuilder(cfg, "conv", with_mod, with_kv)
    nc, DC, T = L.nc, cfg.DC, cfg.T
    L.setup_common()
    xin = L.din("xT", [DC, 128, T])
    xh = L.din("xh", [DC, 128, 32])
    xs = nc.dram_tensor("xs", [DC, 128, T], F32)
    y = L.dout("yT", [DC, 128, T])
    b_xin, b_xh, b_xs, b_y = bufs("xin", DC), bufs("xh", DC), bufs("xs", DC), bufs("y", DC)
    L.vec_setup(False)
    L.conv(xin, b_xin, xh, b_xh, xs, b_xs)
    L.ffn(xs, b_xs, y, b_y, True)
    if with_kv:
        L.kvproj(y, b_y)
    L.P.finish()
    L.es.close()
    return L


def vecP(v):
    v = np.asarray(v, np.float32)
    return np.ascontiguousarray(v.reshape(-1, 128).T)


def wunits(W):
    K, N = W.shape
    a = W.reshape(K // 128, 128, N // 128, 128).transpose(2, 1, 0, 3)
    return np.ascontiguousarray(a).reshape(N // 128, 128, (K // 128) * 128)


def xT_layout(x2d):
    T, D = x2d.shape
    return np.ascontiguousarray(x2d.T).reshape(D // 128, 128, T)


def xT_unlayout(a):
    DC, _, T = a.shape
    return np.ascontiguousarray(a.reshape(DC * 128, T).T)


def conv_launch_inputs(cfg, layer, x_full, mod_in, P):
    DC, T, D, NCr = cfg.DC, cfg.T, cfg.D, cfg.NCORES
    shared = {}
    if mod_in is None:
        shared["c_l"] = vecP(P["c"][0])
        shared["b_ada_l"] = vecP(P["b_ada"])
        shared["w_ada_l"] = np.ascontiguousarray(P["w_ada"]).reshape(DC, 128, 6 * D)
    else:
        shared["mod_in"] = mod_in
    shared["ada_l"] = vecP(P["ada_emb"][layer].reshape(-1))
    shared["norm_mix_l"] = vecP(P["norm_mix"][layer])
    shared["norm_ffn_l"] = vecP(P["norm_ffn"][layer])
    shared["w_pw1_l"] = wunits(P["conv_w_pw1"][layer])
    shared["w_pw2_l"] = wunits(P["conv_w_pw2"][layer])
    shared["b_pw1_l"] = vecP(P["conv_b_pw1"][layer])
    shared["w_dw_l"] = np.ascontiguousarray(P["conv_w_dw"][layer].T.reshape(DC, 128, cfg.CONVW).transpose(1, 0, 2))
    shared["b_dw_l"] = vecP(P["conv_b_dw"][layer])
    shared["ln_g_l"] = vecP(P["conv_ln_g"][layer])
    shared["ln_b_l"] = vecP(P["conv_ln_b"][layer])
    shared["b_pw2_l"] = vecP(P["conv_b_pw2"][layer])
    shared["w_gu_l"] = wunits(P["ffn_w_gu"][layer])
    wd = P["ffn_w_down"][layer]
    shared["w_dn_l"] = wunits(wd)
    maps = []
    for c in range(NCr):
        m = dict(shared)
        xs_ = x_full[c * T:(c + 1) * T]
        m["xT"] = xT_layout(xs_)
        if c == 0:
            m["xh"] = np.zeros((DC, 128, 32), np.float32)
            m["hmask"] = np.zeros((128, 1), np.float32)
        else:
            m["xh"] = xT_layout(x_full[c * T - 32:c * T])
            m["hmask"] = np.ones((128, 1), np.float32)
        maps.append(m)
    return maps


def rope_const_inputs():
    half = 64
    inv = (10000.0 ** (-np.arange(half, dtype=np.float32) / half)).astype(np.float32)
    invf = np.concatenate([inv, inv]).reshape(128, 1).astype(np.float32)
    rotm = np.zeros((128, 128), np.float32)
    for m in range(64):
        rotm[m + 64, m] = -1.0
        rotm[m, m + 64] = 1.0
    return invf, rotm


def kv_launch_extra_inputs(cfg, maps, P):
    DC, T, D = cfg.DC, cfg.T, cfg.D
    invf, rotm = rope_const_inputs()
    kvw = P["kv_w"]
    uf = wunits(kvw)
    sel = [br * 4 + g for br in (0, 1, 2, 4) for g in range(4)]
    w_kvf = np.ascontiguousarray(uf[sel])
    wt = []
    for br in (3, 5):
        w = kvw[:, br * 512:(br + 1) * 512]
        wt.append(np.ascontiguousarray(w.reshape(DC, 128, 512).transpose(1, 0, 2)).reshape(128, DC * 512))
    w_kvt = np.stack(wt)
    pos = np.asarray(P["positions"][0], np.int32)
    for c, m in enumerate(maps):
        m["norm_kv_l"] = vecP(P["norm_kv"])
        m["kv_ada_l"] = vecP(P["kv_ada_emb"].reshape(-1))
        m["kn_l"] = np.ascontiguousarray(np.stack([P["kv_k_norm"][1], P["kv_k_norm"][2]], 1).astype(np.float32))
        m["pos_l"] = np.ascontiguousarray(np.broadcast_to(pos[c * T:(c + 1) * T][None, :], (128, T)))
        m["w_kvf_l"] = w_kvf
        m["w_kvt_l"] = w_kvt
        m["invf"] = invf
        m["rotm"] = rotm
    return maps


def build_nsa_launch(cfg):
    L = LayerBuilder(cfg, "nsa", False)
    nc, DC, T = L.nc, cfg.DC, cfg.T
    L.setup_common()
    xin = L.din("xT", [DC, 128, T])
    xs = nc.dram_tensor("xs", [DC, 128, T], F32)
    y = L.dout("yT", [DC, 128, T])
    b_xin, b_xs, b_y = bufs("xin", DC), bufs("xs", DC), bufs("y", DC)
    L.vec_setup(False)
    L.nsa(xin, b_xin, xs, b_xs)
    L.ffn(xs, b_xs, y, b_y, True)
    L.P.finish()
    L.es.close()
    return L


def bf(a):
    return np.ascontiguousarray(a).astype(ml_dtypes.bfloat16)


def nsa_tile_order(cfg):
    NQT = cfg.T // 128
    return [[8 * i + c for i in range(NQT)] for c in range(cfg.NCORES)]


def nsa_shard_x(cfg, x_full):
    order = nsa_tile_order(cfg)
    out = []
    for c in range(cfg.NCORES):
        rows = np.concatenate([x_full[t * 128:(t + 1) * 128] for t in order[c]], 0)
        out.append(rows)
    return out


def nsa_unshard_x(cfg, parts):
    order = nsa_tile_order(cfg)
    S, D = cfg.S, cfg.D
    x = np.empty((S, D), np.float32)
    for c in range(cfg.NCORES):
        for li, t in enumerate(order[c]):
            x[t * 128:(t + 1) * 128] = parts[c][li * 128:(li + 1) * 128]
    return x


def nsa_launch_inputs(cfg, li, x_full, mod_in, P, kv):
    DC, T, D, NCr, S, G, H = cfg.DC, cfg.T, cfg.D, cfg.NCORES, cfg.S, cfg.G, cfg.H
    layer = 2 + li
    NQT = T // 128
    NSLC, NCMP = cfg.NSLC, cfg.NCMP
    NCT = -(-NCMP // 128)
    NCP = NCT * 128
    NWT = cfg.WINDOW // 128 + 1
    invf, rotm = rope_const_inputs()
    sh = {"mod_in": mod_in, "invf": invf, "rotm": rotm}
    sh["ada_l"] = vecP(P["ada_emb"][layer].reshape(-1))
    sh["norm_mix_l"] = vecP(P["norm_mix"][layer])
    sh["norm_ffn_l"] = vecP(P["norm_ffn"][layer])
    sh["w_gu_l"] = wunits(P["ffn_w_gu"][layer])
    sh["w_dn_l"] = wunits(P["ffn_w_down"][layer])
    sh["w_q_l"] = wunits(P["nsa_w_q"][li])
    sh["w_o_l"] = wunits(P["nsa_w_o"][li])
    wg = P["nsa_w_gate"][li]
    NGT = 3 * H
    sh["w_gate_l"] = np.ascontiguousarray(wg.reshape(DC, 128, NGT).transpose(1, 0, 2)).reshape(128, DC * NGT)
    sh["b_gate_l"] = np.ascontiguousarray(P["nsa_b_gate"][li].reshape(NGT, 1).astype(np.float32))
    sh["qn_l"] = np.ascontiguousarray(P["nsa_q_norm"][li].reshape(128, 1).astype(np.float32))
    sh["kn0_l"] = np.ascontiguousarray(P["kv_k_norm"][0].reshape(128, 1).astype(np.float32))
    pos = np.asarray(P["positions"][0], np.int32)
    posc = np.zeros((NCP,), np.int32)
    posc[:NCMP] = pos[np.arange(NCMP) * 16 + 31]
    sh["posc_l"] = np.ascontiguousarray(np.broadcast_to(posc[None, :], (128, NCP)))
    sh["ident"] = np.eye(128, dtype=np.float32)
    sh["iotab"] = np.ascontiguousarray(np.broadcast_to(np.arange(NSLC, dtype=np.float32)[None, :], (128, NSLC)))
    sh["b0c"] = np.ascontiguousarray((sh["iotab"] == 0).astype(np.float32))
    jj, qq = np.meshgrid(np.arange(128), np.arange(128), indexing="ij")
    sh["iotad"] = (qq - jj).astype(np.float32)
    cio = np.zeros((128, NCT, 128), np.float32)
    for kt in range(NCT):
        cio[:, kt, :] = qq - 16 * (128 * kt + jj) - 31
    sh["cmpiota"] = cio
    r, cl = 4, 2
    offs = (np.arange(r)[:, None] - np.arange(cl)[None, :]).reshape(-1)
    tgt = r * np.arange(NSLC)[None, :, None] + offs[None, None, :]
    C = (np.arange(NCP)[:, None, None] == tgt).sum(-1).astype(np.float32)
    C[NCMP:] = 0
    sh["cmat"] = bf(C.reshape(NCT, 128, NSLC).transpose(1, 0, 2))
    wm = np.zeros((128, NWT, 128), np.float32)
    for jt in range(NWT):
        dist = (cfg.WINDOW + qq) - (128 * jt + jj)
        wm[:, jt, :] = ((dist >= 0) & (dist < cfg.WINDOW)).astype(np.float32)
    sh["wmask"] = bf(wm)
    x0 = np.zeros((128, 2048), np.float32)
    mcol = np.arange(2048) // 64
    for p in range(128):
        x0[p] = (mcol == (p % 32))
    sh["x0r"] = bf(x0)
    sh["cmp_b1_l"] = np.ascontiguousarray(np.concatenate([vecP(P["cmp_b1"][0]), vecP(P["cmp_b1"][1])], 1))
    sh["cmp_b2k_l"] = np.ascontiguousarray(P["cmp_b2"][0].reshape(128, 1).astype(np.float32))
    sh["cmp_b2v_l"] = np.ascontiguousarray(np.broadcast_to(P["cmp_b2"][1][None, :], (128, 128)).astype(np.float32))
    sh["cmp_pos_l"] = bf(np.stack([P["cmp_pos"][0].T, P["cmp_pos"][1].T], 1))
    w2 = np.stack([P["cmp_w2"][j].reshape(4, 128, 128).transpose(1, 0, 2).reshape(128, 512) for j in range(2)], 1)
    sh["cmp_w2_l"] = bf(w2)
    sh["cmp_w1_l"] = np.concatenate([wunits(P["cmp_w1"][0]), wunits(P["cmp_w1"][1])], 0)
    sh["kcrT"], sh["vcrT"], sh["kslT"] = kv["kcrT"], kv["vcrT"], kv["kslT"]
    sh["vslk"] = kv["vsl"].reshape(S // 128, 128, G * 128)
    order = nsa_tile_order(cfg)
    xparts = nsa_shard_x(cfg, x_full)
    kwT_full, vw_full = kv["kwnT"], kv["vwn"]
    maps = []
    for c in range(NCr):
        m = dict(sh)
        m["xT"] = xT_layout(xparts[c])
        tl = order[c]
        posq = np.concatenate([pos[t * 128:(t + 1) * 128] for t in tl])
        m["posq_l"] = np.ascontiguousarray(np.broadcast_to(posq[None, :], (128, T)))
        t0 = np.array([t * 128 for t in tl], np.float32)
        m["t0_l"] = np.ascontiguousarray(np.broadcast_to(t0[None, :], (128, NQT)))
        tq = (t0[None, :] / 64.0 + (np.arange(128)[:, None] >= 64)).astype(np.float32)
        m["tq_l"] = np.ascontiguousarray(tq)
        m["coff_l"] = np.ascontiguousarray(np.broadcast_to((128.0 * (c - np.arange(8)))[None, :], (128, 8)).astype(np.float32))
        wv = np.zeros((128, NQT * NWT), np.float32)
        kw = np.zeros((NQT, G, 128, NWT * 128), ml_dtypes.bfloat16)
        vw = np.zeros((NQT, NWT, 128, G * 128), ml_dtypes.bfloat16)
        for li_, t in enumerate(tl):
            k0 = t * 128 - cfg.WINDOW
            for jt in range(NWT):
                kp = k0 + jt * 128
                if kp >= 0:
                    wv[:, li_ * NWT + jt] = 1.0
                    kw[li_, :, :, jt * 128:(jt + 1) * 128] = kwT_full[:, :, kp:kp + 128]
                    vw[li_, jt] = vw_full[kp:kp + 128]
        m["wvalid_l"] = wv
        m["kwT"] = kw
        m["vwk"] = vw
        maps.append(m)
    return maps


_PROG_CACHE = {}


def _prog(key, fn):
    return fn()


def _run(L, maps, ncores):
    for k, (shp, dt) in L.inputs.items():
        assert k in maps[0], k
        assert tuple(maps[0][k].shape) == tuple(shp), (k, maps[0][k].shape, shp)
    maps = [{k: m[k] for k in L.inputs} for m in maps]
    return run_bass_kernel_spmd(L.nc, maps, core_ids=list(range(ncores))).results


_CFG = None


def kernel(**inputs):
    cfg = _CFG or Cfg()
    P = {k: np.asarray(v) for k, v in inputs.items()}
    NCr, S, T, G = cfg.NCORES, cfg.S, cfg.T, cfg.G
    x0 = np.ascontiguousarray(P["x"][0], dtype=np.float32)
    LA = build_conv_launch(cfg, True, False)
    res = _run(LA, conv_launch_inputs(cfg, 0, x0, None, P), NCr)
    x1 = np.concatenate([xT_unlayout(r["yT"]) for r in res], 0)
    mod_l = np.ascontiguousarray(res[0]["mod_out"])
    del res, LA
    LB = build_conv_launch(cfg, False, True)
    maps = kv_launch_extra_inputs(cfg, conv_launch_inputs(cfg, 1, x1, mod_l, P), P)
    res = _run(LB, maps, NCr)
    x2 = np.concatenate([xT_unlayout(r["yT"]) for r in res], 0)
    kv = {
        "kcrT": np.ascontiguousarray(np.concatenate([np.asarray(r["kcr"]) for r in res], 2)),
        "vcrT": np.ascontiguousarray(np.concatenate([np.asarray(r["vcr"]) for r in res], 2)),
        "kslT": np.ascontiguousarray(np.concatenate([np.asarray(r["ksl"]) for r in res], 2)),
        "kwnT": np.ascontiguousarray(np.concatenate([np.asarray(r["kwn"]) for r in res], 2)),
        "vsl": np.ascontiguousarray(np.concatenate([np.asarray(r["vsl"]) for r in res], 0)),
        "vwn": np.ascontiguousarray(np.concatenate([np.asarray(r["vwn"]) for r in res], 0)),
    }
    del res, LB, maps
    x = x2
    for li in range(2):
        LC = build_nsa_launch(cfg)
        res = _run(LC, nsa_launch_inputs(cfg, li, x, mod_l, P, kv), NCr)
        x = nsa_unshard_x(cfg, [xT_unlayout(r["yT"]) for r in res])
        del res, LC
    return x[None].astype(np.float32)
```
